# Optimizing a Trainium2 kernel written in Bass

```python
import math
import jax, jax.numpy as jnp
from jax import lax
import numpy as np

D_MODEL = 2048
BATCH = 8
SEQ = 4096
DEPTH = 1

SSD_HEAD_DIM = 64
SSD_D_INNER = D_MODEL
SSD_HEADS = SSD_D_INNER // SSD_HEAD_DIM
SSD_GROUPS = 8
SSD_HEADS_PER_GROUP = SSD_HEADS // SSD_GROUPS
SSD_STATE = 128
SSD_CONV = 5
SSD_CHUNK = 128
SSD_XBC_WIDTH = SSD_D_INNER + 2 * SSD_GROUPS * SSD_STATE
POOL_WIDTH = D_MODEL // 2
POOL_WINDOWS = (2, 4, 8, 16)
POOL_GROUPS = len(POOL_WINDOWS)
POOL_GROUP_DIM = POOL_WIDTH // POOL_GROUPS
N_BRANCHES = 2
IN_PROJ_WIDTH = SSD_D_INNER + SSD_XBC_WIDTH + 2 * SSD_HEADS + POOL_WIDTH + N_BRANCHES * D_MODEL
PEER_HEADS = 8
PEER_N_KEYS = 128
PEER_N_EXPERTS = PEER_N_KEYS * PEER_N_KEYS
PEER_QUERY_DIM = 256
PEER_HALF_DIM = PEER_QUERY_DIM // 2
PEER_TOPK = 16
PEER_TOKEN_BLOCK = 128
EPS = 1e-6

kernel_name = 'hybrid_ssd_pool_peer_encoder_block'


def rmsnorm(x, w):
    xf = x.astype(jnp.float32)
    y = xf * lax.rsqrt(jnp.mean(xf * xf, axis=-1, keepdims=True) + EPS)
    return (y * w.astype(jnp.float32)).astype(x.dtype)


def depthwise_conv_centred(x, w, b):
    pad = (w.shape[0] - 1) // 2
    y = lax.conv_general_dilated(x, w.astype(jnp.float32), window_strides=(1,), padding=[(pad, pad)],
                                 dimension_numbers=('NWC', 'WIO', 'NWC'), feature_group_count=x.shape[-1])
    return y + b.astype(jnp.float32)


def segsum(a):
    l = a.shape[-1]
    cs = jnp.cumsum(a, axis=-1)
    diff = cs[..., :, None] - cs[..., None, :]
    mask = jnp.tril(jnp.ones((l, l), dtype=bool))
    return jnp.where(mask, diff, -jnp.inf)


def ssd_scan(x, dt, A, B, C):
    b, s, g, r, p = x.shape
    n = B.shape[-1]
    c, l = s // SSD_CHUNK, SSD_CHUNK
    X = (x * dt[..., None]).reshape(b, c, l, g, r, p)
    Adt = jnp.transpose((dt * A).reshape(b, c, l, g, r), (0, 3, 4, 1, 2))
    Bc = B.reshape(b, c, l, g, n)
    Cc = C.reshape(b, c, l, g, n)
    A_cs = jnp.cumsum(Adt, axis=-1)
    Lmat = jnp.exp(segsum(Adt))
    CB = jnp.einsum('bclgn,bcsgn->bgcls', Cc, Bc)
    y_diag = jnp.einsum('bgcls,bgrcls,bcsgrp->bclgrp', CB, Lmat, X)
    decay_states = jnp.exp(A_cs[..., -1:] - A_cs)
    states = jnp.einsum('bclgn,bgrcl,bclgrp->cbgrpn', Bc, decay_states, X)
    chunk_decay = jnp.moveaxis(jnp.exp(A_cs[..., -1]), -1, 0)

    def step(h, inp):
        st, dec = inp
        return h * dec[..., None, None] + st, h

    h0 = jnp.zeros((b, g, r, p, n), jnp.float32)
    _, prev = lax.scan(step, h0, (states, chunk_decay))
    y_off = jnp.einsum('bclgn,cbgrpn,bgrcl->bclgrp', Cc, prev, jnp.exp(A_cs))
    return (y_diag + y_off).reshape(b, s, g, r, p)


def multi_scale_pool(xp):
    b, s, w_all = xp.shape
    P = jnp.concatenate([jnp.zeros((b, 1, w_all), jnp.float32), jnp.cumsum(xp, axis=1)], axis=1)
    t = jnp.arange(s)
    outs = []
    for gi, w in enumerate(POOL_WINDOWS):
        lo = jnp.clip(t - w // 2, 0, s)
        hi = jnp.clip(t + w // 2, 0, s)
        sl = slice(gi * POOL_GROUP_DIM, (gi + 1) * POOL_GROUP_DIM)
        Pg = P[..., sl]
        mean = (Pg[:, hi] - Pg[:, lo]) / (hi - lo).astype(jnp.float32)[None, :, None]
        outs.append(mean - xp[..., sl])
    return jnp.concatenate(outs, axis=-1)


def hybrid_mixer(xn, w_in, conv_w, conv_b, dt_bias, a_log, d_skip, ssd_norm_w, w_ssd_branch,
                 w_pool_group, pool_scale, w_pool_branch, w_out):
    b, s, _ = xn.shape
    G, R, P, N, H = SSD_GROUPS, SSD_HEADS_PER_GROUP, SSD_HEAD_DIM, SSD_STATE, SSD_HEADS
    proj = (xn @ w_in).astype(jnp.float32)
    o1 = SSD_D_INNER
    o2 = o1 + SSD_XBC_WIDTH
    o3 = o2 + 2 * H
    o4 = o3 + POOL_WIDTH
    z, xbc, dt_raw, xp, gate_raw = proj[..., :o1], proj[..., o1:o2], proj[..., o2:o3], proj[..., o3:o4], proj[..., o4:]

    xbc = jax.nn.silu(depthwise_conv_centred(xbc, conv_w, conv_b))
    xs = xbc[..., :SSD_D_INNER].reshape(b, s, G, R, P)
    Bm = xbc[..., SSD_D_INNER:SSD_D_INNER + G * N].reshape(b, s, G, N)
    Cm = xbc[..., SSD_D_INNER + G * N:].reshape(b, s, G, N)
    dtb = dt_bias.astype(jnp.float32)
    dt_f = jax.nn.softplus(dt_raw[..., :H] + dtb[0]).reshape(b, s, G, R)
    dt_b = jax.nn.softplus(dt_raw[..., H:] + dtb[1]).reshape(b, s, G, R)
    A = -jnp.exp(a_log.astype(jnp.float32))
    flip = lambda t: jnp.flip(t, axis=1)
    y_f = ssd_scan(xs, dt_f, A[0].reshape(G, R), Bm, Cm)
    y_b = flip(ssd_scan(flip(xs), flip(dt_b), A[1].reshape(G, R), flip(Bm), flip(Cm)))
    y = y_f + y_b + d_skip.astype(jnp.float32).reshape(G, R)[..., None] * xs
    y = y.reshape(b, s, SSD_D_INNER) * jax.nn.silu(z)
    yg = y.reshape(b, s, G, -1)
    yg = yg * lax.rsqrt(jnp.mean(yg * yg, axis=-1, keepdims=True) + EPS)
    y = yg.reshape(b, s, SSD_D_INNER) * ssd_norm_w.astype(jnp.float32)
    y_ssd = y @ w_ssd_branch.astype(jnp.float32)

    pooled = multi_scale_pool(xp).reshape(b, s, POOL_GROUPS, POOL_GROUP_DIM)
    pooled = jnp.einsum('bsgc,gcd->bsgd', pooled, w_pool_group.astype(jnp.float32)).reshape(b, s, POOL_WIDTH)
    y_pool = (pooled * pool_scale.astype(jnp.float32)) @ w_pool_branch.astype(jnp.float32)

    gates = jax.nn.sigmoid(gate_raw)
    merged = gates[..., :D_MODEL] * y_ssd + gates[..., D_MODEL:] * y_pool
    return (merged @ w_out.astype(jnp.float32)).astype(xn.dtype)


def peer(xn, w_query, sub_keys, expert_u, expert_v):
    b, s, d = xn.shape
    T = b * s
    H, K = PEER_HEADS, PEER_TOPK
    xf = xn.reshape(T, d)
    q = (xf @ w_query).reshape(T, H, 2, PEER_HALF_DIM).astype(jnp.float32)
    scores = jnp.einsum('thkd,hknd->thkn', q, sub_keys.astype(jnp.float32))
    s_half, i_half = lax.top_k(scores, K)
    cand_s = (s_half[:, :, 0, :, None] + s_half[:, :, 1, None, :]).reshape(T, H, K * K)
    cand_i = (i_half[:, :, 0, :, None] * PEER_N_KEYS + i_half[:, :, 1, None, :]).reshape(T, H, K * K)
    top_s, top_pos = lax.top_k(cand_s, K)
    idx = jnp.take_along_axis(cand_i, top_pos, axis=-1)
    gate = jax.nn.softmax(top_s, axis=-1)
    nb = T // PEER_TOKEN_BLOCK
    x_b = xf.reshape(nb, PEER_TOKEN_BLOCK, d)
    idx_b = idx.reshape(nb, PEER_TOKEN_BLOCK, H * K)
    gate_b = gate.reshape(nb, PEER_TOKEN_BLOCK, H * K).astype(xn.dtype)

    def block(args):
        xt, it, gt = args
        u = expert_u[it]
        a = jnp.einsum('tkd,td->tk', u, xt)
        coef = jax.nn.gelu(a, approximate=False) * gt
        v = expert_v[it]
        return jnp.einsum('tk,tkd->td', coef, v)

    out = lax.map(block, (x_b, idx_b, gate_b))
    return out.reshape(b, s, d).astype(xn.dtype)


def setup_inputs(seed: int = 0) -> dict:
    key = jax.random.key(seed)
    ks = jax.random.split(key, 20)
    f32 = jnp.float32
    L = DEPTH

    def nrm(k, shape, scale):
        return jax.random.normal(k, shape, f32) * scale

    x = nrm(ks[0], (BATCH, SEQ, D_MODEL), 1.0)
    mixer_norm_w = 1.0 + nrm(ks[1], (L, D_MODEL), 0.01)
    w_in = nrm(ks[2], (L, D_MODEL, IN_PROJ_WIDTH), D_MODEL ** -0.5)
    conv_w = nrm(ks[3], (L, SSD_CONV, 1, SSD_XBC_WIDTH), SSD_CONV ** -0.5)
    conv_b = nrm(ks[4], (L, SSD_XBC_WIDTH), 0.01)
    dt0 = jnp.exp(jax.random.uniform(ks[5], (L, 2, SSD_HEADS), f32, math.log(1e-3), math.log(1e-1)))
    dt_bias = dt0 + jnp.log(-jnp.expm1(-dt0))
    a_log = jnp.log(jax.random.uniform(ks[6], (L, 2, SSD_HEADS), f32, 1.0, 16.0))
    d_skip = 1.0 + nrm(ks[7], (L, SSD_HEADS), 0.01)
    ssd_norm_w = 1.0 + nrm(ks[8], (L, SSD_D_INNER), 0.01)
    w_ssd_branch = nrm(ks[9], (L, SSD_D_INNER, D_MODEL), SSD_D_INNER ** -0.5)
    w_pool_group = nrm(ks[10], (L, POOL_GROUPS, POOL_GROUP_DIM, POOL_GROUP_DIM), POOL_GROUP_DIM ** -0.5)
    pool_scale = 1.0 + nrm(ks[11], (L, POOL_WIDTH), 0.01)
    w_pool_branch = nrm(ks[12], (L, POOL_WIDTH, D_MODEL), POOL_WIDTH ** -0.5)
    w_out = nrm(ks[13], (L, D_MODEL, D_MODEL), D_MODEL ** -0.5)
    ffn_norm_w = 1.0 + nrm(ks[14], (L, D_MODEL), 0.01)
    w_query = nrm(ks[15], (L, D_MODEL, PEER_HEADS * PEER_QUERY_DIM), D_MODEL ** -0.5)
    sub_keys = nrm(ks[16], (L, PEER_HEADS, 2, PEER_N_KEYS, PEER_HALF_DIM), PEER_HALF_DIM ** -0.5)
    expert_u = nrm(ks[17], (L, PEER_N_EXPERTS, D_MODEL), D_MODEL ** -0.5)
    expert_v = nrm(ks[18], (L, PEER_N_EXPERTS, D_MODEL), 0.25)
    final_norm_w = 1.0 + nrm(ks[19], (D_MODEL,), 0.01)
    return {'x': x, 'mixer_norm_w': mixer_norm_w, 'w_in': w_in, 'conv_w': conv_w, 'conv_b': conv_b,
            'dt_bias': dt_bias, 'a_log': a_log, 'd_skip': d_skip, 'ssd_norm_w': ssd_norm_w,
            'w_ssd_branch': w_ssd_branch, 'w_pool_group': w_pool_group, 'pool_scale': pool_scale,
            'w_pool_branch': w_pool_branch, 'w_out': w_out, 'ffn_norm_w': ffn_norm_w, 'w_query': w_query,
            'sub_keys': sub_keys, 'expert_u': expert_u, 'expert_v': expert_v, 'final_norm_w': final_norm_w}


def reference(x, mixer_norm_w, w_in, conv_w, conv_b, dt_bias, a_log, d_skip, ssd_norm_w, w_ssd_branch,
              w_pool_group, pool_scale, w_pool_branch, w_out, ffn_norm_w, w_query, sub_keys, expert_u,
              expert_v, final_norm_w):
    h = x
    for i in range(DEPTH):
        h = h + hybrid_mixer(rmsnorm(h, mixer_norm_w[i]), w_in[i], conv_w[i], conv_b[i], dt_bias[i], a_log[i],
                             d_skip[i], ssd_norm_w[i], w_ssd_branch[i], w_pool_group[i], pool_scale[i],
                             w_pool_branch[i], w_out[i])
        h = h + peer(rmsnorm(h, ffn_norm_w[i]), w_query[i], sub_keys[i], expert_u[i], expert_v[i])
    return rmsnorm(h, final_norm_w)
```

```python
import numpy as np
import concourse.bass as bass
import concourse.mybir as mybir
from contextlib import ExitStack
from concourse.bass_utils import run_bass_kernel_spmd

F32 = mybir.dt.float32
BF16 = mybir.dt.bfloat16
I32 = mybir.dt.int32
U32 = mybir.dt.uint32
AF = mybir.ActivationFunctionType
ALU = mybir.AluOpType
AX = mybir.AxisListType


class Buf:
    def __init__(self, name, t, space):
        self.name = name
        self.t = t
        self.space = space
        self.w = {}
        self.r = {}
        self.sem = None
        self.semcount = 0

    def __getitem__(self, idx):
        return self.t[idx]


class Prog:
    ENG = ("pe", "act", "dve", "pool", "sp")

    def __init__(self, nc, stack):
        self.nc = nc
        self.stack = stack
        self.streams = {e: [] for e in self.ENG}
        self.esem = {e: stack.enter_context(nc.semaphore("es_" + e)) for e in self.ENG}
        self.ecount = {e: 0 for e in self.ENG}
        self.waited = {e: {} for e in self.ENG}
        self.semobj = {}
        for e in self.ENG:
            self.semobj[("e", e)] = self.esem[e]
        self.nsem = 0
        self.nbuf = 0
        self.outer = stack
        self.allbufs = []
        self.sempool = []
        self.phase_bufs = []
        self._semcounts = {}

    def sb(self, name, shape, dtype):
        self.nbuf += 1
        name = "%s_%d" % (name, self.nbuf)
        t = self.stack.enter_context(self.nc.sbuf_tensor(name, list(shape), dtype))
        return self._reg(Buf(name, t, "sb"))

    def ps(self, name, shape, dtype):
        self.nbuf += 1
        name = "%s_%d" % (name, self.nbuf)
        t = self.stack.enter_context(self.nc.psum_tensor(name, list(shape), dtype))
        return self._reg(Buf(name, t, "ps"))

    def dram(self, name, shape, dtype, kind="Internal"):
        t = self.nc.dram_tensor(name, list(shape), dtype, kind=kind)
        return self._reg(Buf(name, t.ap(), "dram"))

    def view(self, buf, t):
        b = Buf(buf.name + "_v%d" % self.nbuf, t, buf.space)
        self.nbuf += 1
        return self._reg(b)

    def _reg(self, b):
        self.allbufs.append(b)
        return b

    def _bufsem(self, b):
        if b.sem is None:
            if self.sempool:
                b.sem, b.semcount = self.sempool.pop()
            else:
                b.sem = self.outer.enter_context(self.nc.semaphore("bs%d" % self.nsem))
                self.nsem += 1
            self.semobj[("b", id(b))] = b.sem
            self._semcounts[("b", id(b))] = (lambda b=b: b.semcount)
            b.key = ("b", id(b))
            self.phase_bufs.append(b)
        return b.key

    def end_phase(self, keep=()):
        self.barrier()
        self.emit()
        kept = []
        for b in self.phase_bufs:
            if b in keep:
                kept.append(b)
                continue
            self.sempool.append((b.sem, b.semcount))
            del self.semobj[b.key]
            del self._semcounts[b.key]
            for e in self.ENG:
                self.waited[e].pop(b.key, None)
            b.sem = None
        self.phase_bufs = kept
        for b in self.allbufs:
            b.w = {}
            b.r = {}

    def _collect(self, eng, reads, writes, own_key, is_dma):
        need = {}

        def add(k, v):
            if v > need.get(k, 0):
                need[k] = v

        for b in reads:
            for k, v in b.w.items():
                if (not is_dma) and k == own_key and eng == "pe":
                    continue
                add(k, v)
        for b in writes:
            if b.space == "dram":
                continue
            for k, v in b.w.items():
                if k == own_key:
                    continue
                add(k, v)
            for k, v in b.r.items():
                if (not is_dma) and k == own_key:
                    continue
                add(k, v)
        wl = []
        cache = self.waited[eng]
        for k, v in need.items():
            if cache.get(k, 0) >= v:
                continue
            cache[k] = v
            wl.append((k, v))
        return wl

    def op(self, eng, fn, reads=(), writes=()):
        key = ("e", eng)
        waits = self._collect(eng, reads, writes, key, False)
        self.ecount[eng] += 1
        val = self.ecount[eng]
        self.streams[eng].append((waits, fn, key, 1))
        for b in reads:
            b.r[key] = val
        for b in writes:
            b.w = {key: val}
            b.r = {}

    def dma(self, q, fn, reads=(), writes=(), sembuf=None):
        if sembuf is None:
            cands = [b for b in list(writes) + list(reads) if b.space == "sb"]
            sembuf = cands[0] if cands else (list(writes) + list(reads))[0]
        key = self._bufsem(sembuf)
        waits = self._collect(q, reads, writes, key, True)
        sembuf.semcount += 16
        val = sembuf.semcount
        self.streams[q].append((waits, fn, key, 16))
        for b in reads:
            b.r[key] = val
        for b in writes:
            if b.space == "dram":
                b.w = dict(b.w)
                b.w[key] = val
            else:
                b.w = {key: val}
                b.r = {}

    def wait_all(self, eng, bufs):
        need = {}
        for b in bufs:
            for k, v in list(b.w.items()) + list(b.r.items()):
                if v > need.get(k, 0):
                    need[k] = v
        self.streams[eng].append((list(need.items()), None, None, 0))

    def barrier(self):
        need = [(("e", e), self.ecount[e]) for e in self.ENG if self.ecount[e] > 0]
        for k, sem in self.semobj.items():
            if k[0] == "b":
                need.append((k, self._semcounts[k]()))
        for e in self.ENG:
            wl = []
            for k, v in need:
                if k == ("e", e) or v == 0:
                    continue
                if self.waited[e].get(k, 0) >= v:
                    continue
                self.waited[e][k] = v
                wl.append((k, v))
            self.streams[e].append((wl, None, None, 0))

    def emit(self):
        nc = self.nc
        handles = {"pe": "tensor", "act": "scalar", "dve": "vector", "pool": "gpsimd", "sp": "sync"}
        with nc.Block() as block:
            for e in self.ENG:
                stream = self.streams[e]

                def body(eng, stream=stream):
                    for waits, fn, key, inc in stream:
                        for k, v in waits:
                            eng.wait_ge(self.semobj[k], v)
                        if fn is not None:
                            ins = fn(eng)
                            ins.then_inc(self.semobj[key], inc)

                getattr(block, handles[e])(body)
        self.streams = {e: [] for e in self.ENG}


D = 2048; H = 32; G = 8; R = 4; PD = 64; NS = 128; XBC = 4096; PW = 1024; IPW = 11328
O1 = 2048; O2 = 6144; O3 = 6208; O4 = 7232
NE = 16384; TOPK = 16
EPS = 1e-6

BLOCKS = ([(c, 512, "z") for c in range(0, 2048, 512)] + [(c, 512, "xbc") for c in range(O1, O2, 512)]
          + [(O2, 64, "dt")] + [(c, 512, "xp") for c in range(O3, O4, 512)]
          + [(c, 512, "gate") for c in range(O4, IPW, 512)])


class Ring:
    def __init__(self, bufs):
        self.bufs = bufs
        self.i = 0

    def next(self):
        b = self.bufs[self.i % len(self.bufs)]
        self.i += 1
        return b


def build(S, debug=False, phases=(0, 1, 2, 3, 4, 5)):
    nc = bass.Bass("TRN2", target_bir_lowering=False)
    dbg = "ExternalOutput" if debug else "Internal"
    NCH = S // 128
    TT = 512
    NTT = S // TT
    with ExitStack() as st:
        P = Prog(nc, st)
        x = P.dram("x", [S, D], F32, "ExternalInput")
        mixer_norm_w = P.dram("mixer_norm_w", [1, D], F32, "ExternalInput")
        w_in = P.dram("w_in", [D, IPW], F32, "ExternalInput")
        conv_w = P.dram("conv_w", [5, XBC], F32, "ExternalInput")
        conv_b = P.dram("conv_b", [1, XBC], F32, "ExternalInput")
        dt_bias = P.dram("dt_bias", [1, 64], F32, "ExternalInput")
        a_log = P.dram("a_log", [1, 64], F32, "ExternalInput")
        d_skip = P.dram("d_skip", [1, H], F32, "ExternalInput")
        ssd_norm_w = P.dram("ssd_norm_w", [1, D], F32, "ExternalInput")
        w_ssd = P.dram("w_ssd_branch", [D, D], F32, "ExternalInput")
        w_pg = P.dram("w_pool_group", [4, 256, 256], F32, "ExternalInput")
        pool_scale = P.dram("pool_scale", [1, PW], F32, "ExternalInput")
        w_pb = P.dram("w_pool_branch", [PW, D], F32, "ExternalInput")
        w_out = P.dram("w_out", [D, D], F32, "ExternalInput")
        ffn_norm_w = P.dram("ffn_norm_w", [1, D], F32, "ExternalInput")
        w_query = P.dram("w_query", [D, D], F32, "ExternalInput")
        sub_keys = P.dram("sub_keys", [16, 128, 128], F32, "ExternalInput")
        expert_u = P.dram("expert_u", [NE, D], F32, "ExternalInput")
        expert_v = P.dram("expert_v", [NE, D], F32, "ExternalInput")
        final_norm_w = P.dram("final_norm_w", [1, D], F32, "ExternalInput")
        out = P.dram("out", [S, D], F32, "ExternalOutput")

        wblk_d = [P.dram("winbf%d" % i, [128, 16, w], BF16) for i, (c0, w, k) in enumerate(BLOCKS)]
        xbc_raw = P.dram("xbc_raw", [XBC, S + 4], BF16, dbg)
        xp_raw = P.dram("xp_raw", [PW, S + 16], BF16, dbg)
        gT = P.dram("gT", [XBC, S], BF16, dbg)
        sz = P.dram("sz", [S, D], BF16, dbg)
        dts = P.dram("dts", [S, 64], F32, dbg)
        xbc_c = P.dram("xbc_c", [XBC, S], F32, dbg)
        yf = P.dram("yf", [S, D], F32, dbg)
        ynT = P.dram("ynT", [D, S], BF16, dbg)
        hsc = P.dram("hsc", [S, D], F32, dbg)
        wssd_d = [P.dram("wssdbf%d" % i, [128, 16, 512], BF16) for i in range(4)]
        wpb_d = [P.dram("wpbbf%d" % i, [128, 8, 512], BF16) for i in range(4)]
        wout_d = [P.dram("woutbf%d" % i, [128, 16, 512], BF16) for i in range(4)]
        wq_d = [P.dram("wqbf%d" % i, [128, 16, 512], BF16) for i in range(4)]
        wpg_d = P.dram("wpgbf", [128, 8, 256], BF16)
        pedge = P.dram("pedge", [1, 64], F32, "ExternalInput")
        outs = [out]
        if debug:
            idx_dbg = P.dram("idx_dbg", [S, 128], I32, dbg)
            gate_dbg = P.dram("gate_dbg", [S, 128], F32, dbg)
            a_dbg = P.dram("a_dbg", [S, 128], F32, dbg)
            outs += [xbc_raw, xp_raw, gT, sz, dts, xbc_c, yf, ynT, hsc, idx_dbg, gate_dbg, a_dbg]

        if 0 in phases:
            for i, (c0, w, k) in enumerate(BLOCKS):
                src = w_in[:, c0:c0 + w].rearrange("(kc p) c -> p kc c", p=128)
                for j in range(4):
                    P.dma("pool", lambda e, i=i, j=j, src=src: e.dma_start(out=wblk_d[i][:, 4 * j:4 * j + 4, :], in_=src[:, 4 * j:4 * j + 4, :]),
                          reads=[w_in], writes=[wblk_d[i]], sembuf=wblk_d[i])

            for wsrc, wdst, kcn in ((w_ssd, wssd_d, 16), (w_pb, wpb_d, 8), (w_out, wout_d, 16), (w_query, wq_d, 16)):
                for b in range(4):
                    src = wsrc[:, b * 512:(b + 1) * 512].rearrange("(kc p) c -> p kc c", p=128)
                    for j in range(kcn // 4):
                        P.dma("pool", lambda e, b=b, j=j, src=src, wdst=wdst: e.dma_start(out=wdst[b][:, 4 * j:4 * j + 4, :], in_=src[:, 4 * j:4 * j + 4, :]),
                              reads=[wsrc], writes=[wdst[b]], sembuf=wdst[b])
            P.dma("pool", lambda e: e.dma_start(out=wpg_d[:], in_=w_pg[:].rearrange("g (kc p) d -> p (g kc) d", p=128)), reads=[w_pg], writes=[wpg_d], sembuf=wpg_d)

        if 1 in phases:
            with ExitStack() as ph:
                P.stack = ph
                wn1 = P.sb("wn1", [128, D], F32)
                dtb = P.sb("dtb", [128, 64], F32)
                epsb = P.sb("epsb", [128, 1], F32)
                identb = P.sb("identb", [128, 128], BF16)
                identf = P.sb("identf", [128, 128], F32)
                zeros = P.sb("zeros", [128, 32, 8], BF16)
                xin = Ring([P.sb("xin%d" % i, [128, D], F32) for i in range(2)])
                junk = P.sb("junk", [128, D], BF16)
                ssq = Ring([P.sb("ssq%d" % i, [128, 1], F32) for i in range(2)])
                nb = Ring([P.sb("nb%d" % i, [128, D], BF16) for i in range(2)])
                nT = Ring([P.sb("nT%d" % i, [128, 16, TT], BF16) for i in range(2)])
                wbk = Ring([P.sb("wbk%d" % i, [128, 16, 512], BF16) for i in range(3)])
                stg = Ring([P.sb("stg%d" % i, [128, 512], BF16) for i in range(4)])
                dtt = Ring([P.sb("dtt%d" % i, [128, 64], F32) for i in range(2)])
                dto = Ring([P.sb("dto%d" % i, [128, 64], F32) for i in range(2)])
                tps = Ring([P.ps("tps%d" % i, [128, 8, 128], BF16) for i in range(2)])
                acc = Ring([P.ps("acc%d" % i, [128, 512], F32) for i in range(4)])

                P.dma("sp", lambda e: e.dma_start(out=wn1[:], in_=mixer_norm_w[0:1, :].to_broadcast([128, D])), reads=[mixer_norm_w], writes=[wn1])
                P.dma("sp", lambda e: e.dma_start(out=dtb[:], in_=dt_bias[0:1, :].to_broadcast([128, 64])), reads=[dt_bias], writes=[dtb])
                P.op("pool", lambda e: e.memset(epsb[:], EPS), writes=[epsb])
                P.op("pool", lambda e: e.memset(zeros[:], 0.0), writes=[zeros])
                P.op("pool", lambda e: e.memset(identf[:], 0.0), writes=[identf])
                P.op("pool", lambda e: e.affine_select(out=identf[:], in_=identf[:], pattern=[[-1, 128]], compare_op=ALU.not_equal, fill=1.0, base=0, channel_multiplier=1), reads=[identf], writes=[identf])
                P.op("dve", lambda e: e.tensor_copy(out=identb[:], in_=identf[:]), reads=[identf], writes=[identb])
                xr3 = xbc_raw[:].rearrange("(cc p) t -> p cc t", p=128)
                P.dma("sp", lambda e: e.dma_start(out=xr3[:, :, 0:2], in_=zeros[:, :, 0:2]), reads=[zeros], writes=[xbc_raw])
                P.dma("sp", lambda e: e.dma_start(out=xr3[:, :, S + 2:S + 4], in_=zeros[:, :, 0:2]), reads=[zeros], writes=[xbc_raw])
                xp3 = xp_raw[:].rearrange("(cc p) t -> p cc t", p=128)
                P.dma("sp", lambda e: e.dma_start(out=xp3[:, :, 0:8], in_=zeros[:, 0:8, :]), reads=[zeros], writes=[xp_raw])
                P.dma("sp", lambda e: e.dma_start(out=xp3[:, :, S + 8:S + 16], in_=zeros[:, 0:8, :]), reads=[zeros], writes=[xp_raw])

                for ti in range(NTT):
                    nTt = nT.next()
                    for sub in range(4):
                        t0 = ti * TT + sub * 128
                        xi = xin.next(); sq = ssq.next(); nbb = nb.next()
                        P.dma("sp", lambda e, xi=xi, t0=t0: e.dma_start(out=xi[:], in_=x[t0:t0 + 128, :]), reads=[x], writes=[xi])
                        P.op("act", lambda e, xi=xi, sq=sq: e.activation(out=junk[:], in_=xi[:], func=AF.Square, accum_out=sq[:]), reads=[xi], writes=[junk, sq])
                        P.op("act", lambda e, sq=sq: e.activation(out=sq[:], in_=sq[:], func=AF.Sqrt, bias=epsb[:, 0:1], scale=1.0 / D), reads=[sq, epsb], writes=[sq])
                        P.op("dve", lambda e, sq=sq: e.reciprocal(out=sq[:], in_=sq[:]), reads=[sq], writes=[sq])
                        P.op("dve", lambda e, xi=xi, sq=sq, nbb=nbb: e.scalar_tensor_tensor(out=nbb[:], in0=xi[:], scalar=sq[:, 0:1], in1=wn1[:], op0=ALU.mult, op1=ALU.mult), reads=[xi, sq, wn1], writes=[nbb])
                        for half in range(2):
                            tp = tps.next()
                            for j in range(8):
                                kc = half * 8 + j
                                P.op("pe", lambda e, tp=tp, j=j, kc=kc, nbb=nbb: e.transpose(out=tp[:, j, :], in_=nbb[:, kc * 128:(kc + 1) * 128], identity=identb[:]), reads=[nbb, identb], writes=[tp])
                            if half == 0:
                                P.op("act", lambda e, tp=tp, nTt=nTt, sub=sub: e.copy(out=nTt[:, 0:8, sub * 128:(sub + 1) * 128], in_=tp[:]), reads=[tp], writes=[nTt])
                            else:
                                P.op("dve", lambda e, tp=tp, nTt=nTt, sub=sub: e.tensor_copy(out=nTt[:, 8:16, sub * 128:(sub + 1) * 128], in_=tp[:]), reads=[tp], writes=[nTt])
                    for bi, (c0, w, kind) in enumerate(BLOCKS):
                        wb = wbk.next()
                        P.dma("sp", lambda e, wb=wb, bi=bi, w=w: e.dma_start(out=wb[:, :, 0:w], in_=wblk_d[bi][:]), reads=[wblk_d[bi]], writes=[wb])
                        if kind in ("xbc", "xp", "gate"):
                            for cc in range(4):
                                a = acc.next()
                                for kc in range(16):
                                    P.op("pe", lambda e, a=a, wb=wb, kc=kc, cc=cc, nTt=nTt: e.matmul(a[:], lhsT=wb[:, kc, cc * 128:(cc + 1) * 128], rhs=nTt[:, kc, :], start=(kc == 0), stop=(kc == 15)), reads=[wb, nTt], writes=[a])
                                sg = stg.next()
                                ch0 = c0 + cc * 128
                                if kind == "gate":
                                    P.op("act", lambda e, a=a, sg=sg: e.activation(out=sg[:], in_=a[:], func=AF.Sigmoid), reads=[a], writes=[sg])
                                    dst = gT[ch0 - O4:ch0 - O4 + 128, ti * TT:(ti + 1) * TT]; dbuf = gT
                                elif kind == "xbc":
                                    P.op("dve", lambda e, a=a, sg=sg: e.tensor_copy(out=sg[:], in_=a[:]), reads=[a], writes=[sg])
                                    dst = xbc_raw[ch0 - O1:ch0 - O1 + 128, 2 + ti * TT:2 + (ti + 1) * TT]; dbuf = xbc_raw
                                else:
                                    P.op("dve", lambda e, a=a, sg=sg: e.tensor_copy(out=sg[:], in_=a[:]), reads=[a], writes=[sg])
                                    dst = xp_raw[ch0 - O3:ch0 - O3 + 128, 8 + ti * TT:8 + (ti + 1) * TT]; dbuf = xp_raw
                                P.dma("sp", lambda e, dst=dst, sg=sg: e.dma_start(out=dst, in_=sg[:]), reads=[sg], writes=[dbuf])
                        elif kind == "z":
                            for sub in range(4):
                                a = acc.next()
                                for kc in range(16):
                                    P.op("pe", lambda e, a=a, wb=wb, kc=kc, sub=sub, nTt=nTt: e.matmul(a[:], lhsT=nTt[:, kc, sub * 128:(sub + 1) * 128], rhs=wb[:, kc, :], start=(kc == 0), stop=(kc == 15)), reads=[wb, nTt], writes=[a])
                                sg = stg.next()
                                P.op("act", lambda e, a=a, sg=sg: e.activation(out=sg[:], in_=a[:], func=AF.Silu), reads=[a], writes=[sg])
                                t0 = ti * TT + sub * 128
                                dst = sz[t0:t0 + 128, c0:c0 + 512]
                                P.dma("sp", lambda e, dst=dst, sg=sg: e.dma_start(out=dst, in_=sg[:]), reads=[sg], writes=[sz])
                        else:
                            for sub in range(4):
                                a = acc.next()
                                for kc in range(16):
                                    P.op("pe", lambda e, a=a, wb=wb, kc=kc, sub=sub, nTt=nTt: e.matmul(a[:, 0:64], lhsT=nTt[:, kc, sub * 128:(sub + 1) * 128], rhs=wb[:, kc, 0:64], start=(kc == 0), stop=(kc == 15)), reads=[wb, nTt], writes=[a])
                                d1 = dtt.next(); d2 = dto.next()
                                P.op("dve", lambda e, a=a, d1=d1: e.tensor_tensor(out=d1[:], in0=a[:, 0:64], in1=dtb[:], op=ALU.add), reads=[a, dtb], writes=[d1])
                                P.op("act", lambda e, d1=d1: e.activation(out=d1[:], in_=d1[:], func=AF.Exp), reads=[d1], writes=[d1])
                                P.op("act", lambda e, d1=d1, d2=d2: e.activation(out=d2[:], in_=d1[:], func=AF.Ln, bias=1.0), reads=[d1], writes=[d2])
                                t0 = ti * TT + sub * 128
                                P.dma("sp", lambda e, d2=d2, t0=t0: e.dma_start(out=dts[t0:t0 + 128, :], in_=d2[:]), reads=[d2], writes=[dts])
                P.end_phase()


        if 2 in phases:
            with ExitStack() as ph:
                P.stack = ph
                cw = P.sb("cw", [128, 5, 32], F32)
                cb = P.sb("cb", [128, 32], F32)
                for k in range(5):
                    P.dma("sp", lambda e, k=k: e.dma_start(out=cw[:, k, :], in_=conv_w[k, :].rearrange("(cc p) -> p cc", p=128), allow_slow_non_contiguous=True), reads=[conv_w], writes=[cw])
                P.dma("sp", lambda e: e.dma_start(out=cb[:], in_=conv_b[0, :].rearrange("(cc p) -> p cc", p=128), allow_slow_non_contiguous=True), reads=[conv_b], writes=[cb])
                raw = Ring([P.sb("raw%d" % i, [128, S + 4], BF16) for i in range(2)])
                cac = Ring([P.sb("cac%d" % i, [128, S], F32) for i in range(2)])
                cout = Ring([P.sb("cout%d" % i, [128, S], F32) for i in range(2)])
                for cc in range(32):
                    rw = raw.next(); ac = cac.next(); co = cout.next()
                    eng = "dve"
                    P.dma("sp", lambda e, rw=rw, cc=cc: e.dma_start(out=rw[:], in_=xbc_raw[cc * 128:(cc + 1) * 128, :]), reads=[xbc_raw], writes=[rw])
                    P.op(eng, lambda e, rw=rw, ac=ac, cc=cc: e.tensor_scalar(out=ac[:], in0=rw[:, 0:S], scalar1=cw[:, 0, cc:cc + 1], scalar2=None, op0=ALU.mult), reads=[rw, cw], writes=[ac])
                    for k in range(1, 5):
                        P.op(eng, lambda e, rw=rw, ac=ac, cc=cc, k=k: e.scalar_tensor_tensor(out=ac[:], in0=rw[:, k:k + S], scalar=cw[:, k, cc:cc + 1], in1=ac[:], op0=ALU.mult, op1=ALU.add), reads=[rw, cw, ac], writes=[ac])
                    P.op("act", lambda e, ac=ac, co=co, cc=cc: e.activation(out=co[:], in_=ac[:], func=AF.Silu, bias=cb[:, cc:cc + 1]), reads=[ac, cb], writes=[co])
                    P.dma("sp", lambda e, co=co, cc=cc: e.dma_start(out=xbc_c[cc * 128:(cc + 1) * 128, :], in_=co[:]), reads=[co], writes=[xbc_c])
                P.end_phase()

        if 3 in phases:
            with ExitStack() as ph:
                P.stack = ph
                NEG = -60000.0
                identf = P.sb("identf", [128, 128], F32)
                ones = P.sb("ones", [128, 128], F32)
                Tle = P.sb("Tle", [128, 128], F32); Tge = P.sb("Tge", [128, 128], F32)
                Tgt = P.sb("Tgt", [128, 128], F32); Tlt = P.sb("Tlt", [128, 128], F32)
                NEGf = P.sb("NEGf", [128, 4, 128], F32); NEGb = P.sb("NEGb", [128, 4, 128], F32)
                negA = P.sb("negA", [128, 64], F32)
                dsk = P.sb("dsk", [128, H], F32)
                snw = P.sb("snw", [128, D], F32)
                epsb = P.sb("epsb", [128, 1], F32)
                P.op("pool", lambda e: e.memset(epsb[:], EPS), writes=[epsb])
                P.op("pool", lambda e: e.memset(ones[:], 1.0), writes=[ones])
                P.op("pool", lambda e: e.memset(identf[:], 0.0), writes=[identf])
                P.op("pool", lambda e: e.affine_select(out=identf[:], in_=identf[:], pattern=[[-1, 128]], compare_op=ALU.not_equal, fill=1.0, base=0, channel_multiplier=1), reads=[identf], writes=[identf])
                for T_, pat, cm, cop in ((Tle, 1, -1, ALU.is_ge), (Tge, -1, 1, ALU.is_ge), (Tgt, -1, 1, ALU.is_gt), (Tlt, 1, -1, ALU.is_gt)):
                    P.op("pool", lambda e, T_=T_: e.memset(T_[:], 1.0), writes=[T_])
                    P.op("pool", lambda e, T_=T_, pat=pat, cm=cm, cop=cop: e.affine_select(out=T_[:], in_=T_[:], pattern=[[pat, 128]], compare_op=cop, fill=0.0, base=0, channel_multiplier=cm), reads=[T_], writes=[T_])
                for T_, pat, cm in ((NEGf, -1, 1), (NEGb, 1, -1)):
                    P.op("pool", lambda e, T_=T_: e.memset(T_[:], NEG), writes=[T_])
                    P.op("pool", lambda e, T_=T_, pat=pat, cm=cm: e.affine_select(out=T_[:], in_=T_[:], pattern=[[0, 4], [pat, 128]], compare_op=ALU.is_gt, fill=0.0, base=0, channel_multiplier=cm), reads=[T_], writes=[T_])
                P.dma("sp", lambda e: e.dma_start(out=negA[:], in_=a_log[0:1, :].to_broadcast([128, 64])), reads=[a_log], writes=[negA])
                P.op("act", lambda e: e.activation(out=negA[:], in_=negA[:], func=AF.Exp), reads=[negA], writes=[negA])
                P.op("dve", lambda e: e.tensor_scalar(out=negA[:], in0=negA[:], scalar1=-1.0, scalar2=None, op0=ALU.mult), reads=[negA], writes=[negA])
                P.dma("sp", lambda e: e.dma_start(out=dsk[:], in_=d_skip[0:1, :].to_broadcast([128, H])), reads=[d_skip], writes=[dsk])
                P.dma("sp", lambda e: e.dma_start(out=snw[:], in_=ssd_norm_w[0:1, :].to_broadcast([128, D])), reads=[ssd_norm_w], writes=[snw])

                xcr = Ring([P.sb("xc%d" % i, [128, 32, 128], F32) for i in range(2)])
                dtr = Ring([P.sb("dtc%d" % i, [128, 64], F32) for i in range(2)])
                adt = P.sb("adt", [128, 32], F32); nadt = P.sb("nadt", [128, 32], F32)
                esc = P.sb("esc", [128, 96], F32)
                xs_tm = P.sb("xs_tm", [128, D], F32)
                B_tm = P.sb("B_tm", [128, 1024], F32)
                Xm = P.sb("Xm", [128, D], F32); Xd = P.sb("Xd", [128, D], F32)
                hT = [P.sb("hT%d" % g, [128, 256], F32) for g in range(G)]
                yacc_t = ph.enter_context(nc.sbuf_tensor("yacc", [128, D], F32))
                yacc = [P.view(Buf("yacc", yacc_t, "sb"), yacc_t[:, g * 256:(g + 1) * 256]) for g in range(G)]
                cbt = Ring([P.sb("cbt%d" % i, [128, 128], F32) for i in range(2)])
                Eb = Ring([P.sb("Eb%d" % i, [128, 4, 128], F32) for i in range(2)])
                Mb = Ring([P.sb("Mb%d" % i, [128, 4, 128], F32) for i in range(2)])
                tmpb = Ring([P.sb("tmpb%d" % i, [128, 256], F32) for i in range(2)])
                yfc = P.sb("yfc", [128, D], F32)
                szc = P.sb("szc", [128, D], BF16)
                ysq = P.sb("ysq", [128, D], F32)
                gss = P.sb("gss", [128, G], F32)
                ynb = P.sb("ynb", [128, 16, 128], BF16)
                tpp = Ring([P.ps("tpp%d" % i, [128, 512], F32) for i in range(2)])
                smp = Ring([P.ps("smp%d" % i, [128, 128], F32) for i in range(2)])
                Dps = Ring([P.ps("Dps%d" % i, [128, 4, 128], F32) for i in range(2)])
                ydo = P.ps("ydo", [128, 512], F32)
                stp = P.ps("stp", [128, 256], F32)

                for dname, dof, Tin, Tout, NEGd in (("f", 0, Tle, Tgt, NEGf), ("b", 32, Tge, Tlt, NEGb)):
                    for g in range(G):
                        P.op("pool", lambda e, g=g: e.memset(hT[g][:], 0.0), writes=[hT[g]])
                    order = range(NCH) if dname == "f" else range(NCH - 1, -1, -1)
                    for c in order:
                        t0 = c * 128
                        xc = xcr.next(); dtc = dtr.next()
                        P.dma("sp", lambda e, xc=xc, t0=t0: e.dma_start(out=xc[:], in_=xbc_c[:, t0:t0 + 128].rearrange("(cc p) t -> p cc t", p=128)), reads=[xbc_c], writes=[xc])
                        P.dma("sp", lambda e, dtc=dtc, t0=t0: e.dma_start(out=dtc[:], in_=dts[t0:t0 + 128, :]), reads=[dts], writes=[dtc])
                        if dname == "b":
                            P.dma("sp", lambda e, t0=t0: e.dma_start(out=yfc[:], in_=yf[t0:t0 + 128, :]), reads=[yf], writes=[yfc])
                            P.dma("sp", lambda e, t0=t0: e.dma_start(out=szc[:], in_=sz[t0:t0 + 128, :]), reads=[sz], writes=[szc])
                        P.op("dve", lambda e, dtc=dtc, dof=dof: e.tensor_tensor(out=adt[:], in0=dtc[:, dof:dof + 32], in1=negA[:, dof:dof + 32], op=ALU.mult), reads=[dtc, negA], writes=[adt])
                        P.op("dve", lambda e: e.tensor_scalar(out=nadt[:], in0=adt[:], scalar1=-1.0, scalar2=None, op0=ALU.mult), reads=[adt], writes=[nadt])
                        sp_ = smp.next()
                        P.op("pe", lambda e, sp_=sp_, Tin=Tin: e.matmul(sp_[:, 0:32], lhsT=Tin[:], rhs=adt[:], start=True, stop=True), reads=[Tin, adt], writes=[sp_])
                        P.op("pe", lambda e, sp_=sp_, Tout=Tout: e.matmul(sp_[:, 32:64], lhsT=Tout[:], rhs=adt[:], start=True, stop=True), reads=[Tout, adt], writes=[sp_])
                        P.op("pe", lambda e, sp_=sp_: e.matmul(sp_[:, 64:96], lhsT=ones[:], rhs=adt[:], start=True, stop=True), reads=[ones, adt], writes=[sp_])
                        P.op("act", lambda e, sp_=sp_: e.activation(out=esc[:], in_=sp_[:, 0:96], func=AF.Exp), reads=[sp_], writes=[esc])
                        for q in range(6):
                            tp = tpp.next()
                            for j in range(4):
                                cc = q * 4 + j
                                P.op("pe", lambda e, tp=tp, j=j, cc=cc, xc=xc: e.transpose(out=tp[:, j * 128:(j + 1) * 128], in_=xc[:, cc, :], identity=identf[:]), reads=[xc, identf], writes=[tp])
                            if q < 4:
                                P.op("act", lambda e, tp=tp, q=q: e.copy(out=xs_tm[:, q * 512:(q + 1) * 512], in_=tp[:]), reads=[tp], writes=[xs_tm])
                            else:
                                P.op("act", lambda e, tp=tp, q=q: e.copy(out=B_tm[:, (q - 4) * 512:(q - 3) * 512], in_=tp[:]), reads=[tp], writes=[B_tm])
                        xs3 = xs_tm[:].rearrange("p (h d) -> p h d", d=PD)
                        P.op("dve", lambda e, dtc=dtc, dof=dof, xs3=xs3: e.tensor_tensor(out=Xm[:].rearrange("p (h d) -> p h d", d=PD), in0=xs3, in1=dtc[:, dof:dof + 32].unsqueeze(2).to_broadcast([128, H, PD]), op=ALU.mult), reads=[xs_tm, dtc], writes=[Xm])
                        P.op("pool", lambda e: e.tensor_tensor(out=Xd[:].rearrange("p (h d) -> p h d", d=PD), in0=Xm[:].rearrange("p (h d) -> p h d", d=PD), in1=esc[:, 32:64].unsqueeze(2).to_broadcast([128, H, PD]), op=ALU.mult), reads=[Xm, esc], writes=[Xd])
                        for g in range(G):
                            BT = xc[:, 16 + g, :]; CT = xc[:, 24 + g, :]
                            cp = smp.next(); cb_ = cbt.next(); dp = Dps.next(); Et = Eb.next(); Mt = Mb.next(); tb = tmpb.next()
                            P.op("pe", lambda e, cp=cp, BT=BT, CT=CT: e.matmul(cp[:], lhsT=BT, rhs=CT, start=True, stop=True), reads=[xc], writes=[cp])
                            P.op("act", lambda e, cp=cp, cb_=cb_: e.copy(out=cb_[:], in_=cp[:]), reads=[cp], writes=[cb_])
                            P.op("pe", lambda e, dp=dp, NEGd=NEGd: e.matmul(dp[:], lhsT=identf[:], rhs=NEGd[:], start=True, stop=False, skip_group_check=True), reads=[identf, NEGd], writes=[dp])
                            for r in range(R):
                                h = g * R + r
                                P.op("pe", lambda e, dp=dp, r=r, h=h, Tin=Tin: e.matmul(dp[:, r, :], lhsT=adt[:, h:h + 1].to_broadcast([128, 128]), rhs=Tin[:], start=False, stop=False, skip_group_check=True), reads=[adt, Tin], writes=[dp])
                                P.op("pe", lambda e, dp=dp, r=r, h=h, Tin=Tin: e.matmul(dp[:, r, :], lhsT=Tin[:], rhs=nadt[:, h:h + 1].to_broadcast([128, 128]), start=False, stop=(r == R - 1), skip_group_check=True), reads=[nadt, Tin], writes=[dp])
                            P.op("act", lambda e, dp=dp, Et=Et: e.activation(out=Et[:], in_=dp[:], func=AF.Exp), reads=[dp], writes=[Et])
                            P.op("dve", lambda e, Et=Et, Mt=Mt, cb_=cb_: e.tensor_tensor(out=Mt[:], in0=Et[:], in1=cb_[:].unsqueeze(1).to_broadcast([128, R, 128]), op=ALU.mult), reads=[Et, cb_], writes=[Mt])
                            for r in range(R):
                                h = g * R + r
                                P.op("pe", lambda e, Mt=Mt, r=r, h=h: e.matmul(ydo[:, r * 64:(r + 1) * 64], lhsT=Mt[:, r, :], rhs=Xm[:, h * 64:(h + 1) * 64], start=True, stop=True), reads=[Mt, Xm], writes=[ydo])
                            P.op("pe", lambda e, CT=CT, g=g: e.matmul(ydo[:, 256:512], lhsT=CT, rhs=hT[g][:], start=True, stop=True), reads=[xc, hT[g]], writes=[ydo])
                            P.op("dve", lambda e, tb=tb, g=g: e.tensor_tensor(out=tb[:].rearrange("p (h d) -> p h d", d=PD), in0=ydo[:, 256:512].rearrange("p (h d) -> p h d", d=PD), in1=esc[:, g * R:(g + 1) * R].unsqueeze(2).to_broadcast([128, R, PD]), op=ALU.mult), reads=[ydo, esc], writes=[tb])
                            P.op("dve", lambda e, tb=tb, g=g: e.tensor_tensor(out=yacc[g][:], in0=ydo[:, 0:256], in1=tb[:], op=ALU.add), reads=[ydo, tb], writes=[yacc[g]])
                            P.op("pe", lambda e, g=g: e.matmul(stp[:], lhsT=B_tm[:, g * 128:(g + 1) * 128], rhs=Xd[:, g * 256:(g + 1) * 256], start=True, stop=True), reads=[B_tm, Xd], writes=[stp])
                            P.op("pool", lambda e, g=g: e.tensor_tensor(out=hT[g][:].rearrange("p (h d) -> p h d", d=PD), in0=hT[g][:].rearrange("p (h d) -> p h d", d=PD), in1=esc[:, 64 + g * R:64 + (g + 1) * R].unsqueeze(2).to_broadcast([128, R, PD]), op=ALU.mult), reads=[hT[g], esc], writes=[hT[g]])
                            P.op("dve", lambda e, g=g: e.tensor_tensor(out=hT[g][:], in0=hT[g][:], in1=stp[:], op=ALU.add), reads=[hT[g], stp], writes=[hT[g]])
                        if dname == "f":
                            P.dma("sp", lambda e, t0=t0: e.dma_start(out=yf[t0:t0 + 128, :], in_=yacc_t[:]), reads=yacc, writes=[yf], sembuf=yacc[0])
                        else:
                            ya = yacc_t
                            P.op("dve", lambda e: e.tensor_tensor(out=ya[:], in0=ya[:], in1=yfc[:], op=ALU.add), reads=yacc + [yfc], writes=yacc)
                            P.op("pool", lambda e, xs3=xs3: e.tensor_tensor(out=ysq[:].rearrange("p (h d) -> p h d", d=PD), in0=xs3, in1=dsk[:].unsqueeze(2).to_broadcast([128, H, PD]), op=ALU.mult), reads=[xs_tm, dsk], writes=[ysq])
                            P.op("dve", lambda e: e.tensor_tensor(out=ya[:], in0=ya[:], in1=ysq[:], op=ALU.add), reads=yacc + [ysq], writes=yacc)
                            P.op("dve", lambda e: e.tensor_tensor(out=ya[:], in0=ya[:], in1=szc[:], op=ALU.mult), reads=yacc + [szc], writes=yacc)
                            P.op("pool", lambda e: e.tensor_tensor(out=ysq[:], in0=ya[:], in1=ya[:], op=ALU.mult), reads=yacc, writes=[ysq])
                            P.op("dve", lambda e: e.tensor_reduce(out=gss[:], in_=ysq[:].rearrange("p (g d) -> p g d", d=256), axis=AX.X, op=ALU.add), reads=[ysq], writes=[gss])
                            P.op("act", lambda e: e.activation(out=gss[:], in_=gss[:], func=AF.Sqrt, bias=epsb[:, 0:1], scale=1.0 / 256), reads=[gss, epsb], writes=[gss])
                            P.op("dve", lambda e: e.reciprocal(out=gss[:], in_=gss[:]), reads=[gss], writes=[gss])
                            P.op("dve", lambda e: e.tensor_tensor(out=ya[:].rearrange("p (g d) -> p g d", d=256), in0=ya[:].rearrange("p (g d) -> p g d", d=256), in1=gss[:].unsqueeze(2).to_broadcast([128, G, 256]), op=ALU.mult), reads=yacc + [gss], writes=yacc)
                            P.op("pool", lambda e: e.tensor_tensor(out=ya[:], in0=ya[:], in1=snw[:], op=ALU.mult), reads=yacc + [snw], writes=yacc)
                            for q in range(4):
                                tp = tpp.next()
                                for j in range(4):
                                    kc = q * 4 + j
                                    P.op("pe", lambda e, tp=tp, j=j, kc=kc: e.transpose(out=tp[:, j * 128:(j + 1) * 128], in_=ya[:, kc * 128:(kc + 1) * 128], identity=identf[:]), reads=yacc + [identf], writes=[tp])
                                P.op("act", lambda e, tp=tp, q=q: e.copy(out=ynb[:, q * 4:(q + 1) * 4, :], in_=tp[:].rearrange("p (a b) -> p a b", b=128)), reads=[tp], writes=[ynb])
                            P.dma("sp", lambda e, t0=t0: e.dma_start(out=ynT[:, t0:t0 + 128].rearrange("(kc p) t -> p kc t", p=128), in_=ynb[:]), reads=[ynb], writes=[ynT])
                P.end_phase()


        if 4 in phases:
            with ExitStack() as ph:
                P.stack = ph
                TA = 512
                NTA = S // TA
                wpg = P.sb("wpg", [128, 8, 256], BF16)
                psc = P.sb("psc", [128, 8], F32)
                ped = P.sb("ped", [128, 4, 16], F32)
                P.dma("sp", lambda e: e.dma_start(out=wpg[:], in_=wpg_d[:]), reads=[wpg_d], writes=[wpg])
                P.dma("sp", lambda e: e.dma_start(out=psc[:], in_=pool_scale[0, :].rearrange("(cc p) -> p cc", p=128), allow_slow_non_contiguous=True), reads=[pool_scale], writes=[psc])
                P.dma("sp", lambda e: e.dma_start(out=ped[:].rearrange("p a b -> p (a b)"), in_=pedge[0:1, :].to_broadcast([128, 64])), reads=[pedge], writes=[ped])
                ynt_r = Ring([P.sb("ynt%d" % i, [128, 16, TA], BF16) for i in range(1)])
                xpt = P.sb("xpt", [128, 8, TA + 16], BF16)
                lv = [P.sb("lv%d" % i, [128, 2, TA + 16], F32) for i in range(2)]
                pooledT = P.sb("pooledT", [128, 8, TA], BF16)
                p2T = P.sb("p2T", [128, 8, TA], BF16)
                gtr = Ring([P.sb("gt%d" % i, [128, 8, TA], BF16) for i in range(2)])
                merged = P.sb("merged", [128, 16, TA], BF16)
                t1r = Ring([P.sb("t1_%d" % i, [128, TA], F32) for i in range(2)])
                t2r = Ring([P.sb("t2_%d" % i, [128, TA], F32) for i in range(2)])
                wbk = Ring([P.sb("wb3_%d" % i, [128, 16, 512], BF16) for i in range(3)])
                wpbk = Ring([P.sb("wpb%d" % i, [128, 8, 512], BF16) for i in range(2)])
                xsr = Ring([P.sb("xs3_%d" % i, [128, D], F32) for i in range(2)])
                acc = Ring([P.ps("acc3_%d" % i, [128, 512], F32) for i in range(6)])
                WINS = (2, 4, 8, 16)
                for ti in range(NTA):
                    t0 = ti * TA
                    ynt = ynt_r.next()
                    P.dma("sp", lambda e, ynt=ynt, t0=t0: e.dma_start(out=ynt[:], in_=ynT[:, t0:t0 + TA].rearrange("(kc p) t -> p kc t", p=128)), reads=[ynT], writes=[ynt])
                    P.dma("sp", lambda e, t0=t0: e.dma_start(out=xpt[:], in_=xp_raw[:, t0:t0 + TA + 16].rearrange("(cc p) t -> p cc t", p=128)), reads=[xp_raw], writes=[xpt])
                    for gi, w in enumerate(WINS):
                        src = xpt[:, 2 * gi:2 * gi + 2, :]
                        L = TA + 16
                        step = 1
                        cur = None
                        li = 0
                        while step < w:
                            dst = lv[li % 2]
                            a_in = src if cur is None else cur[:, :, :]
                            rd = [xpt] if cur is None else [cur]
                            P.op("pool", lambda e, dst=dst, a_in=a_in, L=L, step=step: e.tensor_tensor(out=dst[:, :, 0:L - step], in0=a_in[:, :, 0:L - step], in1=a_in[:, :, step:L], op=ALU.add), reads=rd, writes=[dst])
                            cur = dst; L -= step; step *= 2; li += 1
                        off = 8 - w // 2
                        if ti == 0:
                            P.op("pool", lambda e, cur=cur, off=off, gi=gi: e.tensor_tensor(out=cur[:, :, off:off + 8], in0=cur[:, :, off:off + 8], in1=ped[:, gi, 0:8].unsqueeze(1).to_broadcast([128, 2, 8]), op=ALU.mult), reads=[cur, ped], writes=[cur])
                        if ti == NTA - 1:
                            P.op("pool", lambda e, cur=cur, off=off, gi=gi: e.tensor_tensor(out=cur[:, :, off + TA - 8:off + TA], in0=cur[:, :, off + TA - 8:off + TA], in1=ped[:, gi, 8:16].unsqueeze(1).to_broadcast([128, 2, 8]), op=ALU.mult), reads=[cur, ped], writes=[cur])
                        P.op("dve", lambda e, cur=cur, off=off, gi=gi, w=w: e.scalar_tensor_tensor(out=pooledT[:, 2 * gi:2 * gi + 2, :], in0=cur[:, :, off:off + TA], scalar=1.0 / w, in1=xpt[:, 2 * gi:2 * gi + 2, 8:8 + TA], op0=ALU.mult, op1=ALU.subtract), reads=[cur, xpt], writes=[pooledT])
                    for gi in range(4):
                        for dc in range(2):
                            a = acc.next()
                            for kc in range(2):
                                P.op("pe", lambda e, a=a, gi=gi, dc=dc, kc=kc: e.matmul(a[:, 0:TA], lhsT=wpg[:, gi * 2 + kc, dc * 128:(dc + 1) * 128], rhs=pooledT[:, gi * 2 + kc, :], start=(kc == 0), stop=(kc == 1)), reads=[wpg, pooledT], writes=[a])
                            P.op("act", lambda e, a=a, gi=gi, dc=dc: e.activation(out=p2T[:, gi * 2 + dc, :], in_=a[:, 0:TA], func=AF.Copy, scale=psc[:, gi * 2 + dc:gi * 2 + dc + 1]), reads=[a, psc], writes=[p2T])
                    for b in range(4):
                        wb = wbk.next(); wp = wpbk.next(); gt = gtr.next()
                        P.dma("sp", lambda e, wb=wb, b=b: e.dma_start(out=wb[:], in_=wssd_d[b][:]), reads=[wssd_d[b]], writes=[wb])
                        P.dma("sp", lambda e, wp=wp, b=b: e.dma_start(out=wp[:], in_=wpb_d[b][:]), reads=[wpb_d[b]], writes=[wp])
                        P.dma("sp", lambda e, gt=gt, b=b, t0=t0: e.dma_start(out=gt[:, 0:4, :], in_=gT[b * 512:(b + 1) * 512, t0:t0 + TA].rearrange("(cc p) t -> p cc t", p=128)), reads=[gT], writes=[gt])
                        P.dma("sp", lambda e, gt=gt, b=b, t0=t0: e.dma_start(out=gt[:, 4:8, :], in_=gT[D + b * 512:D + (b + 1) * 512, t0:t0 + TA].rearrange("(cc p) t -> p cc t", p=128)), reads=[gT], writes=[gt])
                        for cc in range(4):
                            dch = b * 4 + cc
                            a1 = acc.next(); a2 = acc.next()
                            for kc in range(16):
                                P.op("pe", lambda e, a1=a1, wb=wb, kc=kc, cc=cc, ynt=ynt: e.matmul(a1[:, 0:TA], lhsT=wb[:, kc, cc * 128:(cc + 1) * 128], rhs=ynt[:, kc, :], start=(kc == 0), stop=(kc == 15)), reads=[wb, ynt], writes=[a1])
                            for kc in range(8):
                                P.op("pe", lambda e, a2=a2, wp=wp, kc=kc, cc=cc: e.matmul(a2[:, 0:TA], lhsT=wp[:, kc, cc * 128:(cc + 1) * 128], rhs=p2T[:, kc, :], start=(kc == 0), stop=(kc == 7)), reads=[wp, p2T], writes=[a2])
                            t1 = t1r.next(); t2 = t2r.next()
                            P.op("dve", lambda e, a1=a1, t1=t1, gt=gt, cc=cc: e.tensor_tensor(out=t1[:], in0=a1[:, 0:TA], in1=gt[:, cc, :], op=ALU.mult), reads=[a1, gt], writes=[t1])
                            P.op("dve", lambda e, a2=a2, t2=t2, gt=gt, cc=cc: e.tensor_tensor(out=t2[:], in0=a2[:, 0:TA], in1=gt[:, 4 + cc, :], op=ALU.mult), reads=[a2, gt], writes=[t2])
                            P.op("pool", lambda e, t1=t1, t2=t2, dch=dch: e.tensor_tensor(out=merged[:, dch, :], in0=t1[:], in1=t2[:], op=ALU.add), reads=[t1, t2], writes=[merged])
                    xss = []
                    for sub in range(TA // 128):
                        xs_ = xsr.next() if sub < 2 else None
                        xss.append(xs_)
                    for pair in range(TA // 256):
                        subs = (2 * pair, 2 * pair + 1)
                        xt = {}
                        for sub in subs:
                            xt[sub] = xsr.next()
                            P.dma("sp", lambda e, xs_=xt[sub], sub=sub, t0=t0: e.dma_start(out=xs_[:], in_=x[t0 + sub * 128:t0 + (sub + 1) * 128, :]), reads=[x], writes=[xt[sub]])
                        for b in range(4):
                            wb = wbk.next()
                            P.dma("sp", lambda e, wb=wb, b=b: e.dma_start(out=wb[:], in_=wout_d[b][:]), reads=[wout_d[b]], writes=[wb])
                            for sub in subs:
                                a = acc.next()
                                for kc in range(16):
                                    P.op("pe", lambda e, a=a, wb=wb, kc=kc, sub=sub: e.matmul(a[:], lhsT=merged[:, kc, sub * 128:(sub + 1) * 128], rhs=wb[:, kc, :], start=(kc == 0), stop=(kc == 15)), reads=[wb, merged], writes=[a])
                                P.op("dve", lambda e, a=a, xs_=xt[sub], b=b: e.tensor_tensor(out=xs_[:, b * 512:(b + 1) * 512], in0=a[:], in1=xs_[:, b * 512:(b + 1) * 512], op=ALU.add), reads=[a, xt[sub]], writes=[xt[sub]])
                        for sub in subs:
                            P.dma("sp", lambda e, xs_=xt[sub], sub=sub, t0=t0: e.dma_start(out=hsc[t0 + sub * 128:t0 + (sub + 1) * 128, :], in_=xs_[:]), reads=[xt[sub]], writes=[hsc])
                P.end_phase()


        if 5 in phases:
            with ExitStack() as ph:
                P.stack = ph
                wq = P.sb("wq", [128, 4, 16, 512], BF16)
                qtm = P.sb("qtm", [128, D], F32)
                keysN = Buf("keysN", qtm.t[:].rearrange("p (h d) -> p h d", d=128), "sb")
                keysT = P.sb("keysT", [128, 16, 128], F32)
                fw = P.sb("fw", [128, D], F32)
                fnw = P.sb("fnw", [128, D], F32)
                epsb = P.sb("epsb", [128, 1], F32)
                identf = P.sb("identf", [128, 128], F32)
                identb = P.sb("identb", [128, 128], BF16)
                iot_i = P.sb("iot_i", [128, 16], I32)
                iot = P.sb("iot", [128, 16], F32)
                for b in range(4):
                    P.dma("sp", lambda e, b=b: e.dma_start(out=wq[:, b, :, :], in_=wq_d[b][:]), reads=[wq_d[b]], writes=[wq])
                P.dma("sp", lambda e: e.dma_start(out=keysN[:], in_=sub_keys[:].rearrange("h n d -> n h d")), reads=[sub_keys], writes=[qtm])
                P.dma("sp", lambda e: e.dma_start(out=fw[:], in_=ffn_norm_w[0:1, :].to_broadcast([128, D])), reads=[ffn_norm_w], writes=[fw])
                P.dma("sp", lambda e: e.dma_start(out=fnw[:], in_=final_norm_w[0:1, :].to_broadcast([128, D])), reads=[final_norm_w], writes=[fnw])
                P.op("pool", lambda e: e.memset(epsb[:], EPS), writes=[epsb])
                P.op("pool", lambda e: e.memset(identf[:], 0.0), writes=[identf])
                P.op("pool", lambda e: e.affine_select(out=identf[:], in_=identf[:], pattern=[[-1, 128]], compare_op=ALU.not_equal, fill=1.0, base=0, channel_multiplier=1), reads=[identf], writes=[identf])
                P.op("dve", lambda e: e.tensor_copy(out=identb[:], in_=identf[:]), reads=[identf], writes=[identb])
                P.op("pool", lambda e: e.iota(iot_i[:], pattern=[[1, 16]], base=0, channel_multiplier=0), writes=[iot_i])
                P.op("dve", lambda e: e.tensor_copy(out=iot[:], in_=iot_i[:]), reads=[iot_i], writes=[iot])
                tpq = Ring([P.ps("tpq%d" % i, [128, 4, 128], F32) for i in range(2)])
                tpb = Ring([P.ps("tpb%d" % i, [128, 8, 128], BF16) for i in range(2)])
                accq = Ring([P.ps("accq%d" % i, [128, 512], F32) for i in range(2)])
                for q4 in range(4):
                    tp = tpq.next()
                    for j in range(4):
                        hh = q4 * 4 + j
                        P.op("pe", lambda e, tp=tp, j=j, hh=hh: e.transpose(out=tp[:, j, :], in_=keysN[:, hh, :], identity=identf[:]), reads=[qtm, identf], writes=[tp])
                    P.op("act", lambda e, tp=tp, q4=q4: e.copy(out=keysT[:, q4 * 4:(q4 + 1) * 4, :], in_=tp[:]), reads=[tp], writes=[keysT])

                hr = Ring([P.sb("h4_%d" % i, [128, D], F32) for i in range(2)])
                hn = P.sb("hn", [128, D], F32)
                hnb = P.sb("hnb", [128, D], BF16)
                junkf = P.sb("junkf", [128, D], F32)
                ssq = P.sb("ssq4", [128, 1], F32)
                hnT = P.sb("hnT", [128, 16, 128], BF16)
                qT = P.sb("qT", [128, 16, 128], F32)
                sc = P.sb("sc", [128, 16, 128], F32)
                wk = P.sb("wk", [128, 128], F32)
                v1 = P.sb("v1", [128, 16, 16], F32)
                i1 = P.sb("i1", [128, 16, 16], U32)
                i1f = P.sb("i1f", [128, 16, 16], F32)
                cand = P.sb("cand", [128, 8, 256], F32)
                cwk = P.sb("cwk", [128, 256], F32)
                tv = P.sb("tv", [128, 8, 16], F32)
                pos = P.sb("pos", [128, 8, 16], U32)
                posf = P.sb("posf", [128, 8, 16], F32)
                r1f = P.sb("r1f", [128, 8, 16], F32)
                r1u = P.sb("r1u", [128, 8, 16], U32)
                r2u = P.sb("r2u", [128, 8, 16], U32)
                r2f = P.sb("r2f", [128, 8, 16], F32)
                eqb = Buf("eqb", cand.t[:].rearrange("p h (a b) -> p h a b", b=16), "sb")
                sel1 = P.sb("sel1", [128, 8, 16], F32)
                sel2 = P.sb("sel2", [128, 8, 16], F32)
                idx = P.sb("idx", [128, 128], I32)
                gexp = P.sb("gexp", [128, 8, 16], F32)
                gsum = P.sb("gsum", [128, 8], F32)
                gate = P.sb("gate", [128, 8, 16], F32)
                av = P.sb("av", [128, 128], F32)
                coef = P.sb("coef", [128, 128], F32)
                pacc = P.sb("pacc", [128, D], F32)
                ur = Ring([P.sb("ug%d" % i, [128, D], F32) for i in range(3)])
                vr = ur

                for c in range(NCH):
                    t0 = c * 128
                    ht = hr.next()
                    P.dma("sp", lambda e, ht=ht, t0=t0: e.dma_start(out=ht[:], in_=hsc[t0:t0 + 128, :]), reads=[hsc], writes=[ht])
                    P.op("act", lambda e, ht=ht: e.activation(out=junkf[:], in_=ht[:], func=AF.Square, accum_out=ssq[:]), reads=[ht], writes=[junkf, ssq])
                    P.op("act", lambda e: e.activation(out=ssq[:], in_=ssq[:], func=AF.Sqrt, bias=epsb[:, 0:1], scale=1.0 / D), reads=[ssq, epsb], writes=[ssq])
                    P.op("dve", lambda e: e.reciprocal(out=ssq[:], in_=ssq[:]), reads=[ssq], writes=[ssq])
                    P.op("dve", lambda e, ht=ht: e.scalar_tensor_tensor(out=hn[:], in0=ht[:], scalar=ssq[:, 0:1], in1=fw[:], op0=ALU.mult, op1=ALU.mult), reads=[ht, ssq, fw], writes=[hn])
                    P.op("act", lambda e: e.copy(out=hnb[:], in_=hn[:]), reads=[hn], writes=[hnb])
                    for half in range(2):
                        tp = tpb.next()
                        for j in range(8):
                            kc = half * 8 + j
                            P.op("pe", lambda e, tp=tp, j=j, kc=kc: e.transpose(out=tp[:, j, :], in_=hnb[:, kc * 128:(kc + 1) * 128], identity=identb[:]), reads=[hnb, identb], writes=[tp])
                        P.op("act", lambda e, tp=tp, half=half: e.copy(out=hnT[:, half * 8:(half + 1) * 8, :], in_=tp[:]), reads=[tp], writes=[hnT])
                    for b in range(4):
                        a = accq.next()
                        for kc in range(16):
                            P.op("pe", lambda e, a=a, b=b, kc=kc: e.matmul(a[:], lhsT=hnT[:, kc, :], rhs=wq[:, b, kc, :], start=(kc == 0), stop=(kc == 15)), reads=[hnT, wq], writes=[a])
                        P.op("act", lambda e, a=a, b=b: e.copy(out=qtm[:, b * 512:(b + 1) * 512], in_=a[:]), reads=[a], writes=[qtm])
                    for q4 in range(4):
                        tp = tpq.next()
                        for j in range(4):
                            hh = q4 * 4 + j
                            P.op("pe", lambda e, tp=tp, j=j, hh=hh: e.transpose(out=tp[:, j, :], in_=qtm[:, hh * 128:(hh + 1) * 128], identity=identf[:]), reads=[qtm, identf], writes=[tp])
                        P.op("act", lambda e, tp=tp, q4=q4: e.copy(out=qT[:, q4 * 4:(q4 + 1) * 4, :], in_=tp[:]), reads=[tp], writes=[qT])
                    for q4 in range(4):
                        tp = tpq.next()
                        for j in range(4):
                            hh = q4 * 4 + j
                            P.op("pe", lambda e, tp=tp, j=j, hh=hh: e.matmul(tp[:, j, :], lhsT=qT[:, hh, :], rhs=keysT[:, hh, :], start=True, stop=True), reads=[qT, keysT], writes=[tp])
                        P.op("act", lambda e, tp=tp, q4=q4: e.copy(out=sc[:, q4 * 4:(q4 + 1) * 4, :], in_=tp[:]), reads=[tp], writes=[sc])
                    for hh in range(16):
                        P.op("dve", lambda e, hh=hh: e.max(out=v1[:, hh, 0:8], in_=sc[:, hh, :]), reads=[sc], writes=[v1])
                        P.op("dve", lambda e, hh=hh: e.max_index(out=i1[:, hh, 0:8], in_max=v1[:, hh, 0:8], in_values=sc[:, hh, :]), reads=[sc, v1], writes=[i1])
                        P.op("dve", lambda e, hh=hh: e.match_replace(out=wk[:], in_to_replace=v1[:, hh, 0:8], in_values=sc[:, hh, :], imm_value=-1e30), reads=[sc, v1], writes=[wk])
                        P.op("dve", lambda e, hh=hh: e.max(out=v1[:, hh, 8:16], in_=wk[:]), reads=[wk], writes=[v1])
                        P.op("dve", lambda e, hh=hh: e.max_index(out=i1[:, hh, 8:16], in_max=v1[:, hh, 8:16], in_values=wk[:]), reads=[wk, v1], writes=[i1])
                    v4 = v1[:].rearrange("p (h two) r -> p h two r", two=2)
                    P.op("dve", lambda e, v4=v4: e.tensor_tensor(out=cand[:].rearrange("p h (a b) -> p h a b", b=16), in0=v4[:, :, 0, :].unsqueeze(3).to_broadcast([128, 8, 16, 16]), in1=v4[:, :, 1, :].unsqueeze(2).to_broadcast([128, 8, 16, 16]), op=ALU.add), reads=[v1], writes=[cand])
                    for h in range(8):
                        P.op("dve", lambda e, h=h: e.max(out=tv[:, h, 0:8], in_=cand[:, h, :]), reads=[cand], writes=[tv])
                        P.op("dve", lambda e, h=h: e.max_index(out=pos[:, h, 0:8], in_max=tv[:, h, 0:8], in_values=cand[:, h, :]), reads=[cand, tv], writes=[pos])
                        P.op("dve", lambda e, h=h: e.match_replace(out=cwk[:], in_to_replace=tv[:, h, 0:8], in_values=cand[:, h, :], imm_value=-1e30), reads=[cand, tv], writes=[cwk])
                        P.op("dve", lambda e, h=h: e.max(out=tv[:, h, 8:16], in_=cwk[:]), reads=[cwk], writes=[tv])
                        P.op("dve", lambda e, h=h: e.max_index(out=pos[:, h, 8:16], in_max=tv[:, h, 8:16], in_values=cwk[:]), reads=[cwk, tv], writes=[pos])
                    P.op("dve", lambda e: e.tensor_tensor(out=gexp[:], in0=tv[:], in1=tv[:, :, 0:1].to_broadcast([128, 8, 16]), op=ALU.subtract), reads=[tv], writes=[gexp])
                    P.op("act", lambda e: e.activation(out=gexp[:], in_=gexp[:], func=AF.Exp), reads=[gexp], writes=[gexp])
                    P.op("dve", lambda e: e.tensor_reduce(out=gsum[:], in_=gexp[:], axis=AX.X, op=ALU.add), reads=[gexp], writes=[gsum])
                    P.op("dve", lambda e: e.reciprocal(out=gsum[:], in_=gsum[:]), reads=[gsum], writes=[gsum])
                    P.op("dve", lambda e: e.tensor_tensor(out=gate[:], in0=gexp[:], in1=gsum[:].unsqueeze(2).to_broadcast([128, 8, 16]), op=ALU.mult), reads=[gexp, gsum], writes=[gate])
                    P.op("dve", lambda e: e.tensor_copy(out=posf[:], in_=pos[:]), reads=[pos], writes=[posf])
                    P.op("dve", lambda e: e.tensor_copy(out=i1f[:], in_=i1[:]), reads=[i1], writes=[i1f])
                    P.op("dve", lambda e: e.tensor_single_scalar(out=r1u[:], in_=pos[:], scalar=4, op=ALU.logical_shift_right), reads=[pos], writes=[r1u])
                    P.op("dve", lambda e: e.tensor_single_scalar(out=r2u[:], in_=pos[:], scalar=15, op=ALU.bitwise_and), reads=[pos], writes=[r2u])
                    P.op("dve", lambda e: e.tensor_copy(out=r1f[:], in_=r1u[:]), reads=[r1u], writes=[r1f])
                    P.op("dve", lambda e: e.tensor_copy(out=r2f[:], in_=r2u[:]), reads=[r2u], writes=[r2f])
                    i4 = i1f[:].rearrange("p (h two) r -> p h two r", two=2)
                    iot4 = iot[:].unsqueeze(1).unsqueeze(1).to_broadcast([128, 8, 16, 16])
                    for rf, two, sel in ((r1f, 0, sel1), (r2f, 1, sel2)):
                        P.op("dve", lambda e, rf=rf: e.tensor_tensor(out=eqb[:], in0=rf[:].unsqueeze(3).to_broadcast([128, 8, 16, 16]), in1=iot4, op=ALU.is_equal), reads=[rf, iot], writes=[cand])
                        P.op("dve", lambda e, two=two, i4=i4: e.tensor_tensor(out=eqb[:], in0=eqb[:], in1=i4[:, :, two, :].unsqueeze(2).to_broadcast([128, 8, 16, 16]), op=ALU.mult), reads=[cand, i1f], writes=[cand])
                        P.op("dve", lambda e, sel=sel: e.tensor_reduce(out=sel[:], in_=eqb[:], axis=AX.X, op=ALU.add), reads=[cand], writes=[sel])
                    P.op("dve", lambda e: e.scalar_tensor_tensor(out=sel1[:], in0=sel1[:], scalar=128.0, in1=sel2[:], op0=ALU.mult, op1=ALU.add), reads=[sel1, sel2], writes=[sel1])
                    P.op("dve", lambda e: e.tensor_copy(out=idx[:], in_=sel1[:].rearrange("p h k -> p (h k)")), reads=[sel1], writes=[idx])
                    for j in range(128):
                        uj = ur.next()
                        P.dma("pool", lambda e, uj=uj, j=j: e.indirect_dma_start(out=uj[:], out_offset=None, in_=expert_u[:, :], in_offset=bass.IndirectOffsetOnAxis(ap=idx[:, j:j + 1], axis=0)), reads=[expert_u, idx], writes=[uj])
                        P.op("dve", lambda e, uj=uj, j=j: e.scalar_tensor_tensor(out=junkf[:], in0=uj[:], scalar=1.0, in1=hn[:], op0=ALU.mult, op1=ALU.mult, accum_out=av[:, j:j + 1]), reads=[uj, hn], writes=[junkf, av])
                    P.op("act", lambda e: e.activation(out=coef[:], in_=av[:], func=AF.Gelu), reads=[av], writes=[coef])
                    P.op("dve", lambda e: e.tensor_tensor(out=coef[:], in0=coef[:], in1=gate[:].rearrange("p h k -> p (h k)"), op=ALU.mult), reads=[coef, gate], writes=[coef])
                    for j in range(128):
                        vj = vr.next()
                        P.dma("pool", lambda e, vj=vj, j=j: e.indirect_dma_start(out=vj[:], out_offset=None, in_=expert_v[:, :], in_offset=bass.IndirectOffsetOnAxis(ap=idx[:, j:j + 1], axis=0)), reads=[expert_v, idx], writes=[vj])
                        if j == 0:
                            P.op("dve", lambda e, vj=vj: e.tensor_scalar(out=pacc[:], in0=vj[:], scalar1=coef[:, 0:1], scalar2=None, op0=ALU.mult), reads=[vj, coef], writes=[pacc])
                        else:
                            P.op("dve", lambda e, vj=vj, j=j: e.scalar_tensor_tensor(out=pacc[:], in0=vj[:], scalar=coef[:, j:j + 1], in1=pacc[:], op0=ALU.mult, op1=ALU.add), reads=[vj, coef, pacc], writes=[pacc])
                    ot = hn
                    P.op("dve", lambda e, ht=ht: e.tensor_tensor(out=pacc[:], in0=pacc[:], in1=ht[:], op=ALU.add), reads=[pacc, ht], writes=[pacc])
                    P.op("act", lambda e: e.activation(out=junkf[:], in_=pacc[:], func=AF.Square, accum_out=ssq[:]), reads=[pacc], writes=[junkf, ssq])
                    P.op("act", lambda e: e.activation(out=ssq[:], in_=ssq[:], func=AF.Sqrt, bias=epsb[:, 0:1], scale=1.0 / D), reads=[ssq, epsb], writes=[ssq])
                    P.op("dve", lambda e: e.reciprocal(out=ssq[:], in_=ssq[:]), reads=[ssq], writes=[ssq])
                    P.op("dve", lambda e, ot=ot: e.scalar_tensor_tensor(out=ot[:], in0=pacc[:], scalar=ssq[:, 0:1], in1=fnw[:], op0=ALU.mult, op1=ALU.mult), reads=[pacc, ssq, fnw], writes=[ot])
                    P.dma("sp", lambda e, ot=ot, t0=t0: e.dma_start(out=out[t0:t0 + 128, :], in_=ot[:]), reads=[ot], writes=[out])
                    if debug:
                        P.dma("sp", lambda e, t0=t0: e.dma_start(out=idx_dbg[t0:t0 + 128, :], in_=idx[:]), reads=[idx], writes=[idx_dbg])
                        P.dma("sp", lambda e, t0=t0: e.dma_start(out=gate_dbg[t0:t0 + 128, :], in_=gate[:].rearrange("p h k -> p (h k)")), reads=[gate], writes=[gate_dbg])
                        P.dma("sp", lambda e, t0=t0: e.dma_start(out=a_dbg[t0:t0 + 128, :], in_=av[:]), reads=[av], writes=[a_dbg])
                P.end_phase()

        P.wait_all("sp", outs)
        P.emit()
    return nc


def _pedge_const(S):
    pe = np.ones((4, 16), np.float32)
    for gi, w in enumerate((2, 4, 8, 16)):
        for j in range(8):
            for t, col in ((j, j), (S - 8 + j, 8 + j)):
                lo = max(t - w // 2, 0)
                hi = min(t + w // 2, S)
                pe[gi, col] = w / float(hi - lo)
    return pe.reshape(1, 64)


_NC_CACHE = {}


def kernel(**inputs):
    x = np.ascontiguousarray(np.asarray(inputs["x"], dtype=np.float32))
    B, S, _ = x.shape
    f = lambda k: np.ascontiguousarray(np.asarray(inputs[k], dtype=np.float32))
    shared = dict(
        mixer_norm_w=f("mixer_norm_w").reshape(1, D),
        w_in=f("w_in").reshape(D, IPW),
        conv_w=f("conv_w").reshape(5, XBC),
        conv_b=f("conv_b").reshape(1, XBC),
        dt_bias=f("dt_bias").reshape(1, 64),
        a_log=f("a_log").reshape(1, 64),
        d_skip=f("d_skip").reshape(1, H),
        ssd_norm_w=f("ssd_norm_w").reshape(1, D),
        w_ssd_branch=f("w_ssd_branch").reshape(D, D),
        w_pool_group=f("w_pool_group").reshape(4, 256, 256),
        pool_scale=f("pool_scale").reshape(1, PW),
        w_pool_branch=f("w_pool_branch").reshape(PW, D),
        w_out=f("w_out").reshape(D, D),
        ffn_norm_w=f("ffn_norm_w").reshape(1, D),
        w_query=f("w_query").reshape(D, D),
        sub_keys=f("sub_keys").reshape(16, 128, 128),
        expert_u=f("expert_u").reshape(NE, D),
        expert_v=f("expert_v").reshape(NE, D),
        final_norm_w=f("final_norm_w").reshape(1, D),
        pedge=_pedge_const(S),
    )
    if S not in _NC_CACHE:
        _NC_CACHE[S] = build(S)
    nc = _NC_CACHE[S]
    in_maps = [dict(shared, x=x[b]) for b in range(B)]
    res = run_bass_kernel_spmd(nc, in_maps, core_ids=list(range(B)))
    return np.stack([np.asarray(r["out"], dtype=np.float32) for r in res.results], axis=0)
```

```python
import numpy as np
import concourse.bass as bass
import concourse.mybir as mybir
from contextlib import ExitStack
from concourse.bass_utils import run_bass_kernel_spmd

F32 = mybir.dt.float32
BF16 = mybir.dt.bfloat16
I32 = mybir.dt.int32
U32 = mybir.dt.uint32
AF = mybir.ActivationFunctionType
ALU = mybir.AluOpType
AX = mybir.AxisListType


class Buf:
    def __init__(self, name, t, space):
        self.name = name
        self.t = t
        self.space = space
        self.w = {}
        self.r = {}
        self.sem = None
        self.semcount = 0

    def __getitem__(self, idx):
        return self.t[idx]


class Prog:
    ENG = ("pe", "act", "dve", "pool", "sp")

    def __init__(self, nc, stack):
        self.nc = nc
        self.stack = stack
        self.streams = {e: [] for e in self.ENG}
        self.esem = {e: stack.enter_context(nc.semaphore("es_" + e)) for e in self.ENG}
        self.ecount = {e: 0 for e in self.ENG}
        self.waited = {e: {} for e in self.ENG}
        self.semobj = {}
        for e in self.ENG:
            self.semobj[("e", e)] = self.esem[e]
        self.nsem = 0
        self.nbuf = 0
        self.outer = stack
        self.allbufs = []
        self.sempool = []
        self.phase_bufs = []
        self._semcounts = {}

    def sb(self, name, shape, dtype):
        self.nbuf += 1
        name = "%s_%d" % (name, self.nbuf)
        t = self.stack.enter_context(self.nc.sbuf_tensor(name, list(shape), dtype))
        return self._reg(Buf(name, t, "sb"))

    def ps(self, name, shape, dtype):
        self.nbuf += 1
        name = "%s_%d" % (name, self.nbuf)
        t = self.stack.enter_context(self.nc.psum_tensor(name, list(shape), dtype))
        return self._reg(Buf(name, t, "ps"))

    def dram(self, name, shape, dtype, kind="Internal"):
        t = self.nc.dram_tensor(name, list(shape), dtype, kind=kind)
        return self._reg(Buf(name, t.ap(), "dram"))

    def view(self, buf, t):
        b = Buf(buf.name + "_v%d" % self.nbuf, t, buf.space)
        self.nbuf += 1
        return self._reg(b)

    def _reg(self, b):
        self.allbufs.append(b)
        return b

    def _bufsem(self, b):
        if b.sem is None:
            if self.sempool:
                b.sem, b.semcount = self.sempool.pop()
            else:
                b.sem = self.outer.enter_context(self.nc.semaphore("bs%d" % self.nsem))
                self.nsem += 1
            self.semobj[("b", id(b))] = b.sem
            self._semcounts[("b", id(b))] = (lambda b=b: b.semcount)
            b.key = ("b", id(b))
            self.phase_bufs.append(b)
        return b.key

    def end_phase(self, keep=()):
        self.barrier(skip=[b.key for b in keep if b.sem is not None])
        self.emit()
        kept = []
        for b in self.phase_bufs:
            if b in keep:
                kept.append(b)
                continue
            self.sempool.append((b.sem, b.semcount))
            del self.semobj[b.key]
            del self._semcounts[b.key]
            for e in self.ENG:
                self.waited[e].pop(b.key, None)
            b.sem = None
        self.phase_bufs = kept
        for b in self.allbufs:
            if b in keep:
                continue
            b.w = {}
            b.r = {}

    def _collect(self, eng, reads, writes, own_key, is_dma):
        need = {}

        def add(k, v):
            if v > need.get(k, 0):
                need[k] = v

        for b in reads:
            for k, v in b.w.items():
                if (not is_dma) and k == own_key and eng == "pe":
                    continue
                add(k, v)
        for b in writes:
            if b.space == "dram":
                continue
            for k, v in b.w.items():
                if k == own_key:
                    continue
                add(k, v)
            for k, v in b.r.items():
                if (not is_dma) and k == own_key:
                    continue
                add(k, v)
        wl = []
        cache = self.waited[eng]
        for k, v in need.items():
            if cache.get(k, 0) >= v:
                continue
            cache[k] = v
            wl.append((k, v))
        return wl

    def op(self, eng, fn, reads=(), writes=()):
        key = ("e", eng)
        waits = self._collect(eng, reads, writes, key, False)
        self.ecount[eng] += 1
        val = self.ecount[eng]
        self.streams[eng].append((waits, fn, key, 1))
        for b in reads:
            b.r[key] = val
        for b in writes:
            b.w = {key: val}
            b.r = {}

    def dma(self, q, fn, reads=(), writes=(), sembuf=None):
        if sembuf is None:
            cands = [b for b in list(writes) + list(reads) if b.space == "sb"]
            sembuf = cands[0] if cands else (list(writes) + list(reads))[0]
        key = self._bufsem(sembuf)
        waits = self._collect(q, reads, writes, key, True)
        sembuf.semcount += 16
        val = sembuf.semcount
        self.streams[q].append((waits, fn, key, 16))
        for b in reads:
            b.r[key] = val
        for b in writes:
            if b.space == "dram":
                b.w = dict(b.w)
                b.w[key] = val
            else:
                b.w = {key: val}
                b.r = {}

    def wait_all(self, eng, bufs):
        need = {}
        for b in bufs:
            for k, v in list(b.w.items()) + list(b.r.items()):
                if v > need.get(k, 0):
                    need[k] = v
        self.streams[eng].append((list(need.items()), None, None, 0))

    def barrier(self, skip=()):
        need = [(("e", e), self.ecount[e]) for e in self.ENG if self.ecount[e] > 0]
        for k, sem in self.semobj.items():
            if k[0] == "b" and k not in skip:
                need.append((k, self._semcounts[k]()))
        for e in self.ENG:
            wl = []
            for k, v in need:
                if k == ("e", e) or v == 0:
                    continue
                if self.waited[e].get(k, 0) >= v:
                    continue
                self.waited[e][k] = v
                wl.append((k, v))
            self.streams[e].append((wl, None, None, 0))

    def emit(self):
        nc = self.nc
        handles = {"pe": "tensor", "act": "scalar", "dve": "vector", "pool": "gpsimd", "sp": "sync"}
        with nc.Block() as block:
            for e in self.ENG:
                stream = self.streams[e]

                def body(eng, stream=stream):
                    for waits, fn, key, inc in stream:
                        for k, v in waits:
                            eng.wait_ge(self.semobj[k], v)
                        if fn is not None:
                            ins = fn(eng)
                            ins.then_inc(self.semobj[key], inc)

                getattr(block, handles[e])(body)
        self.streams = {e: [] for e in self.ENG}


D = 2048; H = 32; G = 8; R = 4; PD = 64; NS = 128; XBC = 4096; PW = 1024; IPW = 11328
O1 = 2048; O2 = 6144; O3 = 6208; O4 = 7232
NE = 16384; TOPK = 16
EPS = 1e-6

BLOCKS = ([(c, 512, "z") for c in range(0, 2048, 512)] + [(c, 512, "xbc") for c in range(O1, O2, 512)]
          + [(O2, 64, "dt")] + [(c, 512, "xp") for c in range(O3, O4, 512)]
          + [(c, 512, "gate") for c in range(O4, IPW, 512)])


class Ring:
    def __init__(self, bufs):
        self.bufs = bufs
        self.i = 0

    def next(self):
        b = self.bufs[self.i % len(self.bufs)]
        self.i += 1
        return b


class Prefetch:
    def __init__(self, thunks, pf):
        self.thunks = thunks
        self.pf = pf
        self.issued = 0
        self.res = {}

    def get(self, k):
        while self.issued < min(len(self.thunks), k + self.pf + 1):
            self.res[self.issued] = self.thunks[self.issued]()
            self.issued += 1
        return self.res.pop(k)


def build(S, debug=False, phases=(0, 1, 2, 3, 4, 5, 6, 7)):
    nc = bass.Bass("TRN2", target_bir_lowering=False)
    dbg = "ExternalOutput" if debug else "Internal"
    NCH = S // 128
    TT = 512
    NTT = S // TT
    with ExitStack() as st:
        P = Prog(nc, st)
        x = P.dram("x", [S, D], F32, "ExternalInput")
        mixer_norm_w = P.dram("mixer_norm_w", [1, D], F32, "ExternalInput")
        w_in = P.dram("w_in", [D, IPW], F32, "ExternalInput")
        conv_w = P.dram("conv_w", [5, XBC], F32, "ExternalInput")
        conv_b = P.dram("conv_b", [1, XBC], F32, "ExternalInput")
        dt_bias = P.dram("dt_bias", [1, 64], F32, "ExternalInput")
        a_log = P.dram("a_log", [1, 64], F32, "ExternalInput")
        d_skip = P.dram("d_skip", [1, H], F32, "ExternalInput")
        ssd_norm_w = P.dram("ssd_norm_w", [1, D], F32, "ExternalInput")
        w_ssd = P.dram("w_ssd_branch", [D, D], F32, "ExternalInput")
        w_pg = P.dram("w_pool_group", [4, 256, 256], F32, "ExternalInput")
        pool_scale = P.dram("pool_scale", [1, PW], F32, "ExternalInput")
        w_pb = P.dram("w_pool_branch", [PW, D], F32, "ExternalInput")
        w_out = P.dram("w_out", [D, D], F32, "ExternalInput")
        ffn_norm_w = P.dram("ffn_norm_w", [1, D], F32, "ExternalInput")
        w_query = P.dram("w_query", [D, D], F32, "ExternalInput")
        sub_keys = P.dram("sub_keys", [16, 128, 128], F32, "ExternalInput")
        expert_u = P.dram("expert_u", [NE, D], F32, "ExternalInput")
        expert_v = P.dram("expert_v", [NE, D], F32, "ExternalInput")
        final_norm_w = P.dram("final_norm_w", [1, D], F32, "ExternalInput")
        out = P.dram("out", [S, D], F32, "ExternalOutput")

        wblk_d = [P.dram("winbf%d" % i, [128, 16, w], BF16) for i, (c0, w, k) in enumerate(BLOCKS)]
        xbc_raw = P.dram("xbc_raw", [XBC, S + 4], BF16, dbg)
        xp_raw = P.dram("xp_raw", [PW, S + 16], BF16, dbg)
        gT = P.dram("gT", [XBC, S], BF16, dbg)
        sz = P.dram("sz", [S, D], BF16, dbg)
        dts = P.dram("dts", [S, 64], F32, dbg)
        xbc_c = P.dram("xbc_c", [XBC, S], F32, dbg)
        yf = P.dram("yf", [S, D], F32, dbg)
        ynT = P.dram("ynT", [D, S], BF16, dbg)
        hsc = P.dram("hsc", [S, D], F32, dbg)
        wssd_d = [P.dram("wssdbf%d" % i, [128, 16, 512], BF16) for i in range(4)]
        wpb_d = [P.dram("wpbbf%d" % i, [128, 8, 512], BF16) for i in range(4)]
        wout_d = [P.dram("woutbf%d" % i, [128, 16, 512], BF16) for i in range(4)]
        wq_d = [P.dram("wqbf%d" % i, [128, 16, 512], BF16) for i in range(4)]
        wpg_d = P.dram("wpgbf", [128, 8, 256], BF16)
        pedge = P.dram("pedge", [1, 64], F32, "ExternalInput")
        UT_d = P.dram("UT_d", [128, 128, 16, 128], BF16)
        Vbf_d = P.dram("Vbf_d", [NE, D], BF16)
        hnT_d = P.dram("hnT_d", [NCH, 128, 16, 128], BF16)
        rt_d = P.dram("rt_d", [NCH, 128, 3, 128], F32)
        outs = [out]
        if debug:
            idx_dbg = P.dram("idx_dbg", [S, 128], I32, dbg)
            gate_dbg = P.dram("gate_dbg", [S, 128], F32, dbg)
            outs += [xbc_raw, xp_raw, gT, sz, dts, xbc_c, yf, ynT, hsc, idx_dbg, gate_dbg]

        if 0 in phases:
            for i, (c0, w, k) in enumerate(BLOCKS):
                src = w_in[:, c0:c0 + w].rearrange("(kc p) c -> p kc c", p=128)
                for j in range(4):
                    P.dma("pool", lambda e, i=i, j=j, src=src: e.dma_start(out=wblk_d[i][:, 4 * j:4 * j + 4, :], in_=src[:, 4 * j:4 * j + 4, :]),
                          reads=[w_in], writes=[wblk_d[i]], sembuf=wblk_d[i])

            for wsrc, wdst, kcn in ((w_ssd, wssd_d, 16), (w_pb, wpb_d, 8), (w_out, wout_d, 16), (w_query, wq_d, 16)):
                for b in range(4):
                    src = wsrc[:, b * 512:(b + 1) * 512].rearrange("(kc p) c -> p kc c", p=128)
                    for j in range(kcn // 4):
                        P.dma("pool", lambda e, b=b, j=j, src=src, wdst=wdst: e.dma_start(out=wdst[b][:, 4 * j:4 * j + 4, :], in_=src[:, 4 * j:4 * j + 4, :]),
                              reads=[wsrc], writes=[wdst[b]], sembuf=wdst[b])
            P.dma("pool", lambda e: e.dma_start(out=wpg_d[:], in_=w_pg[:].rearrange("g (kc p) d -> p (g kc) d", p=128)), reads=[w_pg], writes=[wpg_d], sembuf=wpg_d)


        if 7 in phases:
            with ExitStack() as ph:
                P.stack = ph
                for c in range(128):
                    P.dma("pool", lambda e, c=c: e.dma_start(out=Vbf_d[c * 128:(c + 1) * 128, :], in_=expert_v[c * 128:(c + 1) * 128, :]), reads=[expert_v], writes=[Vbf_d], sembuf=Vbf_d)
                identf = P.sb("identf", [128, 128], F32)
                identb = P.sb("identb", [128, 128], BF16)
                P.op("pool", lambda e: e.memset(identf[:], 0.0), writes=[identf])
                P.op("pool", lambda e: e.affine_select(out=identf[:], in_=identf[:], pattern=[[-1, 128]], compare_op=ALU.not_equal, fill=1.0, base=0, channel_multiplier=1), reads=[identf], writes=[identf])
                P.op("dve", lambda e: e.tensor_copy(out=identb[:], in_=identf[:]), reads=[identf], writes=[identb])
                ufr = Ring([P.sb("uf%d" % i, [128, D], F32) for i in range(3)])
                ubr = Ring([P.sb("ub%d" % i, [128, D], BF16) for i in range(2)])
                uto = Ring([P.sb("uto%d" % i, [128, 16, 128], BF16) for i in range(2)])
                tpu = Ring([P.ps("tpu%d" % i, [128, 8, 128], BF16) for i in range(4)])

                def mk_uld(c):
                    def f():
                        uf = ufr.next()
                        P.dma("sp", lambda e: e.dma_start(out=uf[:], in_=expert_u[c * 128:(c + 1) * 128, :]), reads=[expert_u], writes=[uf])
                        return uf
                    return f
                upf0 = Prefetch([mk_uld(c) for c in range(128)], 2)
                for c in range(128):
                    uf = upf0.get(c); ub = ubr.next(); uo = uto.next()
                    if c % 2 == 0:
                        P.op("act", lambda e, uf=uf, ub=ub: e.copy(out=ub[:], in_=uf[:]), reads=[uf], writes=[ub])
                    else:
                        P.op("dve", lambda e, uf=uf, ub=ub: e.tensor_copy(out=ub[:], in_=uf[:]), reads=[uf], writes=[ub])
                    for half in range(2):
                        tp = tpu.next()
                        for j in range(8):
                            kc = half * 8 + j
                            P.op("pe", lambda e, tp=tp, j=j, kc=kc, ub=ub: e.transpose(out=tp[:, j, :], in_=ub[:, kc * 128:(kc + 1) * 128], identity=identb[:]), reads=[ub, identb], writes=[tp])
                        if half == 0:
                            P.op("act", lambda e, tp=tp, uo=uo: e.copy(out=uo[:, 0:8, :], in_=tp[:]), reads=[tp], writes=[uo])
                        else:
                            P.op("dve", lambda e, tp=tp, uo=uo: e.tensor_copy(out=uo[:, 8:16, :], in_=tp[:]), reads=[tp], writes=[uo])
                    P.dma("sp", lambda e, uo=uo, c=c: e.dma_start(out=UT_d[c], in_=uo[:]), reads=[uo], writes=[UT_d])
                P.end_phase(keep=[Vbf_d])

        if 1 in phases:
            with ExitStack() as ph:
                P.stack = ph
                wn1 = P.sb("wn1", [128, D], F32)
                dtb = P.sb("dtb", [128, 64], F32)
                epsb = P.sb("epsb", [128, 1], F32)
                identb = P.sb("identb", [128, 128], BF16)
                identf = P.sb("identf", [128, 128], F32)
                zeros = P.sb("zeros", [128, 32, 8], BF16)
                xin = Ring([P.sb("xin%d" % i, [128, D], F32) for i in range(2)])
                junk = P.sb("junk", [128, D], BF16)
                ssq = Ring([P.sb("ssq%d" % i, [128, 1], F32) for i in range(2)])
                nb = Ring([P.sb("nb%d" % i, [128, D], BF16) for i in range(2)])
                nT = Ring([P.sb("nT%d" % i, [128, 16, TT], BF16) for i in range(2)])
                wbk = Ring([P.sb("wbk%d" % i, [128, 16, 512], BF16) for i in range(3)])
                stg = Ring([P.sb("stg%d" % i, [128, 512], BF16) for i in range(4)])
                dtt = Ring([P.sb("dtt%d" % i, [128, 64], F32) for i in range(2)])
                dto = Ring([P.sb("dto%d" % i, [128, 64], F32) for i in range(2)])
                tps = Ring([P.ps("tps%d" % i, [128, 8, 128], BF16) for i in range(2)])
                acc = Ring([P.ps("acc%d" % i, [128, 512], F32) for i in range(4)])

                P.dma("sp", lambda e: e.dma_start(out=wn1[:], in_=mixer_norm_w[0:1, :].to_broadcast([128, D])), reads=[mixer_norm_w], writes=[wn1])
                P.dma("sp", lambda e: e.dma_start(out=dtb[:], in_=dt_bias[0:1, :].to_broadcast([128, 64])), reads=[dt_bias], writes=[dtb])
                P.op("pool", lambda e: e.memset(epsb[:], EPS), writes=[epsb])
                P.op("pool", lambda e: e.memset(zeros[:], 0.0), writes=[zeros])
                P.op("pool", lambda e: e.memset(identf[:], 0.0), writes=[identf])
                P.op("pool", lambda e: e.affine_select(out=identf[:], in_=identf[:], pattern=[[-1, 128]], compare_op=ALU.not_equal, fill=1.0, base=0, channel_multiplier=1), reads=[identf], writes=[identf])
                P.op("dve", lambda e: e.tensor_copy(out=identb[:], in_=identf[:]), reads=[identf], writes=[identb])
                xr3 = xbc_raw[:].rearrange("(cc p) t -> p cc t", p=128)
                P.dma("sp", lambda e: e.dma_start(out=xr3[:, :, 0:2], in_=zeros[:, :, 0:2]), reads=[zeros], writes=[xbc_raw])
                P.dma("sp", lambda e: e.dma_start(out=xr3[:, :, S + 2:S + 4], in_=zeros[:, :, 0:2]), reads=[zeros], writes=[xbc_raw])
                xp3 = xp_raw[:].rearrange("(cc p) t -> p cc t", p=128)
                P.dma("sp", lambda e: e.dma_start(out=xp3[:, :, 0:8], in_=zeros[:, 0:8, :]), reads=[zeros], writes=[xp_raw])
                P.dma("sp", lambda e: e.dma_start(out=xp3[:, :, S + 8:S + 16], in_=zeros[:, 0:8, :]), reads=[zeros], writes=[xp_raw])

                def mk_wload(bi):
                    def f():
                        wb = wbk.next()
                        w = BLOCKS[bi][1]
                        P.dma("sp", lambda e: e.dma_start(out=wb[:, :, 0:w], in_=wblk_d[bi][:]), reads=[wblk_d[bi]], writes=[wb])
                        return wb
                    return f
                wpf = Prefetch([mk_wload(bi) for ti in range(NTT) for bi in range(len(BLOCKS))], 2)
                for ti in range(NTT):
                    nTt = nT.next()
                    for sub in range(4):
                        t0 = ti * TT + sub * 128
                        xi = xin.next(); sq = ssq.next(); nbb = nb.next()
                        P.dma("sp", lambda e, xi=xi, t0=t0: e.dma_start(out=xi[:], in_=x[t0:t0 + 128, :]), reads=[x], writes=[xi])
                        P.op("act", lambda e, xi=xi, sq=sq: e.activation(out=junk[:], in_=xi[:], func=AF.Square, accum_out=sq[:]), reads=[xi], writes=[junk, sq])
                        P.op("act", lambda e, sq=sq: e.activation(out=sq[:], in_=sq[:], func=AF.Sqrt, bias=epsb[:, 0:1], scale=1.0 / D), reads=[sq, epsb], writes=[sq])
                        P.op("dve", lambda e, sq=sq: e.reciprocal(out=sq[:], in_=sq[:]), reads=[sq], writes=[sq])
                        P.op("dve", lambda e, xi=xi, sq=sq, nbb=nbb: e.scalar_tensor_tensor(out=nbb[:], in0=xi[:], scalar=sq[:, 0:1], in1=wn1[:], op0=ALU.mult, op1=ALU.mult), reads=[xi, sq, wn1], writes=[nbb])
                        for half in range(2):
                            tp = tps.next()
                            for j in range(8):
                                kc = half * 8 + j
                                P.op("pe", lambda e, tp=tp, j=j, kc=kc, nbb=nbb: e.transpose(out=tp[:, j, :], in_=nbb[:, kc * 128:(kc + 1) * 128], identity=identb[:]), reads=[nbb, identb], writes=[tp])
                            if half == 0:
                                P.op("act", lambda e, tp=tp, nTt=nTt, sub=sub: e.copy(out=nTt[:, 0:8, sub * 128:(sub + 1) * 128], in_=tp[:]), reads=[tp], writes=[nTt])
                            else:
                                P.op("dve", lambda e, tp=tp, nTt=nTt, sub=sub: e.tensor_copy(out=nTt[:, 8:16, sub * 128:(sub + 1) * 128], in_=tp[:]), reads=[tp], writes=[nTt])
                    for bi, (c0, w, kind) in enumerate(BLOCKS):
                        wb = wpf.get(ti * len(BLOCKS) + bi)
                        if kind in ("xbc", "xp", "gate"):
                            for cc in range(4):
                                a = acc.next()
                                for kc in range(16):
                                    P.op("pe", lambda e, a=a, wb=wb, kc=kc, cc=cc, nTt=nTt: e.matmul(a[:], lhsT=wb[:, kc, cc * 128:(cc + 1) * 128], rhs=nTt[:, kc, :], start=(kc == 0), stop=(kc == 15)), reads=[wb, nTt], writes=[a])
                                sg = stg.next()
                                ch0 = c0 + cc * 128
                                if kind == "gate":
                                    P.op("act", lambda e, a=a, sg=sg: e.activation(out=sg[:], in_=a[:], func=AF.Sigmoid), reads=[a], writes=[sg])
                                    dst = gT[ch0 - O4:ch0 - O4 + 128, ti * TT:(ti + 1) * TT]; dbuf = gT
                                elif kind == "xbc":
                                    P.op("dve", lambda e, a=a, sg=sg: e.tensor_copy(out=sg[:], in_=a[:]), reads=[a], writes=[sg])
                                    dst = xbc_raw[ch0 - O1:ch0 - O1 + 128, 2 + ti * TT:2 + (ti + 1) * TT]; dbuf = xbc_raw
                                else:
                                    P.op("dve", lambda e, a=a, sg=sg: e.tensor_copy(out=sg[:], in_=a[:]), reads=[a], writes=[sg])
                                    dst = xp_raw[ch0 - O3:ch0 - O3 + 128, 8 + ti * TT:8 + (ti + 1) * TT]; dbuf = xp_raw
                                P.dma("sp", lambda e, dst=dst, sg=sg: e.dma_start(out=dst, in_=sg[:]), reads=[sg], writes=[dbuf])
                        elif kind == "z":
                            for sub in range(4):
                                a = acc.next()
                                for kc in range(16):
                                    P.op("pe", lambda e, a=a, wb=wb, kc=kc, sub=sub, nTt=nTt: e.matmul(a[:], lhsT=nTt[:, kc, sub * 128:(sub + 1) * 128], rhs=wb[:, kc, :], start=(kc == 0), stop=(kc == 15)), reads=[wb, nTt], writes=[a])
                                sg = stg.next()
                                P.op("act", lambda e, a=a, sg=sg: e.activation(out=sg[:], in_=a[:], func=AF.Silu), reads=[a], writes=[sg])
                                t0 = ti * TT + sub * 128
                                dst = sz[t0:t0 + 128, c0:c0 + 512]
                                P.dma("sp", lambda e, dst=dst, sg=sg: e.dma_start(out=dst, in_=sg[:]), reads=[sg], writes=[sz])
                        else:
                            for sub in range(4):
                                a = acc.next()
                                for kc in range(16):
                                    P.op("pe", lambda e, a=a, wb=wb, kc=kc, sub=sub, nTt=nTt: e.matmul(a[:, 0:64], lhsT=nTt[:, kc, sub * 128:(sub + 1) * 128], rhs=wb[:, kc, 0:64], start=(kc == 0), stop=(kc == 15)), reads=[wb, nTt], writes=[a])
                                d1 = dtt.next(); d2 = dto.next()
                                P.op("dve", lambda e, a=a, d1=d1: e.tensor_tensor(out=d1[:], in0=a[:, 0:64], in1=dtb[:], op=ALU.add), reads=[a, dtb], writes=[d1])
                                P.op("act", lambda e, d1=d1: e.activation(out=d1[:], in_=d1[:], func=AF.Exp), reads=[d1], writes=[d1])
                                P.op("act", lambda e, d1=d1, d2=d2: e.activation(out=d2[:], in_=d1[:], func=AF.Ln, bias=1.0), reads=[d1], writes=[d2])
                                t0 = ti * TT + sub * 128
                                P.dma("sp", lambda e, d2=d2, t0=t0: e.dma_start(out=dts[t0:t0 + 128, :], in_=d2[:]), reads=[d2], writes=[dts])
                P.end_phase()


        if 2 in phases:
            with ExitStack() as ph:
                P.stack = ph
                cw = P.sb("cw", [128, 5, 32], F32)
                cb = P.sb("cb", [128, 32], F32)
                for k in range(5):
                    P.dma("sp", lambda e, k=k: e.dma_start(out=cw[:, k, :], in_=conv_w[k, :].rearrange("(cc p) -> p cc", p=128), allow_slow_non_contiguous=True), reads=[conv_w], writes=[cw])
                P.dma("sp", lambda e: e.dma_start(out=cb[:], in_=conv_b[0, :].rearrange("(cc p) -> p cc", p=128), allow_slow_non_contiguous=True), reads=[conv_b], writes=[cb])
                raw = Ring([P.sb("raw%d" % i, [128, S + 4], BF16) for i in range(2)])
                cac = Ring([P.sb("cac%d" % i, [128, S], F32) for i in range(2)])
                cout = Ring([P.sb("cout%d" % i, [128, S], F32) for i in range(2)])
                def mk_rload(cc):
                    def f():
                        rw = raw.next()
                        P.dma("sp", lambda e: e.dma_start(out=rw[:], in_=xbc_raw[cc * 128:(cc + 1) * 128, :]), reads=[xbc_raw], writes=[rw])
                        return rw
                    return f
                rpf = Prefetch([mk_rload(cc) for cc in range(32)], 1)
                for cc in range(32):
                    rw = rpf.get(cc); ac = cac.next(); co = cout.next()
                    eng = "dve"
                    P.op(eng, lambda e, rw=rw, ac=ac, cc=cc: e.tensor_scalar(out=ac[:], in0=rw[:, 0:S], scalar1=cw[:, 0, cc:cc + 1], scalar2=None, op0=ALU.mult), reads=[rw, cw], writes=[ac])
                    for k in range(1, 5):
                        P.op(eng, lambda e, rw=rw, ac=ac, cc=cc, k=k: e.scalar_tensor_tensor(out=ac[:], in0=rw[:, k:k + S], scalar=cw[:, k, cc:cc + 1], in1=ac[:], op0=ALU.mult, op1=ALU.add), reads=[rw, cw, ac], writes=[ac])
                    P.op("act", lambda e, ac=ac, co=co, cc=cc: e.activation(out=co[:], in_=ac[:], func=AF.Silu, bias=cb[:, cc:cc + 1]), reads=[ac, cb], writes=[co])
                    P.dma("sp", lambda e, co=co, cc=cc: e.dma_start(out=xbc_c[cc * 128:(cc + 1) * 128, :], in_=co[:]), reads=[co], writes=[xbc_c])
                P.end_phase()

        if 3 in phases:
            with ExitStack() as ph:
                P.stack = ph
                NEG = -60000.0
                identf = P.sb("identf", [128, 128], F32)
                ones = P.sb("ones", [128, 128], F32)
                Tle = P.sb("Tle", [128, 128], F32); Tge = P.sb("Tge", [128, 128], F32)
                Tgt = P.sb("Tgt", [128, 128], F32); Tlt = P.sb("Tlt", [128, 128], F32)
                NEGf = P.sb("NEGf", [128, 4, 128], F32); NEGb = P.sb("NEGb", [128, 4, 128], F32)
                negA = P.sb("negA", [128, 64], F32)
                dsk = P.sb("dsk", [128, H], F32)
                snw = P.sb("snw", [128, D], F32)
                epsb = P.sb("epsb", [128, 1], F32)
                P.op("pool", lambda e: e.memset(epsb[:], EPS), writes=[epsb])
                P.op("pool", lambda e: e.memset(ones[:], 1.0), writes=[ones])
                P.op("pool", lambda e: e.memset(identf[:], 0.0), writes=[identf])
                P.op("pool", lambda e: e.affine_select(out=identf[:], in_=identf[:], pattern=[[-1, 128]], compare_op=ALU.not_equal, fill=1.0, base=0, channel_multiplier=1), reads=[identf], writes=[identf])
                for T_, pat, cm, cop in ((Tle, 1, -1, ALU.is_ge), (Tge, -1, 1, ALU.is_ge), (Tgt, -1, 1, ALU.is_gt), (Tlt, 1, -1, ALU.is_gt)):
                    P.op("pool", lambda e, T_=T_: e.memset(T_[:], 1.0), writes=[T_])
                    P.op("pool", lambda e, T_=T_, pat=pat, cm=cm, cop=cop: e.affine_select(out=T_[:], in_=T_[:], pattern=[[pat, 128]], compare_op=cop, fill=0.0, base=0, channel_multiplier=cm), reads=[T_], writes=[T_])
                for T_, pat, cm in ((NEGf, -1, 1), (NEGb, 1, -1)):
                    P.op("pool", lambda e, T_=T_: e.memset(T_[:], NEG), writes=[T_])
                    P.op("pool", lambda e, T_=T_, pat=pat, cm=cm: e.affine_select(out=T_[:], in_=T_[:], pattern=[[0, 4], [pat, 128]], compare_op=ALU.is_gt, fill=0.0, base=0, channel_multiplier=cm), reads=[T_], writes=[T_])
                P.dma("sp", lambda e: e.dma_start(out=negA[:], in_=a_log[0:1, :].to_broadcast([128, 64])), reads=[a_log], writes=[negA])
                P.op("act", lambda e: e.activation(out=negA[:], in_=negA[:], func=AF.Exp), reads=[negA], writes=[negA])
                P.op("dve", lambda e: e.tensor_scalar(out=negA[:], in0=negA[:], scalar1=-1.0, scalar2=None, op0=ALU.mult), reads=[negA], writes=[negA])
                P.dma("sp", lambda e: e.dma_start(out=dsk[:], in_=d_skip[0:1, :].to_broadcast([128, H])), reads=[d_skip], writes=[dsk])
                P.dma("sp", lambda e: e.dma_start(out=snw[:], in_=ssd_norm_w[0:1, :].to_broadcast([128, D])), reads=[ssd_norm_w], writes=[snw])

                xcr = Ring([P.sb("xc%d" % i, [128, 32, 128], F32) for i in range(2)])
                dtr = Ring([P.sb("dtc%d" % i, [128, 64], F32) for i in range(2)])
                adt = P.sb("adt", [128, 32], F32); nadt = P.sb("nadt", [128, 32], F32)
                esc = P.sb("esc", [128, 96], F32)
                xs_tm = P.sb("xs_tm", [128, D], F32)
                B_tm = P.sb("B_tm", [128, 1024], F32)
                Xm = P.sb("Xm", [128, D], F32); Xd = P.sb("Xd", [128, D], F32)
                hT = [P.sb("hT%d" % g, [128, 256], F32) for g in range(G)]
                yacc_t = ph.enter_context(nc.sbuf_tensor("yacc", [128, D], F32))
                yacc = [P.view(Buf("yacc", yacc_t, "sb"), yacc_t[:, g * 256:(g + 1) * 256]) for g in range(G)]
                cbt = Ring([P.sb("cbt%d" % i, [128, 128], F32) for i in range(2)])
                Eb = Ring([P.sb("Eb%d" % i, [128, 4, 128], F32) for i in range(2)])
                Mb = Ring([P.sb("Mb%d" % i, [128, 4, 128], F32) for i in range(2)])
                tmpb = Ring([P.sb("tmpb%d" % i, [128, 256], F32) for i in range(2)])
                yfr = Ring([P.sb("yfc%d" % i, [128, D], F32) for i in range(2)])
                szr = Ring([P.sb("szc%d" % i, [128, D], BF16) for i in range(2)])
                ysq = P.sb("ysq", [128, D], F32)
                gss = P.sb("gss", [128, G], F32)
                ynb = P.sb("ynb", [128, 16, 128], BF16)
                tpp = Ring([P.ps("tpp%d" % i, [128, 512], F32) for i in range(2)])
                smp = Ring([P.ps("smp%d" % i, [128, 128], F32) for i in range(2)])
                Dps = Ring([P.ps("Dps%d" % i, [128, 4, 128], F32) for i in range(2)])
                ydo = P.ps("ydo", [128, 512], F32)
                stp = P.ps("stp", [128, 256], F32)

                for dname, dof, Tin, Tout, NEGd in (("f", 0, Tle, Tgt, NEGf), ("b", 32, Tge, Tlt, NEGb)):
                    for g in range(G):
                        P.op("pool", lambda e, g=g: e.memset(hT[g][:], 0.0), writes=[hT[g]])
                    order = list(range(NCH)) if dname == "f" else list(range(NCH - 1, -1, -1))

                    def mk_cload(c, dname=dname):
                        def f():
                            t0 = c * 128
                            xc = xcr.next(); dtc = dtr.next()
                            P.dma("sp", lambda e: e.dma_start(out=xc[:], in_=xbc_c[:, t0:t0 + 128].rearrange("(cc p) t -> p cc t", p=128)), reads=[xbc_c], writes=[xc])
                            P.dma("sp", lambda e: e.dma_start(out=dtc[:], in_=dts[t0:t0 + 128, :]), reads=[dts], writes=[dtc])
                            yfc = szc = None
                            if dname == "b":
                                yfc = yfr.next(); szc = szr.next()
                                P.dma("sp", lambda e: e.dma_start(out=yfc[:], in_=yf[t0:t0 + 128, :]), reads=[yf], writes=[yfc])
                                P.dma("sp", lambda e: e.dma_start(out=szc[:], in_=sz[t0:t0 + 128, :]), reads=[sz], writes=[szc])
                            return xc, dtc, yfc, szc
                        return f
                    cpf = Prefetch([mk_cload(c) for c in order], 1)
                    for ci, c in enumerate(order):
                        t0 = c * 128
                        xc, dtc, yfc, szc = cpf.get(ci)
                        P.op("dve", lambda e, dtc=dtc, dof=dof: e.tensor_tensor(out=adt[:], in0=dtc[:, dof:dof + 32], in1=negA[:, dof:dof + 32], op=ALU.mult), reads=[dtc, negA], writes=[adt])
                        P.op("dve", lambda e: e.tensor_scalar(out=nadt[:], in0=adt[:], scalar1=-1.0, scalar2=None, op0=ALU.mult), reads=[adt], writes=[nadt])
                        sp_ = smp.next()
                        P.op("pe", lambda e, sp_=sp_, Tin=Tin: e.matmul(sp_[:, 0:32], lhsT=Tin[:], rhs=adt[:], start=True, stop=True), reads=[Tin, adt], writes=[sp_])
                        P.op("pe", lambda e, sp_=sp_, Tout=Tout: e.matmul(sp_[:, 32:64], lhsT=Tout[:], rhs=adt[:], start=True, stop=True), reads=[Tout, adt], writes=[sp_])
                        P.op("pe", lambda e, sp_=sp_: e.matmul(sp_[:, 64:96], lhsT=ones[:], rhs=adt[:], start=True, stop=True), reads=[ones, adt], writes=[sp_])
                        P.op("act", lambda e, sp_=sp_: e.activation(out=esc[:], in_=sp_[:, 0:96], func=AF.Exp), reads=[sp_], writes=[esc])
                        for q in range(6):
                            tp = tpp.next()
                            for j in range(4):
                                cc = q * 4 + j
                                P.op("pe", lambda e, tp=tp, j=j, cc=cc, xc=xc: e.transpose(out=tp[:, j * 128:(j + 1) * 128], in_=xc[:, cc, :], identity=identf[:]), reads=[xc, identf], writes=[tp])
                            if q < 4:
                                P.op("act", lambda e, tp=tp, q=q: e.copy(out=xs_tm[:, q * 512:(q + 1) * 512], in_=tp[:]), reads=[tp], writes=[xs_tm])
                            else:
                                P.op("act", lambda e, tp=tp, q=q: e.copy(out=B_tm[:, (q - 4) * 512:(q - 3) * 512], in_=tp[:]), reads=[tp], writes=[B_tm])
                        xs3 = xs_tm[:].rearrange("p (h d) -> p h d", d=PD)
                        P.op("dve", lambda e, dtc=dtc, dof=dof, xs3=xs3: e.tensor_tensor(out=Xm[:].rearrange("p (h d) -> p h d", d=PD), in0=xs3, in1=dtc[:, dof:dof + 32].unsqueeze(2).to_broadcast([128, H, PD]), op=ALU.mult), reads=[xs_tm, dtc], writes=[Xm])
                        P.op("pool", lambda e: e.tensor_tensor(out=Xd[:].rearrange("p (h d) -> p h d", d=PD), in0=Xm[:].rearrange("p (h d) -> p h d", d=PD), in1=esc[:, 32:64].unsqueeze(2).to_broadcast([128, H, PD]), op=ALU.mult), reads=[Xm, esc], writes=[Xd])
                        for g in range(G):
                            BT = xc[:, 16 + g, :]; CT = xc[:, 24 + g, :]
                            cp = smp.next(); cb_ = cbt.next(); dp = Dps.next(); Et = Eb.next(); Mt = Mb.next(); tb = tmpb.next()
                            P.op("pe", lambda e, cp=cp, BT=BT, CT=CT: e.matmul(cp[:], lhsT=BT, rhs=CT, start=True, stop=True), reads=[xc], writes=[cp])
                            P.op("act", lambda e, cp=cp, cb_=cb_: e.copy(out=cb_[:], in_=cp[:]), reads=[cp], writes=[cb_])
                            P.op("pe", lambda e, dp=dp, NEGd=NEGd: e.matmul(dp[:], lhsT=identf[:], rhs=NEGd[:], start=True, stop=False, skip_group_check=True), reads=[identf, NEGd], writes=[dp])
                            for r in range(R):
                                h = g * R + r
                                P.op("pe", lambda e, dp=dp, r=r, h=h, Tin=Tin: e.matmul(dp[:, r, :], lhsT=adt[:, h:h + 1].to_broadcast([128, 128]), rhs=Tin[:], start=False, stop=False, skip_group_check=True), reads=[adt, Tin], writes=[dp])
                                P.op("pe", lambda e, dp=dp, r=r, h=h, Tin=Tin: e.matmul(dp[:, r, :], lhsT=Tin[:], rhs=nadt[:, h:h + 1].to_broadcast([128, 128]), start=False, stop=(r == R - 1), skip_group_check=True), reads=[nadt, Tin], writes=[dp])
                            P.op("act", lambda e, dp=dp, Et=Et: e.activation(out=Et[:], in_=dp[:], func=AF.Exp), reads=[dp], writes=[Et])
                            P.op("dve", lambda e, Et=Et, Mt=Mt, cb_=cb_: e.tensor_tensor(out=Mt[:], in0=Et[:], in1=cb_[:].unsqueeze(1).to_broadcast([128, R, 128]), op=ALU.mult), reads=[Et, cb_], writes=[Mt])
                            for r in range(R):
                                h = g * R + r
                                P.op("pe", lambda e, Mt=Mt, r=r, h=h: e.matmul(ydo[:, r * 64:(r + 1) * 64], lhsT=Mt[:, r, :], rhs=Xm[:, h * 64:(h + 1) * 64], start=True, stop=True), reads=[Mt, Xm], writes=[ydo])
                            P.op("pe", lambda e, CT=CT, g=g: e.matmul(ydo[:, 256:512], lhsT=CT, rhs=hT[g][:], start=True, stop=True), reads=[xc, hT[g]], writes=[ydo])
                            P.op("dve", lambda e, tb=tb, g=g: e.tensor_tensor(out=tb[:].rearrange("p (h d) -> p h d", d=PD), in0=ydo[:, 256:512].rearrange("p (h d) -> p h d", d=PD), in1=esc[:, g * R:(g + 1) * R].unsqueeze(2).to_broadcast([128, R, PD]), op=ALU.mult), reads=[ydo, esc], writes=[tb])
                            P.op("dve", lambda e, tb=tb, g=g: e.tensor_tensor(out=yacc[g][:], in0=ydo[:, 0:256], in1=tb[:], op=ALU.add), reads=[ydo, tb], writes=[yacc[g]])
                            P.op("pe", lambda e, g=g: e.matmul(stp[:], lhsT=B_tm[:, g * 128:(g + 1) * 128], rhs=Xd[:, g * 256:(g + 1) * 256], start=True, stop=True), reads=[B_tm, Xd], writes=[stp])
                            P.op("pool", lambda e, g=g: e.tensor_tensor(out=hT[g][:].rearrange("p (h d) -> p h d", d=PD), in0=hT[g][:].rearrange("p (h d) -> p h d", d=PD), in1=esc[:, 64 + g * R:64 + (g + 1) * R].unsqueeze(2).to_broadcast([128, R, PD]), op=ALU.mult), reads=[hT[g], esc], writes=[hT[g]])
                            P.op("dve", lambda e, g=g: e.tensor_tensor(out=hT[g][:], in0=hT[g][:], in1=stp[:], op=ALU.add), reads=[hT[g], stp], writes=[hT[g]])
                        if dname == "f":
                            P.dma("sp", lambda e, t0=t0: e.dma_start(out=yf[t0:t0 + 128, :], in_=yacc_t[:]), reads=yacc, writes=[yf], sembuf=yacc[0])
                        else:
                            ya = yacc_t
                            P.op("dve", lambda e, yfc=yfc: e.tensor_tensor(out=ya[:], in0=ya[:], in1=yfc[:], op=ALU.add), reads=yacc + [yfc], writes=yacc)
                            P.op("pool", lambda e, xs3=xs3: e.tensor_tensor(out=ysq[:].rearrange("p (h d) -> p h d", d=PD), in0=xs3, in1=dsk[:].unsqueeze(2).to_broadcast([128, H, PD]), op=ALU.mult), reads=[xs_tm, dsk], writes=[ysq])
                            P.op("dve", lambda e: e.tensor_tensor(out=ya[:], in0=ya[:], in1=ysq[:], op=ALU.add), reads=yacc + [ysq], writes=yacc)
                            P.op("dve", lambda e, szc=szc: e.tensor_tensor(out=ya[:], in0=ya[:], in1=szc[:], op=ALU.mult), reads=yacc + [szc], writes=yacc)
                            P.op("pool", lambda e: e.tensor_tensor(out=ysq[:], in0=ya[:], in1=ya[:], op=ALU.mult), reads=yacc, writes=[ysq])
                            P.op("dve", lambda e: e.tensor_reduce(out=gss[:], in_=ysq[:].rearrange("p (g d) -> p g d", d=256), axis=AX.X, op=ALU.add), reads=[ysq], writes=[gss])
                            P.op("act", lambda e: e.activation(out=gss[:], in_=gss[:], func=AF.Sqrt, bias=epsb[:, 0:1], scale=1.0 / 256), reads=[gss, epsb], writes=[gss])
                            P.op("dve", lambda e: e.reciprocal(out=gss[:], in_=gss[:]), reads=[gss], writes=[gss])
                            P.op("dve", lambda e: e.tensor_tensor(out=ya[:].rearrange("p (g d) -> p g d", d=256), in0=ya[:].rearrange("p (g d) -> p g d", d=256), in1=gss[:].unsqueeze(2).to_broadcast([128, G, 256]), op=ALU.mult), reads=yacc + [gss], writes=yacc)
                            P.op("pool", lambda e: e.tensor_tensor(out=ya[:], in0=ya[:], in1=snw[:], op=ALU.mult), reads=yacc + [snw], writes=yacc)
                            for q in range(4):
                                tp = tpp.next()
                                for j in range(4):
                                    kc = q * 4 + j
                                    P.op("pe", lambda e, tp=tp, j=j, kc=kc: e.transpose(out=tp[:, j * 128:(j + 1) * 128], in_=ya[:, kc * 128:(kc + 1) * 128], identity=identf[:]), reads=yacc + [identf], writes=[tp])
                                P.op("act", lambda e, tp=tp, q=q: e.copy(out=ynb[:, q * 4:(q + 1) * 4, :], in_=tp[:].rearrange("p (a b) -> p a b", b=128)), reads=[tp], writes=[ynb])
                            P.dma("sp", lambda e, t0=t0: e.dma_start(out=ynT[:, t0:t0 + 128].rearrange("(kc p) t -> p kc t", p=128), in_=ynb[:]), reads=[ynb], writes=[ynT])
                P.end_phase()


        if 4 in phases:
            with ExitStack() as ph:
                P.stack = ph
                TA = 512
                NTA = S // TA
                wpg = P.sb("wpg", [128, 8, 256], BF16)
                psc = P.sb("psc", [128, 8], F32)
                ped = P.sb("ped", [128, 4, 16], F32)
                P.dma("sp", lambda e: e.dma_start(out=wpg[:], in_=wpg_d[:]), reads=[wpg_d], writes=[wpg])
                P.dma("sp", lambda e: e.dma_start(out=psc[:], in_=pool_scale[0, :].rearrange("(cc p) -> p cc", p=128), allow_slow_non_contiguous=True), reads=[pool_scale], writes=[psc])
                P.dma("sp", lambda e: e.dma_start(out=ped[:].rearrange("p a b -> p (a b)"), in_=pedge[0:1, :].to_broadcast([128, 64])), reads=[pedge], writes=[ped])
                ynt_r = Ring([P.sb("ynt%d" % i, [128, 16, TA], BF16) for i in range(2)])
                xpt_r = Ring([P.sb("xpt%d" % i, [128, 8, TA + 16], BF16) for i in range(2)])
                lv = [P.sb("lv%d" % i, [128, 2, TA + 16], F32) for i in range(2)]
                pooledT = P.sb("pooledT", [128, 8, TA], BF16)
                p2T = P.sb("p2T", [128, 8, TA], BF16)
                gtr = Ring([P.sb("gt%d" % i, [128, 8, TA], BF16) for i in range(2)])
                merged = P.sb("merged", [128, 16, TA], BF16)
                t1r = Ring([P.sb("t1_%d" % i, [128, TA], F32) for i in range(2)])
                t2r = Ring([P.sb("t2_%d" % i, [128, TA], F32) for i in range(2)])
                wbk = Ring([P.sb("wb3_%d" % i, [128, 16, 512], BF16) for i in range(3)])
                wpbk = Ring([P.sb("wpb%d" % i, [128, 8, 512], BF16) for i in range(2)])
                xsr = Ring([P.sb("xs3_%d" % i, [128, D], F32) for i in range(2)])
                acc = Ring([P.ps("acc3_%d" % i, [128, 512], F32) for i in range(6)])
                WINS = (2, 4, 8, 16)
                def mk_tload(ti):
                    def f():
                        t0 = ti * TA
                        ynt = ynt_r.next(); xpt = xpt_r.next()
                        P.dma("sp", lambda e: e.dma_start(out=ynt[:], in_=ynT[:, t0:t0 + TA].rearrange("(kc p) t -> p kc t", p=128)), reads=[ynT], writes=[ynt])
                        P.dma("sp", lambda e: e.dma_start(out=xpt[:], in_=xp_raw[:, t0:t0 + TA + 16].rearrange("(cc p) t -> p cc t", p=128)), reads=[xp_raw], writes=[xpt])
                        return ynt, xpt
                    return f

                def mk_mload(ti, b):
                    def f():
                        t0 = ti * TA
                        wb = wbk.next(); wp = wpbk.next(); gt = gtr.next()
                        P.dma("sp", lambda e: e.dma_start(out=wb[:], in_=wssd_d[b][:]), reads=[wssd_d[b]], writes=[wb])
                        P.dma("sp", lambda e: e.dma_start(out=wp[:], in_=wpb_d[b][:]), reads=[wpb_d[b]], writes=[wp])
                        P.dma("sp", lambda e: e.dma_start(out=gt[:, 0:4, :], in_=gT[b * 512:(b + 1) * 512, t0:t0 + TA].rearrange("(cc p) t -> p cc t", p=128)), reads=[gT], writes=[gt])
                        P.dma("sp", lambda e: e.dma_start(out=gt[:, 4:8, :], in_=gT[D + b * 512:D + (b + 1) * 512, t0:t0 + TA].rearrange("(cc p) t -> p cc t", p=128)), reads=[gT], writes=[gt])
                        return wb, wp, gt
                    return f

                def mk_oload(b):
                    def f():
                        wb = wbk.next()
                        P.dma("sp", lambda e: e.dma_start(out=wb[:], in_=wout_d[b][:]), reads=[wout_d[b]], writes=[wb])
                        return wb
                    return f
                tpf = Prefetch([mk_tload(ti) for ti in range(NTA)], 1)
                wl = []
                for ti in range(NTA):
                    wl += [mk_mload(ti, b) for b in range(4)]
                    for pair in range(TA // 256):
                        wl += [mk_oload(b) for b in range(4)]
                wpf3 = Prefetch(wl, 1)
                wk3 = 0
                for ti in range(NTA):
                    t0 = ti * TA
                    ynt, xpt = tpf.get(ti)
                    for gi, w in enumerate(WINS):
                        src = xpt[:, 2 * gi:2 * gi + 2, :]
                        xpt_ = xpt
                        L = TA + 16
                        step = 1
                        cur = None
                        li = 0
                        while step < w:
                            dst = lv[li % 2]
                            a_in = src if cur is None else cur[:, :, :]
                            rd = [xpt] if cur is None else [cur]
                            P.op("pool", lambda e, dst=dst, a_in=a_in, L=L, step=step: e.tensor_tensor(out=dst[:, :, 0:L - step], in0=a_in[:, :, 0:L - step], in1=a_in[:, :, step:L], op=ALU.add), reads=rd, writes=[dst])
                            cur = dst; L -= step; step *= 2; li += 1
                        off = 8 - w // 2
                        if ti == 0:
                            P.op("pool", lambda e, cur=cur, off=off, gi=gi: e.tensor_tensor(out=cur[:, :, off:off + 8], in0=cur[:, :, off:off + 8], in1=ped[:, gi, 0:8].unsqueeze(1).to_broadcast([128, 2, 8]), op=ALU.mult), reads=[cur, ped], writes=[cur])
                        if ti == NTA - 1:
                            P.op("pool", lambda e, cur=cur, off=off, gi=gi: e.tensor_tensor(out=cur[:, :, off + TA - 8:off + TA], in0=cur[:, :, off + TA - 8:off + TA], in1=ped[:, gi, 8:16].unsqueeze(1).to_broadcast([128, 2, 8]), op=ALU.mult), reads=[cur, ped], writes=[cur])
                        P.op("dve", lambda e, cur=cur, off=off, gi=gi, w=w, xpt_=xpt_: e.scalar_tensor_tensor(out=pooledT[:, 2 * gi:2 * gi + 2, :], in0=cur[:, :, off:off + TA], scalar=1.0 / w, in1=xpt_[:, 2 * gi:2 * gi + 2, 8:8 + TA], op0=ALU.mult, op1=ALU.subtract), reads=[cur, xpt_], writes=[pooledT])
                    for gi in range(4):
                        for dc in range(2):
                            a = acc.next()
                            for kc in range(2):
                                P.op("pe", lambda e, a=a, gi=gi, dc=dc, kc=kc: e.matmul(a[:, 0:TA], lhsT=wpg[:, gi * 2 + kc, dc * 128:(dc + 1) * 128], rhs=pooledT[:, gi * 2 + kc, :], start=(kc == 0), stop=(kc == 1)), reads=[wpg, pooledT], writes=[a])
                            P.op("act", lambda e, a=a, gi=gi, dc=dc: e.activation(out=p2T[:, gi * 2 + dc, :], in_=a[:, 0:TA], func=AF.Copy, scale=psc[:, gi * 2 + dc:gi * 2 + dc + 1]), reads=[a, psc], writes=[p2T])
                    for b in range(4):
                        wb, wp, gt = wpf3.get(wk3); wk3 += 1
                        for cc in range(4):
                            dch = b * 4 + cc
                            a1 = acc.next(); a2 = acc.next()
                            for kc in range(16):
                                P.op("pe", lambda e, a1=a1, wb=wb, kc=kc, cc=cc, ynt=ynt: e.matmul(a1[:, 0:TA], lhsT=wb[:, kc, cc * 128:(cc + 1) * 128], rhs=ynt[:, kc, :], start=(kc == 0), stop=(kc == 15)), reads=[wb, ynt], writes=[a1])
                            for kc in range(8):
                                P.op("pe", lambda e, a2=a2, wp=wp, kc=kc, cc=cc: e.matmul(a2[:, 0:TA], lhsT=wp[:, kc, cc * 128:(cc + 1) * 128], rhs=p2T[:, kc, :], start=(kc == 0), stop=(kc == 7)), reads=[wp, p2T], writes=[a2])
                            t1 = t1r.next(); t2 = t2r.next()
                            P.op("dve", lambda e, a1=a1, t1=t1, gt=gt, cc=cc: e.tensor_tensor(out=t1[:], in0=a1[:, 0:TA], in1=gt[:, cc, :], op=ALU.mult), reads=[a1, gt], writes=[t1])
                            P.op("dve", lambda e, a2=a2, t2=t2, gt=gt, cc=cc: e.tensor_tensor(out=t2[:], in0=a2[:, 0:TA], in1=gt[:, 4 + cc, :], op=ALU.mult), reads=[a2, gt], writes=[t2])
                            P.op("pool", lambda e, t1=t1, t2=t2, dch=dch: e.tensor_tensor(out=merged[:, dch, :], in0=t1[:], in1=t2[:], op=ALU.add), reads=[t1, t2], writes=[merged])
                    for pair in range(TA // 256):
                        subs = (2 * pair, 2 * pair + 1)
                        xt = {}
                        for sub in subs:
                            xt[sub] = xsr.next()
                            P.dma("sp", lambda e, xs_=xt[sub], sub=sub, t0=t0: e.dma_start(out=xs_[:], in_=x[t0 + sub * 128:t0 + (sub + 1) * 128, :]), reads=[x], writes=[xt[sub]])
                        for b in range(4):
                            wb = wpf3.get(wk3); wk3 += 1
                            for sub in subs:
                                a = acc.next()
                                for kc in range(16):
                                    P.op("pe", lambda e, a=a, wb=wb, kc=kc, sub=sub: e.matmul(a[:], lhsT=merged[:, kc, sub * 128:(sub + 1) * 128], rhs=wb[:, kc, :], start=(kc == 0), stop=(kc == 15)), reads=[wb, merged], writes=[a])
                                P.op("dve", lambda e, a=a, xs_=xt[sub], b=b: e.tensor_tensor(out=xs_[:, b * 512:(b + 1) * 512], in0=a[:], in1=xs_[:, b * 512:(b + 1) * 512], op=ALU.add), reads=[a, xt[sub]], writes=[xt[sub]])
                        for sub in subs:
                            P.dma("sp", lambda e, xs_=xt[sub], sub=sub, t0=t0: e.dma_start(out=hsc[t0 + sub * 128:t0 + (sub + 1) * 128, :], in_=xs_[:]), reads=[xt[sub]], writes=[hsc])
                P.end_phase()


        if 5 in phases:
            with ExitStack() as ph:
                P.stack = ph
                wq = P.sb("wq", [128, 4, 16, 512], BF16)
                qtm = P.sb("qtm", [128, D], F32)
                keysN = Buf("keysN", qtm.t[:].rearrange("p (h d) -> p h d", d=128), "sb")
                keysT = P.sb("keysT", [128, 16, 128], F32)
                fw = P.sb("fw", [128, D], F32)
                epsb = P.sb("epsb", [128, 1], F32)
                identf = P.sb("identf", [128, 128], F32)
                identb = P.sb("identb", [128, 128], BF16)
                iot_i = P.sb("iot_i", [128, 16], I32)
                iot = P.sb("iot", [128, 16], F32)
                for b in range(4):
                    P.dma("sp", lambda e, b=b: e.dma_start(out=wq[:, b, :, :], in_=wq_d[b][:]), reads=[wq_d[b]], writes=[wq])
                P.dma("sp", lambda e: e.dma_start(out=keysN[:], in_=sub_keys[:].rearrange("h n d -> n h d")), reads=[sub_keys], writes=[qtm])
                P.dma("sp", lambda e: e.dma_start(out=fw[:], in_=ffn_norm_w[0:1, :].to_broadcast([128, D])), reads=[ffn_norm_w], writes=[fw])
                P.op("pool", lambda e: e.memset(epsb[:], EPS), writes=[epsb])
                P.op("pool", lambda e: e.memset(identf[:], 0.0), writes=[identf])
                P.op("pool", lambda e: e.affine_select(out=identf[:], in_=identf[:], pattern=[[-1, 128]], compare_op=ALU.not_equal, fill=1.0, base=0, channel_multiplier=1), reads=[identf], writes=[identf])
                P.op("dve", lambda e: e.tensor_copy(out=identb[:], in_=identf[:]), reads=[identf], writes=[identb])
                P.op("pool", lambda e: e.iota(iot_i[:], pattern=[[1, 16]], base=0, channel_multiplier=0), writes=[iot_i])
                P.op("dve", lambda e: e.tensor_copy(out=iot[:], in_=iot_i[:]), reads=[iot_i], writes=[iot])
                tpq = Ring([P.ps("tpq%d" % i, [128, 4, 128], F32) for i in range(2)])
                tpb = Ring([P.ps("tpb%d" % i, [128, 8, 128], BF16) for i in range(2)])
                accq = Ring([P.ps("accq%d" % i, [128, 512], F32) for i in range(2)])
                for q4 in range(4):
                    tp = tpq.next()
                    for j in range(4):
                        hh = q4 * 4 + j
                        P.op("pe", lambda e, tp=tp, j=j, hh=hh: e.transpose(out=tp[:, j, :], in_=keysN[:, hh, :], identity=identf[:]), reads=[qtm, identf], writes=[tp])
                    P.op("act", lambda e, tp=tp, q4=q4: e.copy(out=keysT[:, q4 * 4:(q4 + 1) * 4, :], in_=tp[:]), reads=[tp], writes=[keysT])

                hr = Ring([P.sb("h4_%d" % i, [128, D], F32) for i in range(2)])
                hn = P.sb("hn", [128, D], F32)
                hnb = P.sb("hnb", [128, D], BF16)
                junkf = P.sb("junkf", [128, D], F32)
                ssq = P.sb("ssq4", [128, 1], F32)
                hnT = P.sb("hnT", [128, 16, 128], BF16)
                qT = P.sb("qT", [128, 16, 128], F32)
                sc = P.sb("sc", [128, 16, 128], F32)
                wk = P.sb("wk", [128, 128], F32)
                v1 = P.sb("v1", [128, 16, 16], F32)
                i1 = P.sb("i1", [128, 16, 16], U32)
                i1f = P.sb("i1f", [128, 16, 16], F32)
                cand = P.sb("cand", [128, 8, 256], F32)
                cwk = P.sb("cwk", [128, 256], F32)
                tv = P.sb("tv", [128, 8, 16], F32)
                pos = P.sb("pos", [128, 8, 16], U32)
                posf = P.sb("posf", [128, 8, 16], F32)
                r1f = P.sb("r1f", [128, 8, 16], F32)
                r1u = P.sb("r1u", [128, 8, 16], U32)
                r2u = P.sb("r2u", [128, 8, 16], U32)
                r2f = P.sb("r2f", [128, 8, 16], F32)
                eqb = Buf("eqb", cand.t[:].rearrange("p h (a b) -> p h a b", b=16), "sb")
                sel1 = P.sb("sel1", [128, 8, 16], F32)
                sel2 = P.sb("sel2", [128, 8, 16], F32)
                sel1r = P.sb("sel1r", [128, 8, 16], F32)
                rtr = Ring([P.sb("rt%d" % i, [128, 3, 128], F32) for i in range(2)])
                idx = P.sb("idx", [128, 128], I32)
                gexp = P.sb("gexp", [128, 8, 16], F32)
                gsum = P.sb("gsum", [128, 8], F32)
                gate = P.sb("gate", [128, 8, 16], F32)

                for c in range(NCH):
                    t0 = c * 128
                    ht = hr.next()
                    P.dma("sp", lambda e, ht=ht, t0=t0: e.dma_start(out=ht[:], in_=hsc[t0:t0 + 128, :]), reads=[hsc], writes=[ht])
                    P.op("act", lambda e, ht=ht: e.activation(out=junkf[:], in_=ht[:], func=AF.Square, accum_out=ssq[:]), reads=[ht], writes=[junkf, ssq])
                    P.op("act", lambda e: e.activation(out=ssq[:], in_=ssq[:], func=AF.Sqrt, bias=epsb[:, 0:1], scale=1.0 / D), reads=[ssq, epsb], writes=[ssq])
                    P.op("dve", lambda e: e.reciprocal(out=ssq[:], in_=ssq[:]), reads=[ssq], writes=[ssq])
                    P.op("dve", lambda e, ht=ht: e.scalar_tensor_tensor(out=hn[:], in0=ht[:], scalar=ssq[:, 0:1], in1=fw[:], op0=ALU.mult, op1=ALU.mult), reads=[ht, ssq, fw], writes=[hn])
                    P.op("act", lambda e: e.copy(out=hnb[:], in_=hn[:]), reads=[hn], writes=[hnb])
                    for half in range(2):
                        tp = tpb.next()
                        for j in range(8):
                            kc = half * 8 + j
                            P.op("pe", lambda e, tp=tp, j=j, kc=kc: e.transpose(out=tp[:, j, :], in_=hnb[:, kc * 128:(kc + 1) * 128], identity=identb[:]), reads=[hnb, identb], writes=[tp])
                        P.op("act", lambda e, tp=tp, half=half: e.copy(out=hnT[:, half * 8:(half + 1) * 8, :], in_=tp[:]), reads=[tp], writes=[hnT])
                    for b in range(4):
                        a = accq.next()
                        for kc in range(16):
                            P.op("pe", lambda e, a=a, b=b, kc=kc: e.matmul(a[:], lhsT=hnT[:, kc, :], rhs=wq[:, b, kc, :], start=(kc == 0), stop=(kc == 15)), reads=[hnT, wq], writes=[a])
                        P.op("act", lambda e, a=a, b=b: e.copy(out=qtm[:, b * 512:(b + 1) * 512], in_=a[:]), reads=[a], writes=[qtm])
                    for q4 in range(4):
                        tp = tpq.next()
                        for j in range(4):
                            hh = q4 * 4 + j
                            P.op("pe", lambda e, tp=tp, j=j, hh=hh: e.transpose(out=tp[:, j, :], in_=qtm[:, hh * 128:(hh + 1) * 128], identity=identf[:]), reads=[qtm, identf], writes=[tp])
                        P.op("act", lambda e, tp=tp, q4=q4: e.copy(out=qT[:, q4 * 4:(q4 + 1) * 4, :], in_=tp[:]), reads=[tp], writes=[qT])
                    for q4 in range(4):
                        tp = tpq.next()
                        for j in range(4):
                            hh = q4 * 4 + j
                            P.op("pe", lambda e, tp=tp, j=j, hh=hh: e.matmul(tp[:, j, :], lhsT=qT[:, hh, :], rhs=keysT[:, hh, :], start=True, stop=True), reads=[qT, keysT], writes=[tp])
                        P.op("act", lambda e, tp=tp, q4=q4: e.copy(out=sc[:, q4 * 4:(q4 + 1) * 4, :], in_=tp[:]), reads=[tp], writes=[sc])
                    for hh in range(16):
                        P.op("dve", lambda e, hh=hh: e.max(out=v1[:, hh, 0:8], in_=sc[:, hh, :]), reads=[sc], writes=[v1])
                        P.op("dve", lambda e, hh=hh: e.max_index(out=i1[:, hh, 0:8], in_max=v1[:, hh, 0:8], in_values=sc[:, hh, :]), reads=[sc, v1], writes=[i1])
                        P.op("dve", lambda e, hh=hh: e.match_replace(out=wk[:], in_to_replace=v1[:, hh, 0:8], in_values=sc[:, hh, :], imm_value=-1e30), reads=[sc, v1], writes=[wk])
                        P.op("dve", lambda e, hh=hh: e.max(out=v1[:, hh, 8:16], in_=wk[:]), reads=[wk], writes=[v1])
                        P.op("dve", lambda e, hh=hh: e.max_index(out=i1[:, hh, 8:16], in_max=v1[:, hh, 8:16], in_values=wk[:]), reads=[wk, v1], writes=[i1])
                    v4 = v1[:].rearrange("p (h two) r -> p h two r", two=2)
                    P.op("dve", lambda e, v4=v4: e.tensor_tensor(out=cand[:].rearrange("p h (a b) -> p h a b", b=16), in0=v4[:, :, 0, :].unsqueeze(3).to_broadcast([128, 8, 16, 16]), in1=v4[:, :, 1, :].unsqueeze(2).to_broadcast([128, 8, 16, 16]), op=ALU.add), reads=[v1], writes=[cand])
                    for h in range(8):
                        P.op("dve", lambda e, h=h: e.max(out=tv[:, h, 0:8], in_=cand[:, h, :]), reads=[cand], writes=[tv])
                        P.op("dve", lambda e, h=h: e.max_index(out=pos[:, h, 0:8], in_max=tv[:, h, 0:8], in_values=cand[:, h, :]), reads=[cand, tv], writes=[pos])
                        P.op("dve", lambda e, h=h: e.match_replace(out=cwk[:], in_to_replace=tv[:, h, 0:8], in_values=cand[:, h, :], imm_value=-1e30), reads=[cand, tv], writes=[cwk])
                        P.op("dve", lambda e, h=h: e.max(out=tv[:, h, 8:16], in_=cwk[:]), reads=[cwk], writes=[tv])
                        P.op("dve", lambda e, h=h: e.max_index(out=pos[:, h, 8:16], in_max=tv[:, h, 8:16], in_values=cwk[:]), reads=[cwk, tv], writes=[pos])
                    P.op("dve", lambda e: e.tensor_tensor(out=gexp[:], in0=tv[:], in1=tv[:, :, 0:1].to_broadcast([128, 8, 16]), op=ALU.subtract), reads=[tv], writes=[gexp])
                    P.op("act", lambda e: e.activation(out=gexp[:], in_=gexp[:], func=AF.Exp), reads=[gexp], writes=[gexp])
                    P.op("dve", lambda e: e.tensor_reduce(out=gsum[:], in_=gexp[:], axis=AX.X, op=ALU.add), reads=[gexp], writes=[gsum])
                    P.op("dve", lambda e: e.reciprocal(out=gsum[:], in_=gsum[:]), reads=[gsum], writes=[gsum])
                    P.op("dve", lambda e: e.tensor_tensor(out=gate[:], in0=gexp[:], in1=gsum[:].unsqueeze(2).to_broadcast([128, 8, 16]), op=ALU.mult), reads=[gexp, gsum], writes=[gate])
                    P.op("dve", lambda e: e.tensor_copy(out=posf[:], in_=pos[:]), reads=[pos], writes=[posf])
                    P.op("dve", lambda e: e.tensor_copy(out=i1f[:], in_=i1[:]), reads=[i1], writes=[i1f])
                    P.op("dve", lambda e: e.tensor_single_scalar(out=r1u[:], in_=pos[:], scalar=4, op=ALU.logical_shift_right), reads=[pos], writes=[r1u])
                    P.op("dve", lambda e: e.tensor_single_scalar(out=r2u[:], in_=pos[:], scalar=15, op=ALU.bitwise_and), reads=[pos], writes=[r2u])
                    P.op("dve", lambda e: e.tensor_copy(out=r1f[:], in_=r1u[:]), reads=[r1u], writes=[r1f])
                    P.op("dve", lambda e: e.tensor_copy(out=r2f[:], in_=r2u[:]), reads=[r2u], writes=[r2f])
                    i4 = i1f[:].rearrange("p (h two) r -> p h two r", two=2)
                    iot4 = iot[:].unsqueeze(1).unsqueeze(1).to_broadcast([128, 8, 16, 16])
                    for rf, two, sel in ((r1f, 0, sel1), (r2f, 1, sel2)):
                        P.op("dve", lambda e, rf=rf: e.tensor_tensor(out=eqb[:], in0=rf[:].unsqueeze(3).to_broadcast([128, 8, 16, 16]), in1=iot4, op=ALU.is_equal), reads=[rf, iot], writes=[cand])
                        P.op("dve", lambda e, two=two, i4=i4: e.tensor_tensor(out=eqb[:], in0=eqb[:], in1=i4[:, :, two, :].unsqueeze(2).to_broadcast([128, 8, 16, 16]), op=ALU.mult), reads=[cand, i1f], writes=[cand])
                        P.op("dve", lambda e, sel=sel: e.tensor_reduce(out=sel[:], in_=eqb[:], axis=AX.X, op=ALU.add), reads=[cand], writes=[sel])
                    P.op("dve", lambda e: e.tensor_copy(out=sel1r[:], in_=sel1[:]), reads=[sel1], writes=[sel1r])
                    if debug:
                        P.op("dve", lambda e: e.scalar_tensor_tensor(out=sel1[:], in0=sel1[:], scalar=128.0, in1=sel2[:], op0=ALU.mult, op1=ALU.add), reads=[sel1, sel2], writes=[sel1])
                        P.op("dve", lambda e: e.tensor_copy(out=idx[:], in_=sel1[:].rearrange("p h k -> p (h k)")), reads=[sel1], writes=[idx])
                    tp = tpq.next()
                    P.op("pe", lambda e, tp=tp: e.transpose(out=tp[:, 0, :], in_=sel1r[:].rearrange("p h k -> p (h k)"), identity=identf[:]), reads=[sel1r, identf], writes=[tp])
                    P.op("pe", lambda e, tp=tp: e.transpose(out=tp[:, 1, :], in_=sel2[:].rearrange("p h k -> p (h k)"), identity=identf[:]), reads=[sel2, identf], writes=[tp])
                    P.op("pe", lambda e, tp=tp: e.transpose(out=tp[:, 2, :], in_=gate[:].rearrange("p h k -> p (h k)"), identity=identf[:]), reads=[gate, identf], writes=[tp])
                    rt = rtr.next()
                    P.op("act", lambda e, tp=tp, rt=rt: e.copy(out=rt[:], in_=tp[:, 0:3, :]), reads=[tp], writes=[rt])
                    P.dma("sp", lambda e, rt=rt, c=c: e.dma_start(out=rt_d[c], in_=rt[:]), reads=[rt], writes=[rt_d])
                    P.dma("sp", lambda e, c=c: e.dma_start(out=hnT_d[c], in_=hnT[:]), reads=[hnT], writes=[hnT_d])
                    if debug:
                        P.dma("sp", lambda e, t0=t0: e.dma_start(out=idx_dbg[t0:t0 + 128, :], in_=idx[:]), reads=[idx], writes=[idx_dbg])
                        P.dma("sp", lambda e, t0=t0: e.dma_start(out=gate_dbg[t0:t0 + 128, :], in_=gate[:].rearrange("p h k -> p (h k)")), reads=[gate], writes=[gate_dbg])
                P.end_phase()


        if 6 in phases:
            with ExitStack() as ph:
                P.stack = ph
                TP = 256
                NTP = S // TP
                NSUB = TP // 128
                GC = 16
                NG = 128 // GC
                fnw = P.sb("fnw", [128, D], F32)
                epsb = P.sb("epsb", [128, 1], F32)
                iot_i = P.sb("iotr_i", [128, 128], I32)
                iotr = P.sb("iotr", [128, 128], F32)
                P.dma("sp", lambda e: e.dma_start(out=fnw[:], in_=final_norm_w[0:1, :].to_broadcast([128, D])), reads=[final_norm_w], writes=[fnw])
                P.op("pool", lambda e: e.memset(epsb[:], EPS), writes=[epsb])
                P.op("pool", lambda e: e.iota(iot_i[:], pattern=[[1, 128]], base=0, channel_multiplier=0), writes=[iot_i])
                P.op("dve", lambda e: e.tensor_copy(out=iotr[:], in_=iot_i[:]), reads=[iot_i], writes=[iotr])
                GT = P.sb("GT", [128, 128, TP], BF16)
                hnr = Ring([P.sb("hnT4_%d" % i, [128, 16, TP], BF16) for i in range(2)])
                rtr4 = Ring([P.sb("rt4_%d" % i, [128, NSUB, 3, 128], F32) for i in range(2)])
                oh1r = Ring([P.sb("oh1_%d" % i, [128, 128], BF16) for i in range(4)])
                oh2r = Ring([P.sb("oh2_%d" % i, [128, 128], BF16) for i in range(4)])
                utr = Ring([P.sb("ut%d" % i, [128, 16, 128], BF16) for i in range(4)])
                actr = Ring([P.sb("gel%d" % i, [128, TP], F32) for i in range(2)])
                wtr = Ring([P.sb("wt%d" % i, [128, GC, TP], BF16) for i in range(2)])
                vtr = Ring([P.sb("vt%d" % i, [128, GC, 512], BF16) for i in range(2)])
                pacc = [P.sb("pacc%d" % i, [128, D], F32) for i in range(NSUB)]
                h4r = Ring([P.sb("h4f%d" % i, [128, D], F32) for i in range(2)])
                junkb = P.sb("junkb", [128, D], BF16)
                ssq = P.sb("ssq6", [128, 1], F32)
                gps = Ring([P.ps("gps%d" % i, [128, 4, 128], F32) for i in range(2)])
                aps = Ring([P.ps("aps%d" % i, [128, TP], F32) for i in range(2)])
                ops = Ring([P.ps("ops%d" % i, [128, 512], F32) for i in range(3)])

                def mk_tl(ti):
                    def f():
                        hn_ = hnr.next(); rt_ = rtr4.next()
                        for sub in range(NSUB):
                            c = ti * NSUB + sub
                            P.dma("sp", lambda e, sub=sub, c=c: e.dma_start(out=hn_[:, :, sub * 128:(sub + 1) * 128], in_=hnT_d[c]), reads=[hnT_d], writes=[hn_])
                            P.dma("sp", lambda e, sub=sub, c=c: e.dma_start(out=rt_[:, sub, :, :], in_=rt_d[c]), reads=[rt_d], writes=[rt_])
                        return hn_, rt_
                    return f

                def mk_ul(c):
                    def f():
                        ut = utr.next()
                        P.dma("sp", lambda e: e.dma_start(out=ut[:], in_=UT_d[c]), reads=[UT_d], writes=[ut])
                        return ut
                    return f

                def mk_vl(g, blk):
                    def f():
                        vt = vtr.next()
                        P.dma("act", lambda e: e.dma_start(out=vt[:], in_=Vbf_d[g * GC * 128:(g + 1) * GC * 128, blk * 512:(blk + 1) * 512].rearrange("(ci p) d -> p ci d", p=128)), reads=[Vbf_d], writes=[vt])
                        return vt
                    return f
                tpf4 = Prefetch([mk_tl(ti) for ti in range(NTP)], 1)
                upf = Prefetch([mk_ul(c) for ti in range(NTP) for c in range(128)], 3)
                vpf = Prefetch([mk_vl(g, blk) for ti in range(NTP) for g in range(NG) for blk in range(4)], 1)
                uk = 0; vk = 0
                for ti in range(NTP):
                    hn_, rt_ = tpf4.get(ti)
                    for tq in range(TP // 4):
                        gp = gps.next()
                        for u in range(4):
                            tl = tq * 4 + u
                            sub, tt = tl // 128, tl % 128
                            o1 = oh1r.next(); o2 = oh2r.next()
                            P.op("pool", lambda e, o1=o1, sub=sub, tt=tt, rt_=rt_: e.tensor_scalar(out=o1[:], in0=iotr[:], scalar1=rt_[:, sub, 0, tt:tt + 1], scalar2=None, op0=ALU.is_equal), reads=[iotr, rt_], writes=[o1])
                            P.op("dve", lambda e, o2=o2, sub=sub, tt=tt, rt_=rt_: e.tensor_scalar(out=o2[:], in0=iotr[:], scalar1=rt_[:, sub, 1, tt:tt + 1], scalar2=rt_[:, sub, 2, tt:tt + 1], op0=ALU.is_equal, op1=ALU.mult), reads=[iotr, rt_], writes=[o2])
                            P.op("pe", lambda e, gp=gp, u=u, o1=o1, o2=o2: e.matmul(gp[:, u, :], lhsT=o2[:], rhs=o1[:], start=True, stop=True), reads=[o1, o2], writes=[gp])
                        P.op("act", lambda e, gp=gp, tq=tq: e.copy(out=GT[:, :, tq * 4:(tq + 1) * 4], in_=gp[:].rearrange("p t c -> p c t")), reads=[gp], writes=[GT])
                    for g in range(NG):
                        wt = wtr.next()
                        for ci in range(GC):
                            c = g * GC + ci
                            ut = upf.get(uk); uk += 1
                            ap_ = aps.next(); ab = actr.next()
                            for kc in range(16):
                                P.op("pe", lambda e, ap_=ap_, ut=ut, kc=kc, hn_=hn_: e.matmul(ap_[:], lhsT=ut[:, kc, :], rhs=hn_[:, kc, :], start=(kc == 0), stop=(kc == 15)), reads=[ut, hn_], writes=[ap_])
                            P.op("act", lambda e, ap_=ap_, ab=ab: e.activation(out=ab[:], in_=ap_[:], func=AF.Gelu), reads=[ap_], writes=[ab])
                            P.op("dve", lambda e, ab=ab, wt=wt, ci=ci, c=c: e.tensor_tensor(out=wt[:, ci, :], in0=ab[:], in1=GT[:, c, :], op=ALU.mult), reads=[ab, GT], writes=[wt])
                        for blk in range(4):
                            vt = vpf.get(vk); vk += 1
                            for sub in range(NSUB):
                                op_ = ops.next()
                                for ci in range(GC):
                                    P.op("pe", lambda e, op_=op_, wt=wt, vt=vt, ci=ci, sub=sub: e.matmul(op_[:], lhsT=wt[:, ci, sub * 128:(sub + 1) * 128], rhs=vt[:, ci, :], start=(ci == 0), stop=(ci == GC - 1)), reads=[wt, vt], writes=[op_])
                                if g == 0:
                                    P.op("dve", lambda e, op_=op_, sub=sub, blk=blk: e.tensor_copy(out=pacc[sub][:, blk * 512:(blk + 1) * 512], in_=op_[:]), reads=[op_], writes=[pacc[sub]])
                                else:
                                    P.op("dve", lambda e, op_=op_, sub=sub, blk=blk: e.tensor_tensor(out=pacc[sub][:, blk * 512:(blk + 1) * 512], in0=pacc[sub][:, blk * 512:(blk + 1) * 512], in1=op_[:], op=ALU.add), reads=[op_, pacc[sub]], writes=[pacc[sub]])
                    for sub in range(NSUB):
                        t0 = ti * TP + sub * 128
                        ht = h4r.next()
                        P.dma("sp", lambda e, ht=ht, t0=t0: e.dma_start(out=ht[:], in_=hsc[t0:t0 + 128, :]), reads=[hsc], writes=[ht])
                        P.op("pool", lambda e, ht=ht, sub=sub: e.tensor_tensor(out=ht[:], in0=ht[:], in1=pacc[sub][:], op=ALU.add), reads=[ht, pacc[sub]], writes=[ht])
                        P.op("act", lambda e, ht=ht: e.activation(out=junkb[:], in_=ht[:], func=AF.Square, accum_out=ssq[:]), reads=[ht], writes=[junkb, ssq])
                        P.op("act", lambda e: e.activation(out=ssq[:], in_=ssq[:], func=AF.Sqrt, bias=epsb[:, 0:1], scale=1.0 / D), reads=[ssq, epsb], writes=[ssq])
                        P.op("dve", lambda e: e.reciprocal(out=ssq[:], in_=ssq[:]), reads=[ssq], writes=[ssq])
                        P.op("dve", lambda e, ht=ht: e.scalar_tensor_tensor(out=ht[:], in0=ht[:], scalar=ssq[:, 0:1], in1=fnw[:], op0=ALU.mult, op1=ALU.mult), reads=[ht, ssq, fnw], writes=[ht])
                        P.dma("sp", lambda e, ht=ht, t0=t0: e.dma_start(out=out[t0:t0 + 128, :], in_=ht[:]), reads=[ht], writes=[out])
                P.end_phase()

        P.wait_all("sp", outs)
        P.emit()
    return nc


def _pedge_const(S):
    pe = np.ones((4, 16), np.float32)
    for gi, w in enumerate((2, 4, 8, 16)):
        for j in range(8):
            for t, col in ((j, j), (S - 8 + j, 8 + j)):
                lo = max(t - w // 2, 0)
                hi = min(t + w // 2, S)
                pe[gi, col] = w / float(hi - lo)
    return pe.reshape(1, 64)


_NC_CACHE = {}


def kernel(**inputs):
    x = np.ascontiguousarray(np.asarray(inputs["x"], dtype=np.float32))
    B, S, _ = x.shape
    f = lambda k: np.ascontiguousarray(np.asarray(inputs[k], dtype=np.float32))
    shared = dict(
        mixer_norm_w=f("mixer_norm_w").reshape(1, D),
        w_in=f("w_in").reshape(D, IPW),
        conv_w=f("conv_w").reshape(5, XBC),
        conv_b=f("conv_b").reshape(1, XBC),
        dt_bias=f("dt_bias").reshape(1, 64),
        a_log=f("a_log").reshape(1, 64),
        d_skip=f("d_skip").reshape(1, H),
        ssd_norm_w=f("ssd_norm_w").reshape(1, D),
        w_ssd_branch=f("w_ssd_branch").reshape(D, D),
        w_pool_group=f("w_pool_group").reshape(4, 256, 256),
        pool_scale=f("pool_scale").reshape(1, PW),
        w_pool_branch=f("w_pool_branch").reshape(PW, D),
        w_out=f("w_out").reshape(D, D),
        ffn_norm_w=f("ffn_norm_w").reshape(1, D),
        w_query=f("w_query").reshape(D, D),
        sub_keys=f("sub_keys").reshape(16, 128, 128),
        expert_u=f("expert_u").reshape(NE, D),
        expert_v=f("expert_v").reshape(NE, D),
        final_norm_w=f("final_norm_w").reshape(1, D),
        pedge=_pedge_const(S),
    )
    if S not in _NC_CACHE:
        _NC_CACHE[S] = build(S)
    nc = _NC_CACHE[S]
    in_maps = [dict(shared, x=x[b]) for b in range(B)]
    res = run_bass_kernel_spmd(nc, in_maps, core_ids=list(range(B)))
    return np.stack([np.asarray(r["out"], dtype=np.float32) for r in res.results], axis=0)
```

```python
import numpy as np
import concourse.bass as bass
import concourse.mybir as mybir
from contextlib import ExitStack
from concourse.bass_utils import run_bass_kernel_spmd

F32 = mybir.dt.float32
BF16 = mybir.dt.bfloat16
I32 = mybir.dt.int32
U32 = mybir.dt.uint32
AF = mybir.ActivationFunctionType
ALU = mybir.AluOpType
AX = mybir.AxisListType


POOL_SYNC = True


class Buf:
    def __init__(self, name, t, space):
        self.name = name
        self.t = t
        self.space = space
        self.w = {}
        self.r = {}
        self.sem = None
        self.semcount = 0

    def __getitem__(self, idx):
        return self.t[idx]


class Prog:
    ENG = ("pe", "act", "dve", "pool", "sp")

    def __init__(self, nc, stack):
        self.nc = nc
        self.stack = stack
        self.streams = {e: [] for e in self.ENG}
        self.esem = {e: stack.enter_context(nc.semaphore("es_" + e)) for e in self.ENG}
        self.ecount = {e: 0 for e in self.ENG}
        self.signal = {e: set() for e in self.ENG}
        self.sigtotal = {e: 0 for e in self.ENG}
        self.sigval = {e: {} for e in self.ENG}
        self.waited = {e: {} for e in self.ENG}
        self.semobj = {}
        for e in self.ENG:
            self.semobj[("e", e)] = self.esem[e]
        self.nsem = 0
        self.nbuf = 0
        self.outer = stack
        self.allbufs = []
        self.sempool = []
        self.phase_bufs = []
        self._semcounts = {}

    def sb(self, name, shape, dtype):
        self.nbuf += 1
        name = "%s_%d" % (name, self.nbuf)
        t = self.stack.enter_context(self.nc.sbuf_tensor(name, list(shape), dtype))
        return self._reg(Buf(name, t, "sb"))

    def ps(self, name, shape, dtype):
        self.nbuf += 1
        name = "%s_%d" % (name, self.nbuf)
        t = self.stack.enter_context(self.nc.psum_tensor(name, list(shape), dtype))
        return self._reg(Buf(name, t, "ps"))

    def dram(self, name, shape, dtype, kind="Internal"):
        t = self.nc.dram_tensor(name, list(shape), dtype, kind=kind)
        return self._reg(Buf(name, t.ap(), "dram"))

    def view(self, buf, t):
        b = Buf(buf.name + "_v%d" % self.nbuf, t, buf.space)
        self.nbuf += 1
        return self._reg(b)

    def _reg(self, b):
        self.allbufs.append(b)
        return b

    def _bufsem(self, b):
        if b.sem is None:
            if self.sempool:
                b.sem, b.semcount = self.sempool.pop()
            else:
                b.sem = self.outer.enter_context(self.nc.semaphore("bs%d" % self.nsem))
                self.nsem += 1
            self.semobj[("b", id(b))] = b.sem
            self._semcounts[("b", id(b))] = (lambda b=b: b.semcount)
            b.key = ("b", id(b))
            self.phase_bufs.append(b)
        return b.key

    def end_phase(self, keep=()):
        self.barrier(skip=[b.key for b in keep if b.sem is not None])
        self.emit()
        kept = []
        for b in self.phase_bufs:
            if b in keep:
                kept.append(b)
                continue
            self.sempool.append((b.sem, b.semcount))
            del self.semobj[b.key]
            del self._semcounts[b.key]
            for e in self.ENG:
                self.waited[e].pop(b.key, None)
            b.sem = None
        self.phase_bufs = kept
        for b in self.allbufs:
            if b in keep:
                continue
            b.w = {}
            b.r = {}

    def _collect(self, eng, reads, writes, own_key, is_dma):
        need = {}

        def add(k, v):
            if v > need.get(k, 0):
                need[k] = v

        for b in reads:
            for k, v in b.w.items():
                if (not is_dma) and k == own_key and eng == "pe":
                    continue
                add(k, v)
        for b in writes:
            if b.space == "dram":
                continue
            for k, v in b.w.items():
                if k == own_key and not (POOL_SYNC and eng == "pool" and not is_dma):
                    continue
                add(k, v)
            for k, v in b.r.items():
                if (not is_dma) and k == own_key and not (POOL_SYNC and eng == "pool"):
                    continue
                add(k, v)
        wl = []
        cache = self.waited[eng]
        for k, v in need.items():
            if cache.get(k, 0) >= v:
                continue
            cache[k] = v
            wl.append((k, v))
            if k[0] == "e":
                self.signal[k[1]].add(v)
        return wl

    def op(self, eng, fn, reads=(), writes=()):
        key = ("e", eng)
        waits = self._collect(eng, reads, writes, key, False)
        self.ecount[eng] += 1
        val = self.ecount[eng]
        self.streams[eng].append((waits, fn, key, 1, val))
        for b in reads:
            b.r[key] = val
        for b in writes:
            b.w = {key: val}
            b.r = {}

    def dma(self, q, fn, reads=(), writes=(), sembuf=None):
        if sembuf is None:
            cands = [b for b in list(writes) + list(reads) if b.space == "sb"]
            sembuf = cands[0] if cands else (list(writes) + list(reads))[0]
        key = self._bufsem(sembuf)
        waits = self._collect(q, reads, writes, key, True)
        sembuf.semcount += 16
        val = sembuf.semcount
        self.streams[q].append((waits, fn, key, 16, None))
        for b in reads:
            b.r[key] = val
        for b in writes:
            if b.space == "dram":
                b.w = dict(b.w)
                b.w[key] = val
            else:
                b.w = {key: val}
                b.r = {}

    def wait_all(self, eng, bufs):
        need = {}
        for b in bufs:
            for k, v in list(b.w.items()) + list(b.r.items()):
                if v > need.get(k, 0):
                    need[k] = v
        for k, v in need.items():
            if k[0] == "e":
                self.signal[k[1]].add(v)
        self.streams[eng].append((list(need.items()), None, None, 0, None))

    def barrier(self, skip=()):
        need = [(("e", e), self.ecount[e]) for e in self.ENG if self.ecount[e] > 0]
        for k, sem in self.semobj.items():
            if k[0] == "b" and k not in skip:
                need.append((k, self._semcounts[k]()))
        for e in self.ENG:
            wl = []
            for k, v in need:
                if k == ("e", e) or v == 0:
                    continue
                if self.waited[e].get(k, 0) >= v:
                    continue
                self.waited[e][k] = v
                wl.append((k, v))
                if k[0] == "e":
                    self.signal[k[1]].add(v)
            self.streams[e].append((wl, None, None, 0, None))

    def emit(self):
        nc = self.nc
        handles = {"pe": "tensor", "act": "scalar", "dve": "vector", "pool": "gpsimd", "sp": "sync"}
        for e in self.ENG:
            run = self.sigtotal[e]
            for waits, fn, key, inc, idx in self.streams[e]:
                if idx is not None and idx in self.signal[e]:
                    run += 1
                    self.sigval[e][idx] = run
            self.sigtotal[e] = run
        with nc.Block() as block:
            for e in self.ENG:
                stream = self.streams[e]

                def body(eng, stream=stream, e=e):
                    for waits, fn, key, inc, idx in stream:
                        for k, v in waits:
                            if k[0] == "e":
                                eng.wait_ge(self.semobj[k], self.sigval[k[1]][v])
                            else:
                                eng.wait_ge(self.semobj[k], v)
                        if fn is not None:
                            ins = fn(eng)
                            if idx is None:
                                ins.then_inc(self.semobj[key], inc)
                            elif idx in self.signal[e]:
                                ins.then_inc(self.semobj[key], 1)

                getattr(block, handles[e])(body)
        self.streams = {e: [] for e in self.ENG}
        for e in self.ENG:
            self.signal[e] = set()
            self.sigval[e] = {}


D = 2048; H = 32; G = 8; R = 4; PD = 64; NS = 128; XBC = 4096; PW = 1024; IPW = 11328
O1 = 2048; O2 = 6144; O3 = 6208; O4 = 7232
NE = 16384; TOPK = 16
EPS = 1e-6

BLOCKS = ([(c, 512, "z") for c in range(0, 2048, 512)] + [(c, 512, "xbc") for c in range(O1, O2, 512)]
          + [(O2, 64, "dt")] + [(c, 512, "xp") for c in range(O3, O4, 512)]
          + [(c, 512, "gate") for c in range(O4, IPW, 512)])


class Ring:
    def __init__(self, bufs):
        self.bufs = bufs
        self.i = 0

    def next(self):
        b = self.bufs[self.i % len(self.bufs)]
        self.i += 1
        return b


class Prefetch:
    def __init__(self, thunks, pf):
        self.thunks = thunks
        self.pf = pf
        self.issued = 0
        self.res = {}

    def get(self, k):
        while self.issued < min(len(self.thunks), k + self.pf + 1):
            self.res[self.issued] = self.thunks[self.issued]()
            self.issued += 1
        return self.res.pop(k)


def build(S, debug=False, phases=(0, 1, 2, 3, 4, 5, 6, 7)):
    nc = bass.Bass("TRN2", target_bir_lowering=False)
    dbg = "ExternalOutput" if debug else "Internal"
    NCH = S // 128
    TT = 512
    NTT = S // TT
    with ExitStack() as st:
        P = Prog(nc, st)
        x = P.dram("x", [S, D], F32, "ExternalInput")
        mixer_norm_w = P.dram("mixer_norm_w", [1, D], F32, "ExternalInput")
        w_in = P.dram("w_in", [D, IPW], F32, "ExternalInput")
        conv_w = P.dram("conv_w", [5, XBC], F32, "ExternalInput")
        conv_b = P.dram("conv_b", [1, XBC], F32, "ExternalInput")
        dt_bias = P.dram("dt_bias", [1, 64], F32, "ExternalInput")
        a_log = P.dram("a_log", [1, 64], F32, "ExternalInput")
        d_skip = P.dram("d_skip", [1, H], F32, "ExternalInput")
        ssd_norm_w = P.dram("ssd_norm_w", [1, D], F32, "ExternalInput")
        w_ssd = P.dram("w_ssd_branch", [D, D], F32, "ExternalInput")
        w_pg = P.dram("w_pool_group", [4, 256, 256], F32, "ExternalInput")
        pool_scale = P.dram("pool_scale", [1, PW], F32, "ExternalInput")
        w_pb = P.dram("w_pool_branch", [PW, D], F32, "ExternalInput")
        w_out = P.dram("w_out", [D, D], F32, "ExternalInput")
        ffn_norm_w = P.dram("ffn_norm_w", [1, D], F32, "ExternalInput")
        w_query = P.dram("w_query", [D, D], F32, "ExternalInput")
        sub_keys = P.dram("sub_keys", [16, 128, 128], F32, "ExternalInput")
        expert_u = P.dram("expert_u", [NE, D], F32, "ExternalInput")
        expert_v = P.dram("expert_v", [NE, D], F32, "ExternalInput")
        final_norm_w = P.dram("final_norm_w", [1, D], F32, "ExternalInput")
        out = P.dram("out", [S, D], F32, "ExternalOutput")

        wblk_d = [P.dram("winbf%d" % i, [128, 16, w], BF16) for i, (c0, w, k) in enumerate(BLOCKS)]
        xbc_raw = P.dram("xbc_raw", [XBC, S + 4], BF16, dbg)
        xp_raw = P.dram("xp_raw", [PW, S + 16], BF16, dbg)
        gT = P.dram("gT", [XBC, S], BF16, dbg)
        sz = P.dram("sz", [S, D], BF16, dbg)
        dts = P.dram("dts", [S, 64], F32, dbg)
        xbc_c = P.dram("xbc_c", [XBC, S], F32, dbg)
        yf = P.dram("yf", [S, D], F32, dbg)
        ynT = P.dram("ynT", [D, S], BF16, dbg)
        hsc = P.dram("hsc", [S, D], F32, dbg)
        wssd_d = [P.dram("wssdbf%d" % i, [128, 16, 512], BF16) for i in range(4)]
        wpb_d = [P.dram("wpbbf%d" % i, [128, 8, 512], BF16) for i in range(4)]
        wout_d = [P.dram("woutbf%d" % i, [128, 16, 512], BF16) for i in range(4)]
        wq_d = [P.dram("wqbf%d" % i, [128, 16, 512], BF16) for i in range(4)]
        wpg_d = P.dram("wpgbf", [128, 8, 256], BF16)
        pedge = P.dram("pedge", [1, 64], F32, "ExternalInput")
        UT_d = P.dram("UT_d", [128, 128, 16, 128], BF16)
        Vbf_d = P.dram("Vbf_d", [NE, D], BF16)
        hnT_d = P.dram("hnT_d", [NCH, 128, 16, 128], BF16)
        rt_d = P.dram("rt_d", [NCH, 128, 3, 128], F32)
        outs = [out]
        if debug:
            idx_dbg = P.dram("idx_dbg", [S, 128], I32, dbg)
            gate_dbg = P.dram("gate_dbg", [S, 128], F32, dbg)
            outs += [xbc_raw, xp_raw, gT, sz, dts, xbc_c, yf, ynT, hsc, idx_dbg, gate_dbg]

        if 0 in phases:
            for i, (c0, w, k) in enumerate(BLOCKS):
                src = w_in[:, c0:c0 + w].rearrange("(kc p) c -> p kc c", p=128)
                for j in range(4):
                    P.dma("pool", lambda e, i=i, j=j, src=src: e.dma_start(out=wblk_d[i][:, 4 * j:4 * j + 4, :], in_=src[:, 4 * j:4 * j + 4, :]),
                          reads=[w_in], writes=[wblk_d[i]], sembuf=wblk_d[i])

            for wsrc, wdst, kcn in ((w_ssd, wssd_d, 16), (w_pb, wpb_d, 8), (w_out, wout_d, 16), (w_query, wq_d, 16)):
                for b in range(4):
                    src = wsrc[:, b * 512:(b + 1) * 512].rearrange("(kc p) c -> p kc c", p=128)
                    for j in range(kcn // 4):
                        P.dma("pool", lambda e, b=b, j=j, src=src, wdst=wdst: e.dma_start(out=wdst[b][:, 4 * j:4 * j + 4, :], in_=src[:, 4 * j:4 * j + 4, :]),
                              reads=[wsrc], writes=[wdst[b]], sembuf=wdst[b])
            P.dma("pool", lambda e: e.dma_start(out=wpg_d[:], in_=w_pg[:].rearrange("g (kc p) d -> p (g kc) d", p=128)), reads=[w_pg], writes=[wpg_d], sembuf=wpg_d)


        if 7 in phases:
            with ExitStack() as ph:
                P.stack = ph
                for c in range(128):
                    P.dma("pool", lambda e, c=c: e.dma_start(out=Vbf_d[c * 128:(c + 1) * 128, :], in_=expert_v[c * 128:(c + 1) * 128, :]), reads=[expert_v], writes=[Vbf_d], sembuf=Vbf_d)
                identf = P.sb("identf", [128, 128], F32)
                identb = P.sb("identb", [128, 128], BF16)
                P.op("pool", lambda e: e.memset(identf[:], 0.0), writes=[identf])
                P.op("pool", lambda e: e.affine_select(out=identf[:], in_=identf[:], pattern=[[-1, 128]], compare_op=ALU.not_equal, fill=1.0, base=0, channel_multiplier=1), reads=[identf], writes=[identf])
                P.op("dve", lambda e: e.tensor_copy(out=identb[:], in_=identf[:]), reads=[identf], writes=[identb])
                ufr = Ring([P.sb("uf%d" % i, [128, D], F32) for i in range(3)])
                ubr = Ring([P.sb("ub%d" % i, [128, D], BF16) for i in range(2)])
                uto = Ring([P.sb("uto%d" % i, [128, 16, 128], BF16) for i in range(2)])
                tpu = Ring([P.ps("tpu%d" % i, [128, 8, 128], BF16) for i in range(4)])

                def mk_uld(c):
                    def f():
                        uf = ufr.next()
                        P.dma("sp", lambda e: e.dma_start(out=uf[:], in_=expert_u[c * 128:(c + 1) * 128, :]), reads=[expert_u], writes=[uf])
                        return uf
                    return f
                upf0 = Prefetch([mk_uld(c) for c in range(128)], 2)
                for c in range(128):
                    uf = upf0.get(c); ub = ubr.next(); uo = uto.next()
                    if c % 2 == 0:
                        P.op("act", lambda e, uf=uf, ub=ub: e.copy(out=ub[:], in_=uf[:]), reads=[uf], writes=[ub])
                    else:
                        P.op("dve", lambda e, uf=uf, ub=ub: e.tensor_copy(out=ub[:], in_=uf[:]), reads=[uf], writes=[ub])
                    for half in range(2):
                        tp = tpu.next()
                        for j in range(8):
                            kc = half * 8 + j
                            P.op("pe", lambda e, tp=tp, j=j, kc=kc, ub=ub: e.transpose(out=tp[:, j, :], in_=ub[:, kc * 128:(kc + 1) * 128], identity=identb[:]), reads=[ub, identb], writes=[tp])
                        if half == 0:
                            P.op("act", lambda e, tp=tp, uo=uo: e.copy(out=uo[:, 0:8, :], in_=tp[:]), reads=[tp], writes=[uo])
                        else:
                            P.op("dve", lambda e, tp=tp, uo=uo: e.tensor_copy(out=uo[:, 8:16, :], in_=tp[:]), reads=[tp], writes=[uo])
                    P.dma("sp", lambda e, uo=uo, c=c: e.dma_start(out=UT_d[c], in_=uo[:]), reads=[uo], writes=[UT_d])
                P.end_phase(keep=[Vbf_d])

        if 1 in phases:
            with ExitStack() as ph:
                P.stack = ph
                wn1 = P.sb("wn1", [128, D], F32)
                dtb = P.sb("dtb", [128, 64], F32)
                epsb = P.sb("epsb", [128, 1], F32)
                identb = P.sb("identb", [128, 128], BF16)
                identf = P.sb("identf", [128, 128], F32)
                zeros = P.sb("zeros", [128, 32, 8], BF16)
                xin = Ring([P.sb("xin%d" % i, [128, D], F32) for i in range(2)])
                junk = P.sb("junk", [128, D], BF16)
                ssq = Ring([P.sb("ssq%d" % i, [128, 1], F32) for i in range(2)])
                nb = Ring([P.sb("nb%d" % i, [128, D], BF16) for i in range(2)])
                nT = Ring([P.sb("nT%d" % i, [128, 16, TT], BF16) for i in range(2)])
                wbk = Ring([P.sb("wbk%d" % i, [128, 16, 512], BF16) for i in range(3)])
                stg = Ring([P.sb("stg%d" % i, [128, 512], BF16) for i in range(4)])
                dtt = Ring([P.sb("dtt%d" % i, [128, 64], F32) for i in range(2)])
                dto = Ring([P.sb("dto%d" % i, [128, 64], F32) for i in range(2)])
                tps = Ring([P.ps("tps%d" % i, [128, 8, 128], BF16) for i in range(2)])
                acc = Ring([P.ps("acc%d" % i, [128, 512], F32) for i in range(4)])

                P.dma("sp", lambda e: e.dma_start(out=wn1[:], in_=mixer_norm_w[0:1, :].to_broadcast([128, D])), reads=[mixer_norm_w], writes=[wn1])
                P.dma("sp", lambda e: e.dma_start(out=dtb[:], in_=dt_bias[0:1, :].to_broadcast([128, 64])), reads=[dt_bias], writes=[dtb])
                P.op("pool", lambda e: e.memset(epsb[:], EPS), writes=[epsb])
                P.op("pool", lambda e: e.memset(zeros[:], 0.0), writes=[zeros])
                P.op("pool", lambda e: e.memset(identf[:], 0.0), writes=[identf])
                P.op("pool", lambda e: e.affine_select(out=identf[:], in_=identf[:], pattern=[[-1, 128]], compare_op=ALU.not_equal, fill=1.0, base=0, channel_multiplier=1), reads=[identf], writes=[identf])
                P.op("dve", lambda e: e.tensor_copy(out=identb[:], in_=identf[:]), reads=[identf], writes=[identb])
                xr3 = xbc_raw[:].rearrange("(cc p) t -> p cc t", p=128)
                P.dma("sp", lambda e: e.dma_start(out=xr3[:, :, 0:2], in_=zeros[:, :, 0:2]), reads=[zeros], writes=[xbc_raw])
                P.dma("sp", lambda e: e.dma_start(out=xr3[:, :, S + 2:S + 4], in_=zeros[:, :, 0:2]), reads=[zeros], writes=[xbc_raw])
                xp3 = xp_raw[:].rearrange("(cc p) t -> p cc t", p=128)
                P.dma("sp", lambda e: e.dma_start(out=xp3[:, :, 0:8], in_=zeros[:, 0:8, :]), reads=[zeros], writes=[xp_raw])
                P.dma("sp", lambda e: e.dma_start(out=xp3[:, :, S + 8:S + 16], in_=zeros[:, 0:8, :]), reads=[zeros], writes=[xp_raw])

                def mk_wload(bi):
                    def f():
                        wb = wbk.next()
                        w = BLOCKS[bi][1]
                        P.dma("sp", lambda e: e.dma_start(out=wb[:, :, 0:w], in_=wblk_d[bi][:]), reads=[wblk_d[bi]], writes=[wb])
                        return wb
                    return f
                wpf = Prefetch([mk_wload(bi) for ti in range(NTT) for bi in range(len(BLOCKS))], 2)
                for ti in range(NTT):
                    nTt = nT.next()
                    for sub in range(4):
                        t0 = ti * TT + sub * 128
                        xi = xin.next(); sq = ssq.next(); nbb = nb.next()
                        P.dma("sp", lambda e, xi=xi, t0=t0: e.dma_start(out=xi[:], in_=x[t0:t0 + 128, :]), reads=[x], writes=[xi])
                        P.op("act", lambda e, xi=xi, sq=sq: e.activation(out=junk[:], in_=xi[:], func=AF.Square, accum_out=sq[:]), reads=[xi], writes=[junk, sq])
                        P.op("act", lambda e, sq=sq: e.activation(out=sq[:], in_=sq[:], func=AF.Sqrt, bias=epsb[:, 0:1], scale=1.0 / D), reads=[sq, epsb], writes=[sq])
                        P.op("dve", lambda e, sq=sq: e.reciprocal(out=sq[:], in_=sq[:]), reads=[sq], writes=[sq])
                        P.op("dve", lambda e, xi=xi, sq=sq, nbb=nbb: e.scalar_tensor_tensor(out=nbb[:], in0=xi[:], scalar=sq[:, 0:1], in1=wn1[:], op0=ALU.mult, op1=ALU.mult), reads=[xi, sq, wn1], writes=[nbb])
                        for half in range(2):
                            tp = tps.next()
                            for j in range(8):
                                kc = half * 8 + j
                                P.op("pe", lambda e, tp=tp, j=j, kc=kc, nbb=nbb: e.transpose(out=tp[:, j, :], in_=nbb[:, kc * 128:(kc + 1) * 128], identity=identb[:]), reads=[nbb, identb], writes=[tp])
                            if half == 0:
                                P.op("act", lambda e, tp=tp, nTt=nTt, sub=sub: e.copy(out=nTt[:, 0:8, sub * 128:(sub + 1) * 128], in_=tp[:]), reads=[tp], writes=[nTt])
                            else:
                                P.op("dve", lambda e, tp=tp, nTt=nTt, sub=sub: e.tensor_copy(out=nTt[:, 8:16, sub * 128:(sub + 1) * 128], in_=tp[:]), reads=[tp], writes=[nTt])
                    for bi, (c0, w, kind) in enumerate(BLOCKS):
                        wb = wpf.get(ti * len(BLOCKS) + bi)
                        if kind in ("xbc", "xp", "gate"):
                            for cc in range(4):
                                a = acc.next()
                                for kc in range(16):
                                    P.op("pe", lambda e, a=a, wb=wb, kc=kc, cc=cc, nTt=nTt: e.matmul(a[:], lhsT=wb[:, kc, cc * 128:(cc + 1) * 128], rhs=nTt[:, kc, :], start=(kc == 0), stop=(kc == 15)), reads=[wb, nTt], writes=[a])
                                sg = stg.next()
                                ch0 = c0 + cc * 128
                                if kind == "gate":
                                    P.op("act", lambda e, a=a, sg=sg: e.activation(out=sg[:], in_=a[:], func=AF.Sigmoid), reads=[a], writes=[sg])
                                    dst = gT[ch0 - O4:ch0 - O4 + 128, ti * TT:(ti + 1) * TT]; dbuf = gT
                                elif kind == "xbc":
                                    P.op("dve", lambda e, a=a, sg=sg: e.tensor_copy(out=sg[:], in_=a[:]), reads=[a], writes=[sg])
                                    dst = xbc_raw[ch0 - O1:ch0 - O1 + 128, 2 + ti * TT:2 + (ti + 1) * TT]; dbuf = xbc_raw
                                else:
                                    P.op("dve", lambda e, a=a, sg=sg: e.tensor_copy(out=sg[:], in_=a[:]), reads=[a], writes=[sg])
                                    dst = xp_raw[ch0 - O3:ch0 - O3 + 128, 8 + ti * TT:8 + (ti + 1) * TT]; dbuf = xp_raw
                                P.dma("sp", lambda e, dst=dst, sg=sg: e.dma_start(out=dst, in_=sg[:]), reads=[sg], writes=[dbuf])
                        elif kind == "z":
                            for sub in range(4):
                                a = acc.next()
                                for kc in range(16):
                                    P.op("pe", lambda e, a=a, wb=wb, kc=kc, sub=sub, nTt=nTt: e.matmul(a[:], lhsT=nTt[:, kc, sub * 128:(sub + 1) * 128], rhs=wb[:, kc, :], start=(kc == 0), stop=(kc == 15)), reads=[wb, nTt], writes=[a])
                                sg = stg.next()
                                P.op("act", lambda e, a=a, sg=sg: e.activation(out=sg[:], in_=a[:], func=AF.Silu), reads=[a], writes=[sg])
                                t0 = ti * TT + sub * 128
                                dst = sz[t0:t0 + 128, c0:c0 + 512]
                                P.dma("sp", lambda e, dst=dst, sg=sg: e.dma_start(out=dst, in_=sg[:]), reads=[sg], writes=[sz])
                        else:
                            for sub in range(4):
                                a = acc.next()
                                for kc in range(16):
                                    P.op("pe", lambda e, a=a, wb=wb, kc=kc, sub=sub, nTt=nTt: e.matmul(a[:, 0:64], lhsT=nTt[:, kc, sub * 128:(sub + 1) * 128], rhs=wb[:, kc, 0:64], start=(kc == 0), stop=(kc == 15)), reads=[wb, nTt], writes=[a])
                                d1 = dtt.next(); d2 = dto.next()
                                P.op("dve", lambda e, a=a, d1=d1: e.tensor_tensor(out=d1[:], in0=a[:, 0:64], in1=dtb[:], op=ALU.add), reads=[a, dtb], writes=[d1])
                                P.op("act", lambda e, d1=d1: e.activation(out=d1[:], in_=d1[:], func=AF.Exp), reads=[d1], writes=[d1])
                                P.op("act", lambda e, d1=d1, d2=d2: e.activation(out=d2[:], in_=d1[:], func=AF.Ln, bias=1.0), reads=[d1], writes=[d2])
                                t0 = ti * TT + sub * 128
                                P.dma("sp", lambda e, d2=d2, t0=t0: e.dma_start(out=dts[t0:t0 + 128, :], in_=d2[:]), reads=[d2], writes=[dts])
                P.end_phase()


        if 2 in phases:
            with ExitStack() as ph:
                P.stack = ph
                cw = P.sb("cw", [128, 5, 32], F32)
                cb = P.sb("cb", [128, 32], F32)
                for k in range(5):
                    P.dma("sp", lambda e, k=k: e.dma_start(out=cw[:, k, :], in_=conv_w[k, :].rearrange("(cc p) -> p cc", p=128), allow_slow_non_contiguous=True), reads=[conv_w], writes=[cw])
                P.dma("sp", lambda e: e.dma_start(out=cb[:], in_=conv_b[0, :].rearrange("(cc p) -> p cc", p=128), allow_slow_non_contiguous=True), reads=[conv_b], writes=[cb])
                raw = Ring([P.sb("raw%d" % i, [128, S + 4], BF16) for i in range(2)])
                cac = Ring([P.sb("cac%d" % i, [128, S], F32) for i in range(2)])
                cout = Ring([P.sb("cout%d" % i, [128, S], F32) for i in range(2)])
                def mk_rload(cc):
                    def f():
                        rw = raw.next()
                        P.dma("sp", lambda e: e.dma_start(out=rw[:], in_=xbc_raw[cc * 128:(cc + 1) * 128, :]), reads=[xbc_raw], writes=[rw])
                        return rw
                    return f
                rpf = Prefetch([mk_rload(cc) for cc in range(32)], 1)
                for cc in range(32):
                    rw = rpf.get(cc); ac = cac.next(); co = cout.next()
                    eng = "dve"
                    P.op(eng, lambda e, rw=rw, ac=ac, cc=cc: e.tensor_scalar(out=ac[:], in0=rw[:, 0:S], scalar1=cw[:, 0, cc:cc + 1], scalar2=None, op0=ALU.mult), reads=[rw, cw], writes=[ac])
                    for k in range(1, 5):
                        P.op(eng, lambda e, rw=rw, ac=ac, cc=cc, k=k: e.scalar_tensor_tensor(out=ac[:], in0=rw[:, k:k + S], scalar=cw[:, k, cc:cc + 1], in1=ac[:], op0=ALU.mult, op1=ALU.add), reads=[rw, cw, ac], writes=[ac])
                    P.op("act", lambda e, ac=ac, co=co, cc=cc: e.activation(out=co[:], in_=ac[:], func=AF.Silu, bias=cb[:, cc:cc + 1]), reads=[ac, cb], writes=[co])
                    P.dma("sp", lambda e, co=co, cc=cc: e.dma_start(out=xbc_c[cc * 128:(cc + 1) * 128, :], in_=co[:]), reads=[co], writes=[xbc_c])
                P.end_phase()

        if 3 in phases:
            with ExitStack() as ph:
                P.stack = ph
                NEG = -60000.0
                identf = P.sb("identf", [128, 128], F32)
                ones = P.sb("ones", [128, 128], F32)
                Tle = P.sb("Tle", [128, 128], F32); Tge = P.sb("Tge", [128, 128], F32)
                Tgt = P.sb("Tgt", [128, 128], F32); Tlt = P.sb("Tlt", [128, 128], F32)
                NEGf = P.sb("NEGf", [128, 4, 128], F32); NEGb = P.sb("NEGb", [128, 4, 128], F32)
                negA = P.sb("negA", [128, 64], F32)
                dsk = P.sb("dsk", [128, H], F32)
                snw = P.sb("snw", [128, D], F32)
                epsb = P.sb("epsb", [128, 1], F32)
                P.op("pool", lambda e: e.memset(epsb[:], EPS), writes=[epsb])
                P.op("pool", lambda e: e.memset(ones[:], 1.0), writes=[ones])
                P.op("pool", lambda e: e.memset(identf[:], 0.0), writes=[identf])
                P.op("pool", lambda e: e.affine_select(out=identf[:], in_=identf[:], pattern=[[-1, 128]], compare_op=ALU.not_equal, fill=1.0, base=0, channel_multiplier=1), reads=[identf], writes=[identf])
                for T_, pat, cm, cop in ((Tle, 1, -1, ALU.is_ge), (Tge, -1, 1, ALU.is_ge), (Tgt, -1, 1, ALU.is_gt), (Tlt, 1, -1, ALU.is_gt)):
                    P.op("pool", lambda e, T_=T_: e.memset(T_[:], 1.0), writes=[T_])
                    P.op("pool", lambda e, T_=T_, pat=pat, cm=cm, cop=cop: e.affine_select(out=T_[:], in_=T_[:], pattern=[[pat, 128]], compare_op=cop, fill=0.0, base=0, channel_multiplier=cm), reads=[T_], writes=[T_])
                for T_, pat, cm in ((NEGf, -1, 1), (NEGb, 1, -1)):
                    P.op("pool", lambda e, T_=T_: e.memset(T_[:], NEG), writes=[T_])
                    P.op("pool", lambda e, T_=T_, pat=pat, cm=cm: e.affine_select(out=T_[:], in_=T_[:], pattern=[[0, 4], [pat, 128]], compare_op=ALU.is_gt, fill=0.0, base=0, channel_multiplier=cm), reads=[T_], writes=[T_])
                P.dma("sp", lambda e: e.dma_start(out=negA[:], in_=a_log[0:1, :].to_broadcast([128, 64])), reads=[a_log], writes=[negA])
                P.op("act", lambda e: e.activation(out=negA[:], in_=negA[:], func=AF.Exp), reads=[negA], writes=[negA])
                P.op("dve", lambda e: e.tensor_scalar(out=negA[:], in0=negA[:], scalar1=-1.0, scalar2=None, op0=ALU.mult), reads=[negA], writes=[negA])
                P.dma("sp", lambda e: e.dma_start(out=dsk[:], in_=d_skip[0:1, :].to_broadcast([128, H])), reads=[d_skip], writes=[dsk])
                P.dma("sp", lambda e: e.dma_start(out=snw[:], in_=ssd_norm_w[0:1, :].to_broadcast([128, D])), reads=[ssd_norm_w], writes=[snw])

                xcr = Ring([P.sb("xc%d" % i, [128, 32, 128], F32) for i in range(2)])
                dtr = Ring([P.sb("dtc%d" % i, [128, 64], F32) for i in range(2)])
                adt = P.sb("adt", [128, 32], F32); nadt = P.sb("nadt", [128, 32], F32)
                esc = P.sb("esc", [128, 96], F32)
                xs_tm = P.sb("xs_tm", [128, D], F32)
                B_tm = P.sb("B_tm", [128, 1024], F32)
                Xm = P.sb("Xm", [128, D], F32); Xd = P.sb("Xd", [128, D], F32)
                hT = [P.sb("hT%d" % g, [128, 256], F32) for g in range(G)]
                yacc_t = ph.enter_context(nc.sbuf_tensor("yacc", [128, D], F32))
                yacc = [P.view(Buf("yacc", yacc_t, "sb"), yacc_t[:, g * 256:(g + 1) * 256]) for g in range(G)]
                cbt = Ring([P.sb("cbt%d" % i, [128, 128], F32) for i in range(2)])
                Eb = Ring([P.sb("Eb%d" % i, [128, 4, 128], F32) for i in range(2)])
                Mb = Ring([P.sb("Mb%d" % i, [128, 4, 128], F32) for i in range(2)])
                tmpb = Ring([P.sb("tmpb%d" % i, [128, 256], F32) for i in range(2)])
                yfr = Ring([P.sb("yfc%d" % i, [128, D], F32) for i in range(2)])
                szr = Ring([P.sb("szc%d" % i, [128, D], BF16) for i in range(2)])
                ysq = P.sb("ysq", [128, D], F32)
                gss = P.sb("gss", [128, G], F32)
                ynb = P.sb("ynb", [128, 16, 128], BF16)
                tpp = Ring([P.ps("tpp%d" % i, [128, 512], F32) for i in range(2)])
                smp = Ring([P.ps("smp%d" % i, [128, 128], F32) for i in range(2)])
                Dps = Ring([P.ps("Dps%d" % i, [128, 4, 128], F32) for i in range(2)])
                ydo = P.ps("ydo", [128, 512], F32)
                stp = P.ps("stp", [128, 256], F32)

                for dname, dof, Tin, Tout, NEGd in (("f", 0, Tle, Tgt, NEGf), ("b", 32, Tge, Tlt, NEGb)):
                    for g in range(G):
                        P.op("pool", lambda e, g=g: e.memset(hT[g][:], 0.0), writes=[hT[g]])
                    order = list(range(NCH)) if dname == "f" else list(range(NCH - 1, -1, -1))

                    def mk_cload(c, dname=dname):
                        def f():
                            t0 = c * 128
                            xc = xcr.next(); dtc = dtr.next()
                            P.dma("sp", lambda e: e.dma_start(out=xc[:], in_=xbc_c[:, t0:t0 + 128].rearrange("(cc p) t -> p cc t", p=128)), reads=[xbc_c], writes=[xc])
                            P.dma("sp", lambda e: e.dma_start(out=dtc[:], in_=dts[t0:t0 + 128, :]), reads=[dts], writes=[dtc])
                            yfc = szc = None
                            if dname == "b":
                                yfc = yfr.next(); szc = szr.next()
                                P.dma("sp", lambda e: e.dma_start(out=yfc[:], in_=yf[t0:t0 + 128, :]), reads=[yf], writes=[yfc])
                                P.dma("sp", lambda e: e.dma_start(out=szc[:], in_=sz[t0:t0 + 128, :]), reads=[sz], writes=[szc])
                            return xc, dtc, yfc, szc
                        return f
                    cpf = Prefetch([mk_cload(c) for c in order], 1)
                    for ci, c in enumerate(order):
                        t0 = c * 128
                        xc, dtc, yfc, szc = cpf.get(ci)
                        P.op("dve", lambda e, dtc=dtc, dof=dof: e.tensor_tensor(out=adt[:], in0=dtc[:, dof:dof + 32], in1=negA[:, dof:dof + 32], op=ALU.mult), reads=[dtc, negA], writes=[adt])
                        P.op("dve", lambda e: e.tensor_scalar(out=nadt[:], in0=adt[:], scalar1=-1.0, scalar2=None, op0=ALU.mult), reads=[adt], writes=[nadt])
                        sp_ = smp.next()
                        P.op("pe", lambda e, sp_=sp_, Tin=Tin: e.matmul(sp_[:, 0:32], lhsT=Tin[:], rhs=adt[:], start=True, stop=True), reads=[Tin, adt], writes=[sp_])
                        P.op("pe", lambda e, sp_=sp_, Tout=Tout: e.matmul(sp_[:, 32:64], lhsT=Tout[:], rhs=adt[:], start=True, stop=True), reads=[Tout, adt], writes=[sp_])
                        P.op("pe", lambda e, sp_=sp_: e.matmul(sp_[:, 64:96], lhsT=ones[:], rhs=adt[:], start=True, stop=True), reads=[ones, adt], writes=[sp_])
                        P.op("act", lambda e, sp_=sp_: e.activation(out=esc[:], in_=sp_[:, 0:96], func=AF.Exp), reads=[sp_], writes=[esc])
                        for q in range(6):
                            tp = tpp.next()
                            for j in range(4):
                                cc = q * 4 + j
                                P.op("pe", lambda e, tp=tp, j=j, cc=cc, xc=xc: e.transpose(out=tp[:, j * 128:(j + 1) * 128], in_=xc[:, cc, :], identity=identf[:]), reads=[xc, identf], writes=[tp])
                            if q < 4:
                                P.op("act", lambda e, tp=tp, q=q: e.copy(out=xs_tm[:, q * 512:(q + 1) * 512], in_=tp[:]), reads=[tp], writes=[xs_tm])
                            else:
                                P.op("act", lambda e, tp=tp, q=q: e.copy(out=B_tm[:, (q - 4) * 512:(q - 3) * 512], in_=tp[:]), reads=[tp], writes=[B_tm])
                        xs3 = xs_tm[:].rearrange("p (h d) -> p h d", d=PD)
                        P.op("dve", lambda e, dtc=dtc, dof=dof, xs3=xs3: e.tensor_tensor(out=Xm[:].rearrange("p (h d) -> p h d", d=PD), in0=xs3, in1=dtc[:, dof:dof + 32].unsqueeze(2).to_broadcast([128, H, PD]), op=ALU.mult), reads=[xs_tm, dtc], writes=[Xm])
                        P.op("pool", lambda e: e.tensor_tensor(out=Xd[:].rearrange("p (h d) -> p h d", d=PD), in0=Xm[:].rearrange("p (h d) -> p h d", d=PD), in1=esc[:, 32:64].unsqueeze(2).to_broadcast([128, H, PD]), op=ALU.mult), reads=[Xm, esc], writes=[Xd])
                        for g in range(G):
                            BT = xc[:, 16 + g, :]; CT = xc[:, 24 + g, :]
                            cp = smp.next(); cb_ = cbt.next(); dp = Dps.next(); Et = Eb.next(); Mt = Mb.next(); tb = tmpb.next()
                            P.op("pe", lambda e, cp=cp, BT=BT, CT=CT: e.matmul(cp[:], lhsT=BT, rhs=CT, start=True, stop=True), reads=[xc], writes=[cp])
                            P.op("act", lambda e, cp=cp, cb_=cb_: e.copy(out=cb_[:], in_=cp[:]), reads=[cp], writes=[cb_])
                            P.op("pe", lambda e, dp=dp, NEGd=NEGd: e.matmul(dp[:], lhsT=identf[:], rhs=NEGd[:], start=True, stop=False, skip_group_check=True), reads=[identf, NEGd], writes=[dp])
                            for r in range(R):
                                h = g * R + r
                                P.op("pe", lambda e, dp=dp, r=r, h=h, Tin=Tin: e.matmul(dp[:, r, :], lhsT=adt[:, h:h + 1].to_broadcast([128, 128]), rhs=Tin[:], start=False, stop=False, skip_group_check=True), reads=[adt, Tin], writes=[dp])
                                P.op("pe", lambda e, dp=dp, r=r, h=h, Tin=Tin: e.matmul(dp[:, r, :], lhsT=Tin[:], rhs=nadt[:, h:h + 1].to_broadcast([128, 128]), start=False, stop=(r == R - 1), skip_group_check=True), reads=[nadt, Tin], writes=[dp])
                            P.op("act", lambda e, dp=dp, Et=Et: e.activation(out=Et[:], in_=dp[:], func=AF.Exp), reads=[dp], writes=[Et])
                            P.op("dve", lambda e, Et=Et, Mt=Mt, cb_=cb_: e.tensor_tensor(out=Mt[:], in0=Et[:], in1=cb_[:].unsqueeze(1).to_broadcast([128, R, 128]), op=ALU.mult), reads=[Et, cb_], writes=[Mt])
                            for r in range(R):
                                h = g * R + r
                                P.op("pe", lambda e, Mt=Mt, r=r, h=h: e.matmul(ydo[:, r * 64:(r + 1) * 64], lhsT=Mt[:, r, :], rhs=Xm[:, h * 64:(h + 1) * 64], start=True, stop=True), reads=[Mt, Xm], writes=[ydo])
                            P.op("pe", lambda e, CT=CT, g=g: e.matmul(ydo[:, 256:512], lhsT=CT, rhs=hT[g][:], start=True, stop=True), reads=[xc, hT[g]], writes=[ydo])
                            P.op("dve", lambda e, tb=tb, g=g: e.tensor_tensor(out=tb[:].rearrange("p (h d) -> p h d", d=PD), in0=ydo[:, 256:512].rearrange("p (h d) -> p h d", d=PD), in1=esc[:, g * R:(g + 1) * R].unsqueeze(2).to_broadcast([128, R, PD]), op=ALU.mult), reads=[ydo, esc], writes=[tb])
                            P.op("dve", lambda e, tb=tb, g=g: e.tensor_tensor(out=yacc[g][:], in0=ydo[:, 0:256], in1=tb[:], op=ALU.add), reads=[ydo, tb], writes=[yacc[g]])
                            P.op("pe", lambda e, g=g: e.matmul(stp[:], lhsT=B_tm[:, g * 128:(g + 1) * 128], rhs=Xd[:, g * 256:(g + 1) * 256], start=True, stop=True), reads=[B_tm, Xd], writes=[stp])
                            P.op("pool", lambda e, g=g: e.tensor_tensor(out=hT[g][:].rearrange("p (h d) -> p h d", d=PD), in0=hT[g][:].rearrange("p (h d) -> p h d", d=PD), in1=esc[:, 64 + g * R:64 + (g + 1) * R].unsqueeze(2).to_broadcast([128, R, PD]), op=ALU.mult), reads=[hT[g], esc], writes=[hT[g]])
                            P.op("dve", lambda e, g=g: e.tensor_tensor(out=hT[g][:], in0=hT[g][:], in1=stp[:], op=ALU.add), reads=[hT[g], stp], writes=[hT[g]])
                        if dname == "f":
                            P.dma("sp", lambda e, t0=t0: e.dma_start(out=yf[t0:t0 + 128, :], in_=yacc_t[:]), reads=yacc, writes=[yf], sembuf=yacc[0])
                        else:
                            ya = yacc_t
                            P.op("dve", lambda e, yfc=yfc: e.tensor_tensor(out=ya[:], in0=ya[:], in1=yfc[:], op=ALU.add), reads=yacc + [yfc], writes=yacc)
                            P.op("pool", lambda e, xs3=xs3: e.tensor_tensor(out=ysq[:].rearrange("p (h d) -> p h d", d=PD), in0=xs3, in1=dsk[:].unsqueeze(2).to_broadcast([128, H, PD]), op=ALU.mult), reads=[xs_tm, dsk], writes=[ysq])
                            P.op("dve", lambda e: e.tensor_tensor(out=ya[:], in0=ya[:], in1=ysq[:], op=ALU.add), reads=yacc + [ysq], writes=yacc)
                            P.op("dve", lambda e, szc=szc: e.tensor_tensor(out=ya[:], in0=ya[:], in1=szc[:], op=ALU.mult), reads=yacc + [szc], writes=yacc)
                            P.op("pool", lambda e: e.tensor_tensor(out=ysq[:], in0=ya[:], in1=ya[:], op=ALU.mult), reads=yacc, writes=[ysq])
                            P.op("dve", lambda e: e.tensor_reduce(out=gss[:], in_=ysq[:].rearrange("p (g d) -> p g d", d=256), axis=AX.X, op=ALU.add), reads=[ysq], writes=[gss])
                            P.op("act", lambda e: e.activation(out=gss[:], in_=gss[:], func=AF.Sqrt, bias=epsb[:, 0:1], scale=1.0 / 256), reads=[gss, epsb], writes=[gss])
                            P.op("dve", lambda e: e.reciprocal(out=gss[:], in_=gss[:]), reads=[gss], writes=[gss])
                            P.op("dve", lambda e: e.tensor_tensor(out=ya[:].rearrange("p (g d) -> p g d", d=256), in0=ya[:].rearrange("p (g d) -> p g d", d=256), in1=gss[:].unsqueeze(2).to_broadcast([128, G, 256]), op=ALU.mult), reads=yacc + [gss], writes=yacc)
                            P.op("pool", lambda e: e.tensor_tensor(out=ya[:], in0=ya[:], in1=snw[:], op=ALU.mult), reads=yacc + [snw], writes=yacc)
                            for q in range(4):
                                tp = tpp.next()
                                for j in range(4):
                                    kc = q * 4 + j
                                    P.op("pe", lambda e, tp=tp, j=j, kc=kc: e.transpose(out=tp[:, j * 128:(j + 1) * 128], in_=ya[:, kc * 128:(kc + 1) * 128], identity=identf[:]), reads=yacc + [identf], writes=[tp])
                                P.op("act", lambda e, tp=tp, q=q: e.copy(out=ynb[:, q * 4:(q + 1) * 4, :], in_=tp[:].rearrange("p (a b) -> p a b", b=128)), reads=[tp], writes=[ynb])
                            P.dma("sp", lambda e, t0=t0: e.dma_start(out=ynT[:, t0:t0 + 128].rearrange("(kc p) t -> p kc t", p=128), in_=ynb[:]), reads=[ynb], writes=[ynT])
                P.end_phase()


        if 4 in phases:
            with ExitStack() as ph:
                P.stack = ph
                TA = 512
                NTA = S // TA
                wpg = P.sb("wpg", [128, 8, 256], BF16)
                psc = P.sb("psc", [128, 8], F32)
                ped = P.sb("ped", [128, 4, 16], F32)
                P.dma("sp", lambda e: e.dma_start(out=wpg[:], in_=wpg_d[:]), reads=[wpg_d], writes=[wpg])
                P.dma("sp", lambda e: e.dma_start(out=psc[:], in_=pool_scale[0, :].rearrange("(cc p) -> p cc", p=128), allow_slow_non_contiguous=True), reads=[pool_scale], writes=[psc])
                P.dma("sp", lambda e: e.dma_start(out=ped[:].rearrange("p a b -> p (a b)"), in_=pedge[0:1, :].to_broadcast([128, 64])), reads=[pedge], writes=[ped])
                ynt_r = Ring([P.sb("ynt%d" % i, [128, 16, TA], BF16) for i in range(2)])
                xpt_r = Ring([P.sb("xpt%d" % i, [128, 8, TA + 16], BF16) for i in range(2)])
                lv = [P.sb("lv%d" % i, [128, 2, TA + 16], F32) for i in range(2)]
                pooledT = P.sb("pooledT", [128, 8, TA], BF16)
                p2T = P.sb("p2T", [128, 8, TA], BF16)
                gtr = Ring([P.sb("gt%d" % i, [128, 8, TA], BF16) for i in range(2)])
                merged = P.sb("merged", [128, 16, TA], BF16)
                t1r = Ring([P.sb("t1_%d" % i, [128, TA], F32) for i in range(2)])
                t2r = Ring([P.sb("t2_%d" % i, [128, TA], F32) for i in range(2)])
                wbk = Ring([P.sb("wb3_%d" % i, [128, 16, 512], BF16) for i in range(3)])
                wpbk = Ring([P.sb("wpb%d" % i, [128, 8, 512], BF16) for i in range(2)])
                xsr = Ring([P.sb("xs3_%d" % i, [128, D], F32) for i in range(2)])
                acc = Ring([P.ps("acc3_%d" % i, [128, 512], F32) for i in range(6)])
                WINS = (2, 4, 8, 16)
                def mk_tload(ti):
                    def f():
                        t0 = ti * TA
                        ynt = ynt_r.next(); xpt = xpt_r.next()
                        P.dma("sp", lambda e: e.dma_start(out=ynt[:], in_=ynT[:, t0:t0 + TA].rearrange("(kc p) t -> p kc t", p=128)), reads=[ynT], writes=[ynt])
                        P.dma("sp", lambda e: e.dma_start(out=xpt[:], in_=xp_raw[:, t0:t0 + TA + 16].rearrange("(cc p) t -> p cc t", p=128)), reads=[xp_raw], writes=[xpt])
                        return ynt, xpt
                    return f

                def mk_mload(ti, b):
                    def f():
                        t0 = ti * TA
                        wb = wbk.next(); wp = wpbk.next(); gt = gtr.next()
                        P.dma("sp", lambda e: e.dma_start(out=wb[:], in_=wssd_d[b][:]), reads=[wssd_d[b]], writes=[wb])
                        P.dma("sp", lambda e: e.dma_start(out=wp[:], in_=wpb_d[b][:]), reads=[wpb_d[b]], writes=[wp])
                        P.dma("sp", lambda e: e.dma_start(out=gt[:, 0:4, :], in_=gT[b * 512:(b + 1) * 512, t0:t0 + TA].rearrange("(cc p) t -> p cc t", p=128)), reads=[gT], writes=[gt])
                        P.dma("sp", lambda e: e.dma_start(out=gt[:, 4:8, :], in_=gT[D + b * 512:D + (b + 1) * 512, t0:t0 + TA].rearrange("(cc p) t -> p cc t", p=128)), reads=[gT], writes=[gt])
                        return wb, wp, gt
                    return f

                def mk_oload(b):
                    def f():
                        wb = wbk.next()
                        P.dma("sp", lambda e: e.dma_start(out=wb[:], in_=wout_d[b][:]), reads=[wout_d[b]], writes=[wb])
                        return wb
                    return f
                tpf = Prefetch([mk_tload(ti) for ti in range(NTA)], 1)
                wl = []
                for ti in range(NTA):
                    wl += [mk_mload(ti, b) for b in range(4)]
                    for pair in range(TA // 256):
                        wl += [mk_oload(b) for b in range(4)]
                wpf3 = Prefetch(wl, 1)
                wk3 = 0
                for ti in range(NTA):
                    t0 = ti * TA
                    ynt, xpt = tpf.get(ti)
                    for gi, w in enumerate(WINS):
                        src = xpt[:, 2 * gi:2 * gi + 2, :]
                        xpt_ = xpt
                        L = TA + 16
                        step = 1
                        cur = None
                        li = 0
                        while step < w:
                            dst = lv[li % 2]
                            a_in = src if cur is None else cur[:, :, :]
                            rd = [xpt] if cur is None else [cur]
                            P.op("pool", lambda e, dst=dst, a_in=a_in, L=L, step=step: e.tensor_tensor(out=dst[:, :, 0:L - step], in0=a_in[:, :, 0:L - step], in1=a_in[:, :, step:L], op=ALU.add), reads=rd, writes=[dst])
                            cur = dst; L -= step; step *= 2; li += 1
                        off = 8 - w // 2
                        if ti == 0:
                            P.op("pool", lambda e, cur=cur, off=off, gi=gi: e.tensor_tensor(out=cur[:, :, off:off + 8], in0=cur[:, :, off:off + 8], in1=ped[:, gi, 0:8].unsqueeze(1).to_broadcast([128, 2, 8]), op=ALU.mult), reads=[cur, ped], writes=[cur])
                        if ti == NTA - 1:
                            P.op("pool", lambda e, cur=cur, off=off, gi=gi: e.tensor_tensor(out=cur[:, :, off + TA - 8:off + TA], in0=cur[:, :, off + TA - 8:off + TA], in1=ped[:, gi, 8:16].unsqueeze(1).to_broadcast([128, 2, 8]), op=ALU.mult), reads=[cur, ped], writes=[cur])
                        P.op("dve", lambda e, cur=cur, off=off, gi=gi, w=w, xpt_=xpt_: e.scalar_tensor_tensor(out=pooledT[:, 2 * gi:2 * gi + 2, :], in0=cur[:, :, off:off + TA], scalar=1.0 / w, in1=xpt_[:, 2 * gi:2 * gi + 2, 8:8 + TA], op0=ALU.mult, op1=ALU.subtract), reads=[cur, xpt_], writes=[pooledT])
                    for gi in range(4):
                        for dc in range(2):
                            a = acc.next()
                            for kc in range(2):
                                P.op("pe", lambda e, a=a, gi=gi, dc=dc, kc=kc: e.matmul(a[:, 0:TA], lhsT=wpg[:, gi * 2 + kc, dc * 128:(dc + 1) * 128], rhs=pooledT[:, gi * 2 + kc, :], start=(kc == 0), stop=(kc == 1)), reads=[wpg, pooledT], writes=[a])
                            P.op("act", lambda e, a=a, gi=gi, dc=dc: e.activation(out=p2T[:, gi * 2 + dc, :], in_=a[:, 0:TA], func=AF.Copy, scale=psc[:, gi * 2 + dc:gi * 2 + dc + 1]), reads=[a, psc], writes=[p2T])
                    for b in range(4):
                        wb, wp, gt = wpf3.get(wk3); wk3 += 1
                        for cc in range(4):
                            dch = b * 4 + cc
                            a1 = acc.next(); a2 = acc.next()
                            for kc in range(16):
                                P.op("pe", lambda e, a1=a1, wb=wb, kc=kc, cc=cc, ynt=ynt: e.matmul(a1[:, 0:TA], lhsT=wb[:, kc, cc * 128:(cc + 1) * 128], rhs=ynt[:, kc, :], start=(kc == 0), stop=(kc == 15)), reads=[wb, ynt], writes=[a1])
                            for kc in range(8):
                                P.op("pe", lambda e, a2=a2, wp=wp, kc=kc, cc=cc: e.matmul(a2[:, 0:TA], lhsT=wp[:, kc, cc * 128:(cc + 1) * 128], rhs=p2T[:, kc, :], start=(kc == 0), stop=(kc == 7)), reads=[wp, p2T], writes=[a2])
                            t1 = t1r.next(); t2 = t2r.next()
                            P.op("dve", lambda e, a1=a1, t1=t1, gt=gt, cc=cc: e.tensor_tensor(out=t1[:], in0=a1[:, 0:TA], in1=gt[:, cc, :], op=ALU.mult), reads=[a1, gt], writes=[t1])
                            P.op("dve", lambda e, a2=a2, t2=t2, gt=gt, cc=cc: e.tensor_tensor(out=t2[:], in0=a2[:, 0:TA], in1=gt[:, 4 + cc, :], op=ALU.mult), reads=[a2, gt], writes=[t2])
                            P.op("pool", lambda e, t1=t1, t2=t2, dch=dch: e.tensor_tensor(out=merged[:, dch, :], in0=t1[:], in1=t2[:], op=ALU.add), reads=[t1, t2], writes=[merged])
                    for pair in range(TA // 256):
                        subs = (2 * pair, 2 * pair + 1)
                        xt = {}
                        for sub in subs:
                            xt[sub] = xsr.next()
                            P.dma("sp", lambda e, xs_=xt[sub], sub=sub, t0=t0: e.dma_start(out=xs_[:], in_=x[t0 + sub * 128:t0 + (sub + 1) * 128, :]), reads=[x], writes=[xt[sub]])
                        for b in range(4):
                            wb = wpf3.get(wk3); wk3 += 1
                            for sub in subs:
                                a = acc.next()
                                for kc in range(16):
                                    P.op("pe", lambda e, a=a, wb=wb, kc=kc, sub=sub: e.matmul(a[:], lhsT=merged[:, kc, sub * 128:(sub + 1) * 128], rhs=wb[:, kc, :], start=(kc == 0), stop=(kc == 15)), reads=[wb, merged], writes=[a])
                                P.op("dve", lambda e, a=a, xs_=xt[sub], b=b: e.tensor_tensor(out=xs_[:, b * 512:(b + 1) * 512], in0=a[:], in1=xs_[:, b * 512:(b + 1) * 512], op=ALU.add), reads=[a, xt[sub]], writes=[xt[sub]])
                        for sub in subs:
                            P.dma("sp", lambda e, xs_=xt[sub], sub=sub, t0=t0: e.dma_start(out=hsc[t0 + sub * 128:t0 + (sub + 1) * 128, :], in_=xs_[:]), reads=[xt[sub]], writes=[hsc])
                P.end_phase()


        if 5 in phases:
            with ExitStack() as ph:
                P.stack = ph
                wq = P.sb("wq", [128, 4, 16, 512], BF16)
                qtm = P.sb("qtm", [128, D], F32)
                keysN = Buf("keysN", qtm.t[:].rearrange("p (h d) -> p h d", d=128), "sb")
                keysT = P.sb("keysT", [128, 16, 128], F32)
                fw = P.sb("fw", [128, D], F32)
                epsb = P.sb("epsb", [128, 1], F32)
                identf = P.sb("identf", [128, 128], F32)
                identb = P.sb("identb", [128, 128], BF16)
                iot_i = P.sb("iot_i", [128, 16], I32)
                iot = P.sb("iot", [128, 16], F32)
                for b in range(4):
                    P.dma("sp", lambda e, b=b: e.dma_start(out=wq[:, b, :, :], in_=wq_d[b][:]), reads=[wq_d[b]], writes=[wq])
                P.dma("sp", lambda e: e.dma_start(out=keysN[:], in_=sub_keys[:].rearrange("h n d -> n h d")), reads=[sub_keys], writes=[qtm])
                P.dma("sp", lambda e: e.dma_start(out=fw[:], in_=ffn_norm_w[0:1, :].to_broadcast([128, D])), reads=[ffn_norm_w], writes=[fw])
                P.op("pool", lambda e: e.memset(epsb[:], EPS), writes=[epsb])
                P.op("pool", lambda e: e.memset(identf[:], 0.0), writes=[identf])
                P.op("pool", lambda e: e.affine_select(out=identf[:], in_=identf[:], pattern=[[-1, 128]], compare_op=ALU.not_equal, fill=1.0, base=0, channel_multiplier=1), reads=[identf], writes=[identf])
                P.op("dve", lambda e: e.tensor_copy(out=identb[:], in_=identf[:]), reads=[identf], writes=[identb])
                P.op("pool", lambda e: e.iota(iot_i[:], pattern=[[1, 16]], base=0, channel_multiplier=0), writes=[iot_i])
                P.op("dve", lambda e: e.tensor_copy(out=iot[:], in_=iot_i[:]), reads=[iot_i], writes=[iot])
                tpq = Ring([P.ps("tpq%d" % i, [128, 4, 128], F32) for i in range(2)])
                tpb = Ring([P.ps("tpb%d" % i, [128, 8, 128], BF16) for i in range(2)])
                accq = Ring([P.ps("accq%d" % i, [128, 512], F32) for i in range(2)])
                for q4 in range(4):
                    tp = tpq.next()
                    for j in range(4):
                        hh = q4 * 4 + j
                        P.op("pe", lambda e, tp=tp, j=j, hh=hh: e.transpose(out=tp[:, j, :], in_=keysN[:, hh, :], identity=identf[:]), reads=[qtm, identf], writes=[tp])
                    P.op("act", lambda e, tp=tp, q4=q4: e.copy(out=keysT[:, q4 * 4:(q4 + 1) * 4, :], in_=tp[:]), reads=[tp], writes=[keysT])

                hr = Ring([P.sb("h4_%d" % i, [128, D], F32) for i in range(2)])
                hn = P.sb("hn", [128, D], F32)
                hnb = P.sb("hnb", [128, D], BF16)
                junkf = P.sb("junkf", [128, D], F32)
                ssq = P.sb("ssq4", [128, 1], F32)
                hnT = P.sb("hnT", [128, 16, 128], BF16)
                qT = P.sb("qT", [128, 16, 128], F32)
                sc = P.sb("sc", [128, 16, 128], F32)
                wk = P.sb("wk", [128, 128], F32)
                v1 = P.sb("v1", [128, 16, 16], F32)
                i1 = P.sb("i1", [128, 16, 16], U32)
                i1f = P.sb("i1f", [128, 16, 16], F32)
                cand = P.sb("cand", [128, 8, 256], F32)
                cwk = P.sb("cwk", [128, 256], F32)
                tv = P.sb("tv", [128, 8, 16], F32)
                pos = P.sb("pos", [128, 8, 16], U32)
                posf = P.sb("posf", [128, 8, 16], F32)
                r1f = P.sb("r1f", [128, 8, 16], F32)
                r1u = P.sb("r1u", [128, 8, 16], U32)
                r2u = P.sb("r2u", [128, 8, 16], U32)
                r2f = P.sb("r2f", [128, 8, 16], F32)
                eqb = Buf("eqb", cand.t[:].rearrange("p h (a b) -> p h a b", b=16), "sb")
                sel1 = P.sb("sel1", [128, 8, 16], F32)
                sel2 = P.sb("sel2", [128, 8, 16], F32)
                sel1r = P.sb("sel1r", [128, 8, 16], F32)
                rtr = Ring([P.sb("rt%d" % i, [128, 3, 128], F32) for i in range(2)])
                idx = P.sb("idx", [128, 128], I32)
                gexp = P.sb("gexp", [128, 8, 16], F32)
                gsum = P.sb("gsum", [128, 8], F32)
                gate = P.sb("gate", [128, 8, 16], F32)

                for c in range(NCH):
                    t0 = c * 128
                    ht = hr.next()
                    P.dma("sp", lambda e, ht=ht, t0=t0: e.dma_start(out=ht[:], in_=hsc[t0:t0 + 128, :]), reads=[hsc], writes=[ht])
                    P.op("act", lambda e, ht=ht: e.activation(out=junkf[:], in_=ht[:], func=AF.Square, accum_out=ssq[:]), reads=[ht], writes=[junkf, ssq])
                    P.op("act", lambda e: e.activation(out=ssq[:], in_=ssq[:], func=AF.Sqrt, bias=epsb[:, 0:1], scale=1.0 / D), reads=[ssq, epsb], writes=[ssq])
                    P.op("dve", lambda e: e.reciprocal(out=ssq[:], in_=ssq[:]), reads=[ssq], writes=[ssq])
                    P.op("dve", lambda e, ht=ht: e.scalar_tensor_tensor(out=hn[:], in0=ht[:], scalar=ssq[:, 0:1], in1=fw[:], op0=ALU.mult, op1=ALU.mult), reads=[ht, ssq, fw], writes=[hn])
                    P.op("act", lambda e: e.copy(out=hnb[:], in_=hn[:]), reads=[hn], writes=[hnb])
                    for half in range(2):
                        tp = tpb.next()
                        for j in range(8):
                            kc = half * 8 + j
                            P.op("pe", lambda e, tp=tp, j=j, kc=kc: e.transpose(out=tp[:, j, :], in_=hnb[:, kc * 128:(kc + 1) * 128], identity=identb[:]), reads=[hnb, identb], writes=[tp])
                        P.op("act", lambda e, tp=tp, half=half: e.copy(out=hnT[:, half * 8:(half + 1) * 8, :], in_=tp[:]), reads=[tp], writes=[hnT])
                    for b in range(4):
                        a = accq.next()
                        for kc in range(16):
                            P.op("pe", lambda e, a=a, b=b, kc=kc: e.matmul(a[:], lhsT=hnT[:, kc, :], rhs=wq[:, b, kc, :], start=(kc == 0), stop=(kc == 15)), reads=[hnT, wq], writes=[a])
                        P.op("act", lambda e, a=a, b=b: e.copy(out=qtm[:, b * 512:(b + 1) * 512], in_=a[:]), reads=[a], writes=[qtm])
                    for q4 in range(4):
                        tp = tpq.next()
                        for j in range(4):
                            hh = q4 * 4 + j
                            P.op("pe", lambda e, tp=tp, j=j, hh=hh: e.transpose(out=tp[:, j, :], in_=qtm[:, hh * 128:(hh + 1) * 128], identity=identf[:]), reads=[qtm, identf], writes=[tp])
                        P.op("act", lambda e, tp=tp, q4=q4: e.copy(out=qT[:, q4 * 4:(q4 + 1) * 4, :], in_=tp[:]), reads=[tp], writes=[qT])
                    for q4 in range(4):
                        tp = tpq.next()
                        for j in range(4):
                            hh = q4 * 4 + j
                            P.op("pe", lambda e, tp=tp, j=j, hh=hh: e.matmul(tp[:, j, :], lhsT=qT[:, hh, :], rhs=keysT[:, hh, :], start=True, stop=True), reads=[qT, keysT], writes=[tp])
                        P.op("act", lambda e, tp=tp, q4=q4: e.copy(out=sc[:, q4 * 4:(q4 + 1) * 4, :], in_=tp[:]), reads=[tp], writes=[sc])
                    for hh in range(16):
                        P.op("dve", lambda e, hh=hh: e.max(out=v1[:, hh, 0:8], in_=sc[:, hh, :]), reads=[sc], writes=[v1])
                        P.op("dve", lambda e, hh=hh: e.max_index(out=i1[:, hh, 0:8], in_max=v1[:, hh, 0:8], in_values=sc[:, hh, :]), reads=[sc, v1], writes=[i1])
                        P.op("dve", lambda e, hh=hh: e.match_replace(out=wk[:], in_to_replace=v1[:, hh, 0:8], in_values=sc[:, hh, :], imm_value=-1e30), reads=[sc, v1], writes=[wk])
                        P.op("dve", lambda e, hh=hh: e.max(out=v1[:, hh, 8:16], in_=wk[:]), reads=[wk], writes=[v1])
                        P.op("dve", lambda e, hh=hh: e.max_index(out=i1[:, hh, 8:16], in_max=v1[:, hh, 8:16], in_values=wk[:]), reads=[wk, v1], writes=[i1])
                    v4 = v1[:].rearrange("p (h two) r -> p h two r", two=2)
                    P.op("dve", lambda e, v4=v4: e.tensor_tensor(out=cand[:].rearrange("p h (a b) -> p h a b", b=16), in0=v4[:, :, 0, :].unsqueeze(3).to_broadcast([128, 8, 16, 16]), in1=v4[:, :, 1, :].unsqueeze(2).to_broadcast([128, 8, 16, 16]), op=ALU.add), reads=[v1], writes=[cand])
                    for h in range(8):
                        P.op("dve", lambda e, h=h: e.max(out=tv[:, h, 0:8], in_=cand[:, h, :]), reads=[cand], writes=[tv])
                        P.op("dve", lambda e, h=h: e.max_index(out=pos[:, h, 0:8], in_max=tv[:, h, 0:8], in_values=cand[:, h, :]), reads=[cand, tv], writes=[pos])
                        P.op("dve", lambda e, h=h: e.match_replace(out=cwk[:], in_to_replace=tv[:, h, 0:8], in_values=cand[:, h, :], imm_value=-1e30), reads=[cand, tv], writes=[cwk])
                        P.op("dve", lambda e, h=h: e.max(out=tv[:, h, 8:16], in_=cwk[:]), reads=[cwk], writes=[tv])
                        P.op("dve", lambda e, h=h: e.max_index(out=pos[:, h, 8:16], in_max=tv[:, h, 8:16], in_values=cwk[:]), reads=[cwk, tv], writes=[pos])
                    P.op("dve", lambda e: e.tensor_tensor(out=gexp[:], in0=tv[:], in1=tv[:, :, 0:1].to_broadcast([128, 8, 16]), op=ALU.subtract), reads=[tv], writes=[gexp])
                    P.op("act", lambda e: e.activation(out=gexp[:], in_=gexp[:], func=AF.Exp), reads=[gexp], writes=[gexp])
                    P.op("dve", lambda e: e.tensor_reduce(out=gsum[:], in_=gexp[:], axis=AX.X, op=ALU.add), reads=[gexp], writes=[gsum])
                    P.op("dve", lambda e: e.reciprocal(out=gsum[:], in_=gsum[:]), reads=[gsum], writes=[gsum])
                    P.op("dve", lambda e: e.tensor_tensor(out=gate[:], in0=gexp[:], in1=gsum[:].unsqueeze(2).to_broadcast([128, 8, 16]), op=ALU.mult), reads=[gexp, gsum], writes=[gate])
                    P.op("dve", lambda e: e.tensor_copy(out=posf[:], in_=pos[:]), reads=[pos], writes=[posf])
                    P.op("dve", lambda e: e.tensor_copy(out=i1f[:], in_=i1[:]), reads=[i1], writes=[i1f])
                    P.op("dve", lambda e: e.tensor_single_scalar(out=r1u[:], in_=pos[:], scalar=4, op=ALU.logical_shift_right), reads=[pos], writes=[r1u])
                    P.op("dve", lambda e: e.tensor_single_scalar(out=r2u[:], in_=pos[:], scalar=15, op=ALU.bitwise_and), reads=[pos], writes=[r2u])
                    P.op("dve", lambda e: e.tensor_copy(out=r1f[:], in_=r1u[:]), reads=[r1u], writes=[r1f])
                    P.op("dve", lambda e: e.tensor_copy(out=r2f[:], in_=r2u[:]), reads=[r2u], writes=[r2f])
                    i4 = i1f[:].rearrange("p (h two) r -> p h two r", two=2)
                    iot4 = iot[:].unsqueeze(1).unsqueeze(1).to_broadcast([128, 8, 16, 16])
                    for rf, two, sel in ((r1f, 0, sel1), (r2f, 1, sel2)):
                        P.op("dve", lambda e, rf=rf: e.tensor_tensor(out=eqb[:], in0=rf[:].unsqueeze(3).to_broadcast([128, 8, 16, 16]), in1=iot4, op=ALU.is_equal), reads=[rf, iot], writes=[cand])
                        P.op("dve", lambda e, two=two, i4=i4: e.tensor_tensor(out=eqb[:], in0=eqb[:], in1=i4[:, :, two, :].unsqueeze(2).to_broadcast([128, 8, 16, 16]), op=ALU.mult), reads=[cand, i1f], writes=[cand])
                        P.op("dve", lambda e, sel=sel: e.tensor_reduce(out=sel[:], in_=eqb[:], axis=AX.X, op=ALU.add), reads=[cand], writes=[sel])
                    P.op("dve", lambda e: e.tensor_copy(out=sel1r[:], in_=sel1[:]), reads=[sel1], writes=[sel1r])
                    if debug:
                        P.op("dve", lambda e: e.scalar_tensor_tensor(out=sel1[:], in0=sel1[:], scalar=128.0, in1=sel2[:], op0=ALU.mult, op1=ALU.add), reads=[sel1, sel2], writes=[sel1])
                        P.op("dve", lambda e: e.tensor_copy(out=idx[:], in_=sel1[:].rearrange("p h k -> p (h k)")), reads=[sel1], writes=[idx])
                    tp = tpq.next()
                    P.op("pe", lambda e, tp=tp: e.transpose(out=tp[:, 0, :], in_=sel1r[:].rearrange("p h k -> p (h k)"), identity=identf[:]), reads=[sel1r, identf], writes=[tp])
                    P.op("pe", lambda e, tp=tp: e.transpose(out=tp[:, 1, :], in_=sel2[:].rearrange("p h k -> p (h k)"), identity=identf[:]), reads=[sel2, identf], writes=[tp])
                    P.op("pe", lambda e, tp=tp: e.transpose(out=tp[:, 2, :], in_=gate[:].rearrange("p h k -> p (h k)"), identity=identf[:]), reads=[gate, identf], writes=[tp])
                    rt = rtr.next()
                    P.op("act", lambda e, tp=tp, rt=rt: e.copy(out=rt[:], in_=tp[:, 0:3, :]), reads=[tp], writes=[rt])
                    P.dma("sp", lambda e, rt=rt, c=c: e.dma_start(out=rt_d[c], in_=rt[:]), reads=[rt], writes=[rt_d])
                    P.dma("sp", lambda e, c=c: e.dma_start(out=hnT_d[c], in_=hnT[:]), reads=[hnT], writes=[hnT_d])
                    if debug:
                        P.dma("sp", lambda e, t0=t0: e.dma_start(out=idx_dbg[t0:t0 + 128, :], in_=idx[:]), reads=[idx], writes=[idx_dbg])
                        P.dma("sp", lambda e, t0=t0: e.dma_start(out=gate_dbg[t0:t0 + 128, :], in_=gate[:].rearrange("p h k -> p (h k)")), reads=[gate], writes=[gate_dbg])
                P.end_phase()


        if 6 in phases:
            with ExitStack() as ph:
                P.stack = ph
                TP = 256
                NTP = S // TP
                NSUB = TP // 128
                GC = 16
                NG = 128 // GC
                fnw = P.sb("fnw", [128, D], F32)
                epsb = P.sb("epsb", [128, 1], F32)
                iot_i = P.sb("iotr_i", [128, 128], I32)
                iotr = P.sb("iotr", [128, 128], F32)
                P.dma("sp", lambda e: e.dma_start(out=fnw[:], in_=final_norm_w[0:1, :].to_broadcast([128, D])), reads=[final_norm_w], writes=[fnw])
                P.op("pool", lambda e: e.memset(epsb[:], EPS), writes=[epsb])
                P.op("pool", lambda e: e.iota(iot_i[:], pattern=[[1, 128]], base=0, channel_multiplier=0), writes=[iot_i])
                P.op("dve", lambda e: e.tensor_copy(out=iotr[:], in_=iot_i[:]), reads=[iot_i], writes=[iotr])
                GT = P.sb("GT", [128, 128, TP], BF16)
                hnr = Ring([P.sb("hnT4_%d" % i, [128, 16, TP], BF16) for i in range(2)])
                rtr4 = Ring([P.sb("rt4_%d" % i, [128, NSUB, 3, 128], F32) for i in range(2)])
                oh1r = Ring([P.sb("oh1_%d" % i, [128, 8, 128], BF16) for i in range(2)])
                oh2r = Ring([P.sb("oh2_%d" % i, [128, 8, 128], BF16) for i in range(2)])
                utr = Ring([P.sb("ut%d" % i, [128, 16, 128], BF16) for i in range(4)])
                actr = Ring([P.sb("gel%d" % i, [128, TP], F32) for i in range(2)])
                wtr = Ring([P.sb("wt%d" % i, [128, GC, TP], BF16) for i in range(2)])
                vtr = Ring([P.sb("vt%d" % i, [128, GC, 512], BF16) for i in range(2)])
                pacc = [P.sb("pacc%d" % i, [128, D], F32) for i in range(NSUB)]
                h4r = Ring([P.sb("h4f%d" % i, [128, D], F32) for i in range(2)])
                junkb = P.sb("junkb", [128, D], BF16)
                ssq = P.sb("ssq6", [128, 1], F32)
                gps = Ring([P.ps("gps%d" % i, [128, 4, 128], F32) for i in range(2)])
                aps = Ring([P.ps("aps%d" % i, [128, TP], F32) for i in range(2)])
                ops = Ring([P.ps("ops%d" % i, [128, 512], F32) for i in range(3)])

                def mk_tl(ti):
                    def f():
                        hn_ = hnr.next(); rt_ = rtr4.next()
                        for sub in range(NSUB):
                            c = ti * NSUB + sub
                            P.dma("sp", lambda e, sub=sub, c=c: e.dma_start(out=hn_[:, :, sub * 128:(sub + 1) * 128], in_=hnT_d[c]), reads=[hnT_d], writes=[hn_])
                            P.dma("sp", lambda e, sub=sub, c=c: e.dma_start(out=rt_[:, sub, :, :], in_=rt_d[c]), reads=[rt_d], writes=[rt_])
                        return hn_, rt_
                    return f

                def mk_ul(c):
                    def f():
                        ut = utr.next()
                        P.dma("sp", lambda e: e.dma_start(out=ut[:], in_=UT_d[c]), reads=[UT_d], writes=[ut])
                        return ut
                    return f

                def mk_vl(g, blk):
                    def f():
                        vt = vtr.next()
                        P.dma("act", lambda e: e.dma_start(out=vt[:], in_=Vbf_d[g * GC * 128:(g + 1) * GC * 128, blk * 512:(blk + 1) * 512].rearrange("(ci p) d -> p ci d", p=128)), reads=[Vbf_d], writes=[vt])
                        return vt
                    return f
                tpf4 = Prefetch([mk_tl(ti) for ti in range(NTP)], 1)
                upf = Prefetch([mk_ul(c) for ti in range(NTP) for c in range(128)], 3)
                vpf = Prefetch([mk_vl(g, blk) for ti in range(NTP) for g in range(NG) for blk in range(4)], 1)
                cnt = {"u": 0, "v": 0}

                def GT_units(rt_):
                    units = []
                    for tb in range(TP // 8):
                        def f(tb=tb):
                            sub, tt = (tb * 8) // 128, (tb * 8) % 128
                            o1 = oh1r.next(); o2 = oh2r.next()
                            iob = iotr[:].unsqueeze(1).to_broadcast([128, 8, 128])
                            P.op("dve", lambda e: e.tensor_tensor(out=o1[:], in0=iob, in1=rt_[:, sub, 0, tt:tt + 8].unsqueeze(2).to_broadcast([128, 8, 128]), op=ALU.is_equal), reads=[iotr, rt_], writes=[o1])
                            P.op("dve", lambda e: e.tensor_tensor(out=o2[:], in0=iob, in1=rt_[:, sub, 1, tt:tt + 8].unsqueeze(2).to_broadcast([128, 8, 128]), op=ALU.is_equal), reads=[iotr, rt_], writes=[o2])
                            P.op("pool", lambda e: e.tensor_tensor(out=o2[:], in0=o2[:], in1=rt_[:, sub, 2, tt:tt + 8].unsqueeze(2).to_broadcast([128, 8, 128]), op=ALU.mult), reads=[o2, rt_], writes=[o2])
                            for q in range(2):
                                gp = gps.next()
                                for u in range(4):
                                    P.op("pe", lambda e, gp=gp, u=u, q=q: e.matmul(gp[:, u, :], lhsT=o2[:, q * 4 + u, :], rhs=o1[:, q * 4 + u, :], start=True, stop=True), reads=[o1, o2], writes=[gp])
                                tq = tb * 2 + q
                                P.op("act", lambda e, gp=gp, tq=tq: e.copy(out=GT[:, :, tq * 4:(tq + 1) * 4], in_=gp[:].rearrange("p t c -> p c t")), reads=[gp], writes=[GT])
                        units.append(f)
                    return units

                def A_units(hn_, g, wt):
                    units = []
                    for ci in range(GC):
                        def f(ci=ci):
                            c = g * GC + ci
                            ut = upf.get(cnt["u"]); cnt["u"] += 1
                            ap_ = aps.next(); ab = actr.next()
                            for kc in range(16):
                                P.op("pe", lambda e, kc=kc: e.matmul(ap_[:], lhsT=ut[:, kc, :], rhs=hn_[:, kc, :], start=(kc == 0), stop=(kc == 15)), reads=[ut, hn_], writes=[ap_])
                            P.op("act", lambda e: e.activation(out=ab[:], in_=ap_[:], func=AF.Gelu), reads=[ap_], writes=[ab])
                            P.op("dve", lambda e: e.tensor_tensor(out=wt[:, ci, :], in0=ab[:], in1=GT[:, c, :], op=ALU.mult), reads=[ab, GT], writes=[wt])
                        units.append(f)
                    return units

                def WV_units(g, wt):
                    units = []
                    for blk in range(4):
                        for sub in range(NSUB):
                            def f(blk=blk, sub=sub):
                                if sub == 0:
                                    WV_units.vt = vpf.get(cnt["v"]); cnt["v"] += 1
                                vt = WV_units.vt
                                op_ = ops.next()
                                for ci in range(GC):
                                    P.op("pe", lambda e, ci=ci: e.matmul(op_[:], lhsT=wt[:, ci, sub * 128:(sub + 1) * 128], rhs=vt[:, ci, :], start=(ci == 0), stop=(ci == GC - 1)), reads=[wt, vt], writes=[op_])
                                if g == 0:
                                    P.op("dve", lambda e: e.tensor_copy(out=pacc[sub][:, blk * 512:(blk + 1) * 512], in_=op_[:]), reads=[op_], writes=[pacc[sub]])
                                else:
                                    P.op("dve", lambda e: e.tensor_tensor(out=pacc[sub][:, blk * 512:(blk + 1) * 512], in0=pacc[sub][:, blk * 512:(blk + 1) * 512], in1=op_[:], op=ALU.add), reads=[op_, pacc[sub]], writes=[pacc[sub]])
                            units.append(f)
                    return units

                def interleave(xs_, ys_):
                    nx, ny = len(xs_), len(ys_)
                    ix = 0
                    for iy in range(ny):
                        tgt = (iy + 1) * nx // ny
                        while ix < tgt:
                            xs_[ix](); ix += 1
                        ys_[iy]()
                    while ix < nx:
                        xs_[ix](); ix += 1

                def final(ti):
                    for sub in range(NSUB):
                        t0 = ti * TP + sub * 128
                        ht = h4r.next()
                        P.dma("sp", lambda e, ht=ht, t0=t0: e.dma_start(out=ht[:], in_=hsc[t0:t0 + 128, :]), reads=[hsc], writes=[ht])
                        P.op("pool", lambda e, ht=ht, sub=sub: e.tensor_tensor(out=ht[:], in0=ht[:], in1=pacc[sub][:], op=ALU.add), reads=[ht, pacc[sub]], writes=[ht])
                        P.op("act", lambda e, ht=ht: e.activation(out=junkb[:], in_=ht[:], func=AF.Square, accum_out=ssq[:]), reads=[ht], writes=[junkb, ssq])
                        P.op("act", lambda e: e.activation(out=ssq[:], in_=ssq[:], func=AF.Sqrt, bias=epsb[:, 0:1], scale=1.0 / D), reads=[ssq, epsb], writes=[ssq])
                        P.op("dve", lambda e: e.reciprocal(out=ssq[:], in_=ssq[:]), reads=[ssq], writes=[ssq])
                        P.op("dve", lambda e, ht=ht: e.scalar_tensor_tensor(out=ht[:], in0=ht[:], scalar=ssq[:, 0:1], in1=fnw[:], op0=ALU.mult, op1=ALU.mult), reads=[ht, ssq, fnw], writes=[ht])
                        P.dma("sp", lambda e, ht=ht, t0=t0: e.dma_start(out=out[t0:t0 + 128, :], in_=ht[:]), reads=[ht], writes=[out])

                hn_, rt_ = tpf4.get(0)
                for f in GT_units(rt_):
                    f()
                wt_cur = wtr.next()
                for f in A_units(hn_, 0, wt_cur):
                    f()
                for ti in range(NTP):
                    for g in range(NG):
                        if g < NG - 1:
                            wt_nxt = wtr.next()
                            interleave(A_units(hn_, g + 1, wt_nxt), WV_units(g, wt_cur))
                            wt_cur = wt_nxt
                        else:
                            if ti + 1 < NTP:
                                hn_n, rt_n = tpf4.get(ti + 1)
                                interleave(GT_units(rt_n), WV_units(g, wt_cur))
                                final(ti)
                                wt_cur = wtr.next()
                                for f in A_units(hn_n, 0, wt_cur):
                                    f()
                                hn_, rt_ = hn_n, rt_n
                            else:
                                for f in WV_units(g, wt_cur):
                                    f()
                                final(ti)
                P.end_phase()

        P.wait_all("sp", outs)
        P.emit()
    return nc


def _pedge_const(S):
    pe = np.ones((4, 16), np.float32)
    for gi, w in enumerate((2, 4, 8, 16)):
        for j in range(8):
            for t, col in ((j, j), (S - 8 + j, 8 + j)):
                lo = max(t - w // 2, 0)
                hi = min(t + w // 2, S)
                pe[gi, col] = w / float(hi - lo)
    return pe.reshape(1, 64)


_NC_CACHE = {}


def kernel(**inputs):
    x = np.ascontiguousarray(np.asarray(inputs["x"], dtype=np.float32))
    B, S, _ = x.shape
    f = lambda k: np.ascontiguousarray(np.asarray(inputs[k], dtype=np.float32))
    shared = dict(
        mixer_norm_w=f("mixer_norm_w").reshape(1, D),
        w_in=f("w_in").reshape(D, IPW),
        conv_w=f("conv_w").reshape(5, XBC),
        conv_b=f("conv_b").reshape(1, XBC),
        dt_bias=f("dt_bias").reshape(1, 64),
        a_log=f("a_log").reshape(1, 64),
        d_skip=f("d_skip").reshape(1, H),
        ssd_norm_w=f("ssd_norm_w").reshape(1, D),
        w_ssd_branch=f("w_ssd_branch").reshape(D, D),
        w_pool_group=f("w_pool_group").reshape(4, 256, 256),
        pool_scale=f("pool_scale").reshape(1, PW),
        w_pool_branch=f("w_pool_branch").reshape(PW, D),
        w_out=f("w_out").reshape(D, D),
        ffn_norm_w=f("ffn_norm_w").reshape(1, D),
        w_query=f("w_query").reshape(D, D),
        sub_keys=f("sub_keys").reshape(16, 128, 128),
        expert_u=f("expert_u").reshape(NE, D),
        expert_v=f("expert_v").reshape(NE, D),
        final_norm_w=f("final_norm_w").reshape(1, D),
        pedge=_pedge_const(S),
    )
    if S not in _NC_CACHE:
        _NC_CACHE[S] = build(S)
    nc = _NC_CACHE[S]
    in_maps = [dict(shared, x=x[b]) for b in range(B)]
    res = run_bass_kernel_spmd(nc, in_maps, core_ids=list(range(B)))
    return np.stack([np.asarray(r["out"], dtype=np.float32) for r in res.results], axis=0)
```

```python
import numpy as np
import concourse.bass as bass
import concourse.mybir as mybir
from contextlib import ExitStack
from concourse.bass_utils import run_bass_kernel_spmd

F32 = mybir.dt.float32
BF16 = mybir.dt.bfloat16
I32 = mybir.dt.int32
U32 = mybir.dt.uint32
AF = mybir.ActivationFunctionType
ALU = mybir.AluOpType
AX = mybir.AxisListType


POOL_SYNC = True


class Buf:
    def __init__(self, name, t, space):
        self.name = name
        self.t = t
        self.space = space
        self.w = {}
        self.r = {}
        self.sem = None
        self.semcount = 0

    def __getitem__(self, idx):
        return self.t[idx]


class Prog:
    ENG = ("pe", "act", "dve", "pool", "sp")

    def __init__(self, nc, stack):
        self.nc = nc
        self.stack = stack
        self.streams = {e: [] for e in self.ENG}
        self.esem = {e: stack.enter_context(nc.semaphore("es_" + e)) for e in self.ENG}
        self.ecount = {e: 0 for e in self.ENG}
        self.signal = {e: set() for e in self.ENG}
        self.sigtotal = {e: 0 for e in self.ENG}
        self.sigval = {e: {} for e in self.ENG}
        self.waited = {e: {} for e in self.ENG}
        self.semobj = {}
        for e in self.ENG:
            self.semobj[("e", e)] = self.esem[e]
        self.nsem = 0
        self.nbuf = 0
        self.outer = stack
        self.allbufs = []
        self.sempool = []
        self.phase_bufs = []
        self._semcounts = {}

    def sb(self, name, shape, dtype):
        self.nbuf += 1
        name = "%s_%d" % (name, self.nbuf)
        t = self.stack.enter_context(self.nc.sbuf_tensor(name, list(shape), dtype))
        return self._reg(Buf(name, t, "sb"))

    def ps(self, name, shape, dtype):
        self.nbuf += 1
        name = "%s_%d" % (name, self.nbuf)
        t = self.stack.enter_context(self.nc.psum_tensor(name, list(shape), dtype))
        return self._reg(Buf(name, t, "ps"))

    def dram(self, name, shape, dtype, kind="Internal"):
        t = self.nc.dram_tensor(name, list(shape), dtype, kind=kind)
        return self._reg(Buf(name, t.ap(), "dram"))

    def view(self, buf, t):
        b = Buf(buf.name + "_v%d" % self.nbuf, t, buf.space)
        self.nbuf += 1
        return self._reg(b)

    def _reg(self, b):
        self.allbufs.append(b)
        return b

    def _bufsem(self, b):
        if b.sem is None:
            if self.sempool:
                b.sem, b.semcount = self.sempool.pop()
            else:
                b.sem = self.outer.enter_context(self.nc.semaphore("bs%d" % self.nsem))
                self.nsem += 1
            self.semobj[("b", id(b))] = b.sem
            self._semcounts[("b", id(b))] = (lambda b=b: b.semcount)
            b.key = ("b", id(b))
            self.phase_bufs.append(b)
        return b.key

    def end_phase(self, keep=()):
        self.barrier(skip=[b.key for b in keep if b.sem is not None])
        self.emit()
        kept = []
        for b in self.phase_bufs:
            if b in keep:
                kept.append(b)
                continue
            self.sempool.append((b.sem, b.semcount))
            del self.semobj[b.key]
            del self._semcounts[b.key]
            for e in self.ENG:
                self.waited[e].pop(b.key, None)
            b.sem = None
        self.phase_bufs = kept
        for b in self.allbufs:
            if b in keep:
                continue
            b.w = {}
            b.r = {}

    def _collect(self, eng, reads, writes, own_key, is_dma):
        need = {}

        def add(k, v):
            if v > need.get(k, 0):
                need[k] = v

        for b in reads:
            for k, v in b.w.items():
                if (not is_dma) and k == own_key and eng == "pe":
                    continue
                add(k, v)
        for b in writes:
            if b.space == "dram":
                continue
            for k, v in b.w.items():
                if k == own_key and not (POOL_SYNC and eng == "pool" and not is_dma):
                    continue
                add(k, v)
            for k, v in b.r.items():
                if (not is_dma) and k == own_key and not (POOL_SYNC and eng == "pool"):
                    continue
                add(k, v)
        wl = []
        cache = self.waited[eng]
        for k, v in need.items():
            if cache.get(k, 0) >= v:
                continue
            cache[k] = v
            wl.append((k, v))
            if k[0] == "e":
                self.signal[k[1]].add(v)
        return wl

    def op(self, eng, fn, reads=(), writes=()):
        key = ("e", eng)
        waits = self._collect(eng, reads, writes, key, False)
        self.ecount[eng] += 1
        val = self.ecount[eng]
        self.streams[eng].append((waits, fn, key, 1, val))
        for b in reads:
            b.r[key] = val
        for b in writes:
            b.w = {key: val}
            b.r = {}

    def dma(self, q, fn, reads=(), writes=(), sembuf=None):
        if sembuf is None:
            cands = [b for b in list(writes) + list(reads) if b.space == "sb"]
            sembuf = cands[0] if cands else (list(writes) + list(reads))[0]
        key = self._bufsem(sembuf)
        waits = self._collect(q, reads, writes, key, True)
        sembuf.semcount += 16
        val = sembuf.semcount
        self.streams[q].append((waits, fn, key, 16, None))
        for b in reads:
            b.r[key] = val
        for b in writes:
            if b.space == "dram":
                b.w = dict(b.w)
                b.w[key] = val
            else:
                b.w = {key: val}
                b.r = {}

    def wait_all(self, eng, bufs):
        need = {}
        for b in bufs:
            for k, v in list(b.w.items()) + list(b.r.items()):
                if v > need.get(k, 0):
                    need[k] = v
        for k, v in need.items():
            if k[0] == "e":
                self.signal[k[1]].add(v)
        self.streams[eng].append((list(need.items()), None, None, 0, None))

    def barrier(self, skip=()):
        need = [(("e", e), self.ecount[e]) for e in self.ENG if self.ecount[e] > 0]
        for k, sem in self.semobj.items():
            if k[0] == "b" and k not in skip:
                need.append((k, self._semcounts[k]()))
        for e in self.ENG:
            wl = []
            for k, v in need:
                if k == ("e", e) or v == 0:
                    continue
                if self.waited[e].get(k, 0) >= v:
                    continue
                self.waited[e][k] = v
                wl.append((k, v))
                if k[0] == "e":
                    self.signal[k[1]].add(v)
            self.streams[e].append((wl, None, None, 0, None))

    def emit(self):
        nc = self.nc
        handles = {"pe": "tensor", "act": "scalar", "dve": "vector", "pool": "gpsimd", "sp": "sync"}
        for e in self.ENG:
            run = self.sigtotal[e]
            for waits, fn, key, inc, idx in self.streams[e]:
                if idx is not None and idx in self.signal[e]:
                    run += 1
                    self.sigval[e][idx] = run
            self.sigtotal[e] = run
        with nc.Block() as block:
            for e in self.ENG:
                stream = self.streams[e]

                def body(eng, stream=stream, e=e):
                    for waits, fn, key, inc, idx in stream:
                        for k, v in waits:
                            if k[0] == "e":
                                eng.wait_ge(self.semobj[k], self.sigval[k[1]][v])
                            else:
                                eng.wait_ge(self.semobj[k], v)
                        if fn is not None:
                            ins = fn(eng)
                            if idx is None:
                                ins.then_inc(self.semobj[key], inc)
                            elif idx in self.signal[e]:
                                ins.then_inc(self.semobj[key], 1)

                getattr(block, handles[e])(body)
        self.streams = {e: [] for e in self.ENG}
        for e in self.ENG:
            self.signal[e] = set()
            self.sigval[e] = {}


D = 2048; H = 32; G = 8; R = 4; PD = 64; NS = 128; XBC = 4096; PW = 1024; IPW = 11328
O1 = 2048; O2 = 6144; O3 = 6208; O4 = 7232
NE = 16384; TOPK = 16
EPS = 1e-6

BLOCKS = ([(c, 512, "z") for c in range(0, 2048, 512)] + [(c, 512, "xbc") for c in range(O1, O2, 512)]
          + [(O2, 64, "dt")] + [(c, 512, "xp") for c in range(O3, O4, 512)]
          + [(c, 512, "gate") for c in range(O4, IPW, 512)])


class Ring:
    def __init__(self, bufs):
        self.bufs = bufs
        self.i = 0

    def next(self):
        b = self.bufs[self.i % len(self.bufs)]
        self.i += 1
        return b


class Prefetch:
    def __init__(self, thunks, pf):
        self.thunks = thunks
        self.pf = pf
        self.issued = 0
        self.res = {}

    def get(self, k):
        while self.issued < min(len(self.thunks), k + self.pf + 1):
            self.res[self.issued] = self.thunks[self.issued]()
            self.issued += 1
        return self.res.pop(k)


def build(S, debug=False, phases=(0, 1, 2, 3, 4, 5, 6, 7)):
    nc = bass.Bass("TRN2", target_bir_lowering=False)
    dbg = "ExternalOutput" if debug else "Internal"
    NCH = S // 128
    TT = 512
    NTT = S // TT
    with ExitStack() as st:
        P = Prog(nc, st)
        x = P.dram("x", [S, D], F32, "ExternalInput")
        mixer_norm_w = P.dram("mixer_norm_w", [1, D], F32, "ExternalInput")
        w_in = P.dram("w_in", [D, IPW], F32, "ExternalInput")
        conv_w = P.dram("conv_w", [5, XBC], F32, "ExternalInput")
        conv_b = P.dram("conv_b", [1, XBC], F32, "ExternalInput")
        dt_bias = P.dram("dt_bias", [1, 64], F32, "ExternalInput")
        a_log = P.dram("a_log", [1, 64], F32, "ExternalInput")
        d_skip = P.dram("d_skip", [1, H], F32, "ExternalInput")
        ssd_norm_w = P.dram("ssd_norm_w", [1, D], F32, "ExternalInput")
        w_ssd = P.dram("w_ssd_branch", [D, D], F32, "ExternalInput")
        w_pg = P.dram("w_pool_group", [4, 256, 256], F32, "ExternalInput")
        pool_scale = P.dram("pool_scale", [1, PW], F32, "ExternalInput")
        w_pb = P.dram("w_pool_branch", [PW, D], F32, "ExternalInput")
        w_out = P.dram("w_out", [D, D], F32, "ExternalInput")
        ffn_norm_w = P.dram("ffn_norm_w", [1, D], F32, "ExternalInput")
        w_query = P.dram("w_query", [D, D], F32, "ExternalInput")
        sub_keys = P.dram("sub_keys", [16, 128, 128], F32, "ExternalInput")
        expert_u = P.dram("expert_u", [NE, D], F32, "ExternalInput")
        expert_v = P.dram("expert_v", [NE, D], F32, "ExternalInput")
        final_norm_w = P.dram("final_norm_w", [1, D], F32, "ExternalInput")
        out = P.dram("out", [S, D], F32, "ExternalOutput")

        wblk_d = [P.dram("winbf%d" % i, [128, 16, w], BF16) for i, (c0, w, k) in enumerate(BLOCKS)]
        xbc_raw = P.dram("xbc_raw", [XBC, S + 4], BF16, dbg)
        xp_raw = P.dram("xp_raw", [PW, S + 16], BF16, dbg)
        gT = P.dram("gT", [XBC, S], BF16, dbg)
        sz = P.dram("sz", [S, D], BF16, dbg)
        dts = P.dram("dts", [S, 64], F32, dbg)
        xbc_c = P.dram("xbc_c", [XBC, S], F32, dbg)
        yf = P.dram("yf", [S, D], F32, dbg)
        ynT = P.dram("ynT", [D, S], BF16, dbg)
        hsc = P.dram("hsc", [S, D], F32, dbg)
        wssd_d = [P.dram("wssdbf%d" % i, [128, 16, 512], BF16) for i in range(4)]
        wpb_d = [P.dram("wpbbf%d" % i, [128, 8, 512], BF16) for i in range(4)]
        wout_d = [P.dram("woutbf%d" % i, [128, 16, 512], BF16) for i in range(4)]
        wq_d = [P.dram("wqbf%d" % i, [128, 16, 512], BF16) for i in range(4)]
        wpg_d = P.dram("wpgbf", [128, 8, 256], BF16)
        pedge = P.dram("pedge", [1, 64], F32, "ExternalInput")
        UT_d = P.dram("UT_d", [128, 128, 16, 128], BF16)
        Vbf_d = P.dram("Vbf_d", [NE, D], BF16)
        hnT_d = P.dram("hnT_d", [NCH, 128, 16, 128], BF16)
        rt_d = P.dram("rt_d", [NCH, 128, 3, 128], F32)
        outs = [out]
        if debug:
            idx_dbg = P.dram("idx_dbg", [S, 128], I32, dbg)
            gate_dbg = P.dram("gate_dbg", [S, 128], F32, dbg)
            outs += [xbc_raw, xp_raw, gT, sz, dts, xbc_c, yf, ynT, hsc, idx_dbg, gate_dbg]

        if 0 in phases:
            for i, (c0, w, k) in enumerate(BLOCKS):
                src = w_in[:, c0:c0 + w].rearrange("(kc p) c -> p kc c", p=128)
                for j in range(4):
                    P.dma("pool", lambda e, i=i, j=j, src=src: e.dma_start(out=wblk_d[i][:, 4 * j:4 * j + 4, :], in_=src[:, 4 * j:4 * j + 4, :]),
                          reads=[w_in], writes=[wblk_d[i]], sembuf=wblk_d[i])

            for wsrc, wdst, kcn in ((w_ssd, wssd_d, 16), (w_pb, wpb_d, 8), (w_out, wout_d, 16), (w_query, wq_d, 16)):
                for b in range(4):
                    src = wsrc[:, b * 512:(b + 1) * 512].rearrange("(kc p) c -> p kc c", p=128)
                    for j in range(kcn // 4):
                        P.dma("pool", lambda e, b=b, j=j, src=src, wdst=wdst: e.dma_start(out=wdst[b][:, 4 * j:4 * j + 4, :], in_=src[:, 4 * j:4 * j + 4, :]),
                              reads=[wsrc], writes=[wdst[b]], sembuf=wdst[b])
            P.dma("pool", lambda e: e.dma_start(out=wpg_d[:], in_=w_pg[:].rearrange("g (kc p) d -> p (g kc) d", p=128)), reads=[w_pg], writes=[wpg_d], sembuf=wpg_d)


        if 7 in phases:
            with ExitStack() as ph:
                P.stack = ph
                for c in range(128):
                    P.dma("pool", lambda e, c=c: e.dma_start(out=Vbf_d[c * 128:(c + 1) * 128, :], in_=expert_v[c * 128:(c + 1) * 128, :]), reads=[expert_v], writes=[Vbf_d], sembuf=Vbf_d)
                identf = P.sb("identf", [128, 128], F32)
                identb = P.sb("identb", [128, 128], BF16)
                P.op("pool", lambda e: e.memset(identf[:], 0.0), writes=[identf])
                P.op("pool", lambda e: e.affine_select(out=identf[:], in_=identf[:], pattern=[[-1, 128]], compare_op=ALU.not_equal, fill=1.0, base=0, channel_multiplier=1), reads=[identf], writes=[identf])
                P.op("dve", lambda e: e.tensor_copy(out=identb[:], in_=identf[:]), reads=[identf], writes=[identb])
                ufr = Ring([P.sb("uf%d" % i, [128, D], F32) for i in range(3)])
                ubr = Ring([P.sb("ub%d" % i, [128, D], BF16) for i in range(2)])
                uto = Ring([P.sb("uto%d" % i, [128, 16, 128], BF16) for i in range(2)])
                tpu = Ring([P.ps("tpu%d" % i, [128, 8, 128], BF16) for i in range(4)])

                def mk_uld(c):
                    def f():
                        uf = ufr.next()
                        P.dma("sp", lambda e: e.dma_start(out=uf[:], in_=expert_u[c * 128:(c + 1) * 128, :]), reads=[expert_u], writes=[uf])
                        return uf
                    return f
                upf0 = Prefetch([mk_uld(c) for c in range(128)], 2)
                for c in range(128):
                    uf = upf0.get(c); ub = ubr.next(); uo = uto.next()
                    if c % 2 == 0:
                        P.op("act", lambda e, uf=uf, ub=ub: e.copy(out=ub[:], in_=uf[:]), reads=[uf], writes=[ub])
                    else:
                        P.op("dve", lambda e, uf=uf, ub=ub: e.tensor_copy(out=ub[:], in_=uf[:]), reads=[uf], writes=[ub])
                    for half in range(2):
                        tp = tpu.next()
                        for j in range(8):
                            kc = half * 8 + j
                            P.op("pe", lambda e, tp=tp, j=j, kc=kc, ub=ub: e.transpose(out=tp[:, j, :], in_=ub[:, kc * 128:(kc + 1) * 128], identity=identb[:]), reads=[ub, identb], writes=[tp])
                        if half == 0:
                            P.op("act", lambda e, tp=tp, uo=uo: e.copy(out=uo[:, 0:8, :], in_=tp[:]), reads=[tp], writes=[uo])
                        else:
                            P.op("dve", lambda e, tp=tp, uo=uo: e.tensor_copy(out=uo[:, 8:16, :], in_=tp[:]), reads=[tp], writes=[uo])
                    P.dma("sp", lambda e, uo=uo, c=c: e.dma_start(out=UT_d[c], in_=uo[:]), reads=[uo], writes=[UT_d])
                P.end_phase(keep=[Vbf_d])

        if 1 in phases:
            with ExitStack() as ph:
                P.stack = ph
                wn1 = P.sb("wn1", [128, D], F32)
                dtb = P.sb("dtb", [128, 64], F32)
                epsb = P.sb("epsb", [128, 1], F32)
                identb = P.sb("identb", [128, 128], BF16)
                identf = P.sb("identf", [128, 128], F32)
                zeros = P.sb("zeros", [128, 32, 8], BF16)
                xin = Ring([P.sb("xin%d" % i, [128, D], F32) for i in range(2)])
                junk = P.sb("junk", [128, D], BF16)
                ssq = Ring([P.sb("ssq%d" % i, [128, 1], F32) for i in range(2)])
                nb = Ring([P.sb("nb%d" % i, [128, D], BF16) for i in range(2)])
                nT = Ring([P.sb("nT%d" % i, [128, 16, TT], BF16) for i in range(2)])
                wbk = Ring([P.sb("wbk%d" % i, [128, 16, 512], BF16) for i in range(3)])
                stg = Ring([P.sb("stg%d" % i, [128, 512], BF16) for i in range(4)])
                dtt = Ring([P.sb("dtt%d" % i, [128, 64], F32) for i in range(2)])
                dto = Ring([P.sb("dto%d" % i, [128, 64], F32) for i in range(2)])
                tps = Ring([P.ps("tps%d" % i, [128, 8, 128], BF16) for i in range(2)])
                acc = Ring([P.ps("acc%d" % i, [128, 512], F32) for i in range(4)])

                P.dma("sp", lambda e: e.dma_start(out=wn1[:], in_=mixer_norm_w[0:1, :].to_broadcast([128, D])), reads=[mixer_norm_w], writes=[wn1])
                P.dma("sp", lambda e: e.dma_start(out=dtb[:], in_=dt_bias[0:1, :].to_broadcast([128, 64])), reads=[dt_bias], writes=[dtb])
                P.op("pool", lambda e: e.memset(epsb[:], EPS), writes=[epsb])
                P.op("pool", lambda e: e.memset(zeros[:], 0.0), writes=[zeros])
                P.op("pool", lambda e: e.memset(identf[:], 0.0), writes=[identf])
                P.op("pool", lambda e: e.affine_select(out=identf[:], in_=identf[:], pattern=[[-1, 128]], compare_op=ALU.not_equal, fill=1.0, base=0, channel_multiplier=1), reads=[identf], writes=[identf])
                P.op("dve", lambda e: e.tensor_copy(out=identb[:], in_=identf[:]), reads=[identf], writes=[identb])
                xr3 = xbc_raw[:].rearrange("(cc p) t -> p cc t", p=128)
                P.dma("sp", lambda e: e.dma_start(out=xr3[:, :, 0:2], in_=zeros[:, :, 0:2]), reads=[zeros], writes=[xbc_raw])
                P.dma("sp", lambda e: e.dma_start(out=xr3[:, :, S + 2:S + 4], in_=zeros[:, :, 0:2]), reads=[zeros], writes=[xbc_raw])
                xp3 = xp_raw[:].rearrange("(cc p) t -> p cc t", p=128)
                P.dma("sp", lambda e: e.dma_start(out=xp3[:, :, 0:8], in_=zeros[:, 0:8, :]), reads=[zeros], writes=[xp_raw])
                P.dma("sp", lambda e: e.dma_start(out=xp3[:, :, S + 8:S + 16], in_=zeros[:, 0:8, :]), reads=[zeros], writes=[xp_raw])

                def mk_wload(bi):
                    def f():
                        wb = wbk.next()
                        w = BLOCKS[bi][1]
                        P.dma("sp", lambda e: e.dma_start(out=wb[:, :, 0:w], in_=wblk_d[bi][:]), reads=[wblk_d[bi]], writes=[wb])
                        return wb
                    return f
                wpf = Prefetch([mk_wload(bi) for ti in range(NTT) for bi in range(len(BLOCKS))], 2)
                for ti in range(NTT):
                    nTt = nT.next()
                    for sub in range(4):
                        t0 = ti * TT + sub * 128
                        xi = xin.next(); sq = ssq.next(); nbb = nb.next()
                        P.dma("sp", lambda e, xi=xi, t0=t0: e.dma_start(out=xi[:], in_=x[t0:t0 + 128, :]), reads=[x], writes=[xi])
                        P.op("act", lambda e, xi=xi, sq=sq: e.activation(out=junk[:], in_=xi[:], func=AF.Square, accum_out=sq[:]), reads=[xi], writes=[junk, sq])
                        P.op("act", lambda e, sq=sq: e.activation(out=sq[:], in_=sq[:], func=AF.Sqrt, bias=epsb[:, 0:1], scale=1.0 / D), reads=[sq, epsb], writes=[sq])
                        P.op("dve", lambda e, sq=sq: e.reciprocal(out=sq[:], in_=sq[:]), reads=[sq], writes=[sq])
                        P.op("dve", lambda e, xi=xi, sq=sq, nbb=nbb: e.scalar_tensor_tensor(out=nbb[:], in0=xi[:], scalar=sq[:, 0:1], in1=wn1[:], op0=ALU.mult, op1=ALU.mult), reads=[xi, sq, wn1], writes=[nbb])
                        for half in range(2):
                            tp = tps.next()
                            for j in range(8):
                                kc = half * 8 + j
                                P.op("pe", lambda e, tp=tp, j=j, kc=kc, nbb=nbb: e.transpose(out=tp[:, j, :], in_=nbb[:, kc * 128:(kc + 1) * 128], identity=identb[:]), reads=[nbb, identb], writes=[tp])
                            if half == 0:
                                P.op("act", lambda e, tp=tp, nTt=nTt, sub=sub: e.copy(out=nTt[:, 0:8, sub * 128:(sub + 1) * 128], in_=tp[:]), reads=[tp], writes=[nTt])
                            else:
                                P.op("dve", lambda e, tp=tp, nTt=nTt, sub=sub: e.tensor_copy(out=nTt[:, 8:16, sub * 128:(sub + 1) * 128], in_=tp[:]), reads=[tp], writes=[nTt])
                    for bi, (c0, w, kind) in enumerate(BLOCKS):
                        wb = wpf.get(ti * len(BLOCKS) + bi)
                        if kind in ("xbc", "xp", "gate"):
                            for cc in range(4):
                                a = acc.next()
                                for kc in range(16):
                                    P.op("pe", lambda e, a=a, wb=wb, kc=kc, cc=cc, nTt=nTt: e.matmul(a[:], lhsT=wb[:, kc, cc * 128:(cc + 1) * 128], rhs=nTt[:, kc, :], start=(kc == 0), stop=(kc == 15)), reads=[wb, nTt], writes=[a])
                                sg = stg.next()
                                ch0 = c0 + cc * 128
                                if kind == "gate":
                                    P.op("act", lambda e, a=a, sg=sg: e.activation(out=sg[:], in_=a[:], func=AF.Sigmoid), reads=[a], writes=[sg])
                                    dst = gT[ch0 - O4:ch0 - O4 + 128, ti * TT:(ti + 1) * TT]; dbuf = gT
                                elif kind == "xbc":
                                    P.op("dve", lambda e, a=a, sg=sg: e.tensor_copy(out=sg[:], in_=a[:]), reads=[a], writes=[sg])
                                    dst = xbc_raw[ch0 - O1:ch0 - O1 + 128, 2 + ti * TT:2 + (ti + 1) * TT]; dbuf = xbc_raw
                                else:
                                    P.op("dve", lambda e, a=a, sg=sg: e.tensor_copy(out=sg[:], in_=a[:]), reads=[a], writes=[sg])
                                    dst = xp_raw[ch0 - O3:ch0 - O3 + 128, 8 + ti * TT:8 + (ti + 1) * TT]; dbuf = xp_raw
                                P.dma("sp", lambda e, dst=dst, sg=sg: e.dma_start(out=dst, in_=sg[:]), reads=[sg], writes=[dbuf])
                        elif kind == "z":
                            for sub in range(4):
                                a = acc.next()
                                for kc in range(16):
                                    P.op("pe", lambda e, a=a, wb=wb, kc=kc, sub=sub, nTt=nTt: e.matmul(a[:], lhsT=nTt[:, kc, sub * 128:(sub + 1) * 128], rhs=wb[:, kc, :], start=(kc == 0), stop=(kc == 15)), reads=[wb, nTt], writes=[a])
                                sg = stg.next()
                                P.op("act", lambda e, a=a, sg=sg: e.activation(out=sg[:], in_=a[:], func=AF.Silu), reads=[a], writes=[sg])
                                t0 = ti * TT + sub * 128
                                dst = sz[t0:t0 + 128, c0:c0 + 512]
                                P.dma("sp", lambda e, dst=dst, sg=sg: e.dma_start(out=dst, in_=sg[:]), reads=[sg], writes=[sz])
                        else:
                            for sub in range(4):
                                a = acc.next()
                                for kc in range(16):
                                    P.op("pe", lambda e, a=a, wb=wb, kc=kc, sub=sub, nTt=nTt: e.matmul(a[:, 0:64], lhsT=nTt[:, kc, sub * 128:(sub + 1) * 128], rhs=wb[:, kc, 0:64], start=(kc == 0), stop=(kc == 15)), reads=[wb, nTt], writes=[a])
                                d1 = dtt.next(); d2 = dto.next()
                                P.op("dve", lambda e, a=a, d1=d1: e.tensor_tensor(out=d1[:], in0=a[:, 0:64], in1=dtb[:], op=ALU.add), reads=[a, dtb], writes=[d1])
                                P.op("act", lambda e, d1=d1: e.activation(out=d1[:], in_=d1[:], func=AF.Exp), reads=[d1], writes=[d1])
                                P.op("act", lambda e, d1=d1, d2=d2: e.activation(out=d2[:], in_=d1[:], func=AF.Ln, bias=1.0), reads=[d1], writes=[d2])
                                t0 = ti * TT + sub * 128
                                P.dma("sp", lambda e, d2=d2, t0=t0: e.dma_start(out=dts[t0:t0 + 128, :], in_=d2[:]), reads=[d2], writes=[dts])
                P.end_phase()


        if 2 in phases:
            with ExitStack() as ph:
                P.stack = ph
                cw = P.sb("cw", [128, 5, 32], F32)
                cb = P.sb("cb", [128, 32], F32)
                for k in range(5):
                    P.dma("sp", lambda e, k=k: e.dma_start(out=cw[:, k, :], in_=conv_w[k, :].rearrange("(cc p) -> p cc", p=128), allow_slow_non_contiguous=True), reads=[conv_w], writes=[cw])
                P.dma("sp", lambda e: e.dma_start(out=cb[:], in_=conv_b[0, :].rearrange("(cc p) -> p cc", p=128), allow_slow_non_contiguous=True), reads=[conv_b], writes=[cb])
                raw = Ring([P.sb("raw%d" % i, [128, S + 4], BF16) for i in range(2)])
                cac = Ring([P.sb("cac%d" % i, [128, S], F32) for i in range(2)])
                cout = Ring([P.sb("cout%d" % i, [128, S], F32) for i in range(2)])
                def mk_rload(cc):
                    def f():
                        rw = raw.next()
                        P.dma("sp", lambda e: e.dma_start(out=rw[:], in_=xbc_raw[cc * 128:(cc + 1) * 128, :]), reads=[xbc_raw], writes=[rw])
                        return rw
                    return f
                rpf = Prefetch([mk_rload(cc) for cc in range(32)], 1)
                for cc in range(32):
                    rw = rpf.get(cc); ac = cac.next(); co = cout.next()
                    eng = "dve"
                    P.op(eng, lambda e, rw=rw, ac=ac, cc=cc: e.tensor_scalar(out=ac[:], in0=rw[:, 0:S], scalar1=cw[:, 0, cc:cc + 1], scalar2=None, op0=ALU.mult), reads=[rw, cw], writes=[ac])
                    for k in range(1, 5):
                        P.op(eng, lambda e, rw=rw, ac=ac, cc=cc, k=k: e.scalar_tensor_tensor(out=ac[:], in0=rw[:, k:k + S], scalar=cw[:, k, cc:cc + 1], in1=ac[:], op0=ALU.mult, op1=ALU.add), reads=[rw, cw, ac], writes=[ac])
                    P.op("act", lambda e, ac=ac, co=co, cc=cc: e.activation(out=co[:], in_=ac[:], func=AF.Silu, bias=cb[:, cc:cc + 1]), reads=[ac, cb], writes=[co])
                    P.dma("sp", lambda e, co=co, cc=cc: e.dma_start(out=xbc_c[cc * 128:(cc + 1) * 128, :], in_=co[:]), reads=[co], writes=[xbc_c])
                P.end_phase()

        if 3 in phases:
            with ExitStack() as ph:
                P.stack = ph
                NEG = -60000.0
                identf = P.sb("identf", [128, 128], F32)
                ones = P.sb("ones", [128, 128], F32)
                Tle = P.sb("Tle", [128, 128], F32); Tge = P.sb("Tge", [128, 128], F32)
                Tgt = P.sb("Tgt", [128, 128], F32); Tlt = P.sb("Tlt", [128, 128], F32)
                NEGf = P.sb("NEGf", [128, 4, 128], F32); NEGb = P.sb("NEGb", [128, 4, 128], F32)
                negA = P.sb("negA", [128, 64], F32)
                dsk = P.sb("dsk", [128, H], F32)
                snw = P.sb("snw", [128, D], F32)
                epsb = P.sb("epsb", [128, 1], F32)
                P.op("pool", lambda e: e.memset(epsb[:], EPS), writes=[epsb])
                P.op("pool", lambda e: e.memset(ones[:], 1.0), writes=[ones])
                P.op("pool", lambda e: e.memset(identf[:], 0.0), writes=[identf])
                P.op("pool", lambda e: e.affine_select(out=identf[:], in_=identf[:], pattern=[[-1, 128]], compare_op=ALU.not_equal, fill=1.0, base=0, channel_multiplier=1), reads=[identf], writes=[identf])
                for T_, pat, cm, cop in ((Tle, 1, -1, ALU.is_ge), (Tge, -1, 1, ALU.is_ge), (Tgt, -1, 1, ALU.is_gt), (Tlt, 1, -1, ALU.is_gt)):
                    P.op("pool", lambda e, T_=T_: e.memset(T_[:], 1.0), writes=[T_])
                    P.op("pool", lambda e, T_=T_, pat=pat, cm=cm, cop=cop: e.affine_select(out=T_[:], in_=T_[:], pattern=[[pat, 128]], compare_op=cop, fill=0.0, base=0, channel_multiplier=cm), reads=[T_], writes=[T_])
                for T_, pat, cm in ((NEGf, -1, 1), (NEGb, 1, -1)):
                    P.op("pool", lambda e, T_=T_: e.memset(T_[:], NEG), writes=[T_])
                    P.op("pool", lambda e, T_=T_, pat=pat, cm=cm: e.affine_select(out=T_[:], in_=T_[:], pattern=[[0, 4], [pat, 128]], compare_op=ALU.is_gt, fill=0.0, base=0, channel_multiplier=cm), reads=[T_], writes=[T_])
                P.dma("sp", lambda e: e.dma_start(out=negA[:], in_=a_log[0:1, :].to_broadcast([128, 64])), reads=[a_log], writes=[negA])
                P.op("act", lambda e: e.activation(out=negA[:], in_=negA[:], func=AF.Exp), reads=[negA], writes=[negA])
                P.op("dve", lambda e: e.tensor_scalar(out=negA[:], in0=negA[:], scalar1=-1.0, scalar2=None, op0=ALU.mult), reads=[negA], writes=[negA])
                P.dma("sp", lambda e: e.dma_start(out=dsk[:], in_=d_skip[0:1, :].to_broadcast([128, H])), reads=[d_skip], writes=[dsk])
                P.dma("sp", lambda e: e.dma_start(out=snw[:], in_=ssd_norm_w[0:1, :].to_broadcast([128, D])), reads=[ssd_norm_w], writes=[snw])

                xcr = Ring([P.sb("xc%d" % i, [128, 32, 128], F32) for i in range(2)])
                dtr = Ring([P.sb("dtc%d" % i, [128, 64], F32) for i in range(2)])
                adt = P.sb("adt", [128, 32], F32); nadt = P.sb("nadt", [128, 32], F32)
                esc = P.sb("esc", [128, 96], F32)
                xs_tm = P.sb("xs_tm", [128, D], F32)
                B_tm = P.sb("B_tm", [128, 1024], F32)
                Xm = P.sb("Xm", [128, D], F32); Xd = P.sb("Xd", [128, D], F32)
                hT = [P.sb("hT%d" % g, [128, 256], F32) for g in range(G)]
                yacc_t = ph.enter_context(nc.sbuf_tensor("yacc", [128, D], F32))
                yacc = [P.view(Buf("yacc", yacc_t, "sb"), yacc_t[:, g * 256:(g + 1) * 256]) for g in range(G)]
                cbt = Ring([P.sb("cbt%d" % i, [128, 128], F32) for i in range(2)])
                Eb = Ring([P.sb("Eb%d" % i, [128, 4, 128], F32) for i in range(2)])
                Mb = Ring([P.sb("Mb%d" % i, [128, 4, 128], F32) for i in range(2)])
                tmpb = Ring([P.sb("tmpb%d" % i, [128, 256], F32) for i in range(2)])
                yfr = Ring([P.sb("yfc%d" % i, [128, D], F32) for i in range(2)])
                szr = Ring([P.sb("szc%d" % i, [128, D], BF16) for i in range(2)])
                ysq = P.sb("ysq", [128, D], F32)
                gss = P.sb("gss", [128, G], F32)
                ynb = P.sb("ynb", [128, 16, 128], BF16)
                tpp = Ring([P.ps("tpp%d" % i, [128, 512], F32) for i in range(2)])
                smp = Ring([P.ps("smp%d" % i, [128, 128], F32) for i in range(2)])
                Dps = Ring([P.ps("Dps%d" % i, [128, 4, 128], F32) for i in range(2)])
                ydo = P.ps("ydo", [128, 512], F32)
                stp = P.ps("stp", [128, 256], F32)

                for dname, dof, Tin, Tout, NEGd in (("f", 0, Tle, Tgt, NEGf), ("b", 32, Tge, Tlt, NEGb)):
                    for g in range(G):
                        P.op("pool", lambda e, g=g: e.memset(hT[g][:], 0.0), writes=[hT[g]])
                    order = list(range(NCH)) if dname == "f" else list(range(NCH - 1, -1, -1))

                    def mk_cload(c, dname=dname):
                        def f():
                            t0 = c * 128
                            xc = xcr.next(); dtc = dtr.next()
                            P.dma("sp", lambda e: e.dma_start(out=xc[:], in_=xbc_c[:, t0:t0 + 128].rearrange("(cc p) t -> p cc t", p=128)), reads=[xbc_c], writes=[xc])
                            P.dma("sp", lambda e: e.dma_start(out=dtc[:], in_=dts[t0:t0 + 128, :]), reads=[dts], writes=[dtc])
                            yfc = szc = None
                            if dname == "b":
                                yfc = yfr.next(); szc = szr.next()
                                P.dma("sp", lambda e: e.dma_start(out=yfc[:], in_=yf[t0:t0 + 128, :]), reads=[yf], writes=[yfc])
                                P.dma("sp", lambda e: e.dma_start(out=szc[:], in_=sz[t0:t0 + 128, :]), reads=[sz], writes=[szc])
                            return xc, dtc, yfc, szc
                        return f
                    cpf = Prefetch([mk_cload(c) for c in order], 1)
                    for ci, c in enumerate(order):
                        t0 = c * 128
                        xc, dtc, yfc, szc = cpf.get(ci)
                        P.op("dve", lambda e, dtc=dtc, dof=dof: e.tensor_tensor(out=adt[:], in0=dtc[:, dof:dof + 32], in1=negA[:, dof:dof + 32], op=ALU.mult), reads=[dtc, negA], writes=[adt])
                        P.op("dve", lambda e: e.tensor_scalar(out=nadt[:], in0=adt[:], scalar1=-1.0, scalar2=None, op0=ALU.mult), reads=[adt], writes=[nadt])
                        sp_ = smp.next()
                        P.op("pe", lambda e, sp_=sp_, Tin=Tin: e.matmul(sp_[:, 0:32], lhsT=Tin[:], rhs=adt[:], start=True, stop=True), reads=[Tin, adt], writes=[sp_])
                        P.op("pe", lambda e, sp_=sp_, Tout=Tout: e.matmul(sp_[:, 32:64], lhsT=Tout[:], rhs=adt[:], start=True, stop=True), reads=[Tout, adt], writes=[sp_])
                        P.op("pe", lambda e, sp_=sp_: e.matmul(sp_[:, 64:96], lhsT=ones[:], rhs=adt[:], start=True, stop=True), reads=[ones, adt], writes=[sp_])
                        P.op("act", lambda e, sp_=sp_: e.activation(out=esc[:], in_=sp_[:, 0:96], func=AF.Exp), reads=[sp_], writes=[esc])
                        for q in range(6):
                            tp = tpp.next()
                            for j in range(4):
                                cc = q * 4 + j
                                P.op("pe", lambda e, tp=tp, j=j, cc=cc, xc=xc: e.transpose(out=tp[:, j * 128:(j + 1) * 128], in_=xc[:, cc, :], identity=identf[:]), reads=[xc, identf], writes=[tp])
                            if q < 4:
                                P.op("act", lambda e, tp=tp, q=q: e.copy(out=xs_tm[:, q * 512:(q + 1) * 512], in_=tp[:]), reads=[tp], writes=[xs_tm])
                            else:
                                P.op("act", lambda e, tp=tp, q=q: e.copy(out=B_tm[:, (q - 4) * 512:(q - 3) * 512], in_=tp[:]), reads=[tp], writes=[B_tm])
                        xs3 = xs_tm[:].rearrange("p (h d) -> p h d", d=PD)
                        P.op("dve", lambda e, dtc=dtc, dof=dof, xs3=xs3: e.tensor_tensor(out=Xm[:].rearrange("p (h d) -> p h d", d=PD), in0=xs3, in1=dtc[:, dof:dof + 32].unsqueeze(2).to_broadcast([128, H, PD]), op=ALU.mult), reads=[xs_tm, dtc], writes=[Xm])
                        P.op("pool", lambda e: e.tensor_tensor(out=Xd[:].rearrange("p (h d) -> p h d", d=PD), in0=Xm[:].rearrange("p (h d) -> p h d", d=PD), in1=esc[:, 32:64].unsqueeze(2).to_broadcast([128, H, PD]), op=ALU.mult), reads=[Xm, esc], writes=[Xd])
                        for g in range(G):
                            BT = xc[:, 16 + g, :]; CT = xc[:, 24 + g, :]
                            cp = smp.next(); cb_ = cbt.next(); dp = Dps.next(); Et = Eb.next(); Mt = Mb.next(); tb = tmpb.next()
                            P.op("pe", lambda e, cp=cp, BT=BT, CT=CT: e.matmul(cp[:], lhsT=BT, rhs=CT, start=True, stop=True), reads=[xc], writes=[cp])
                            P.op("act", lambda e, cp=cp, cb_=cb_: e.copy(out=cb_[:], in_=cp[:]), reads=[cp], writes=[cb_])
                            P.op("pe", lambda e, dp=dp, NEGd=NEGd: e.matmul(dp[:], lhsT=identf[:], rhs=NEGd[:], start=True, stop=False, skip_group_check=True), reads=[identf, NEGd], writes=[dp])
                            for r in range(R):
                                h = g * R + r
                                P.op("pe", lambda e, dp=dp, r=r, h=h, Tin=Tin: e.matmul(dp[:, r, :], lhsT=adt[:, h:h + 1].to_broadcast([128, 128]), rhs=Tin[:], start=False, stop=False, skip_group_check=True), reads=[adt, Tin], writes=[dp])
                                P.op("pe", lambda e, dp=dp, r=r, h=h, Tin=Tin: e.matmul(dp[:, r, :], lhsT=Tin[:], rhs=nadt[:, h:h + 1].to_broadcast([128, 128]), start=False, stop=(r == R - 1), skip_group_check=True), reads=[nadt, Tin], writes=[dp])
                            P.op("act", lambda e, dp=dp, Et=Et: e.activation(out=Et[:], in_=dp[:], func=AF.Exp), reads=[dp], writes=[Et])
                            P.op("dve", lambda e, Et=Et, Mt=Mt, cb_=cb_: e.tensor_tensor(out=Mt[:], in0=Et[:], in1=cb_[:].unsqueeze(1).to_broadcast([128, R, 128]), op=ALU.mult), reads=[Et, cb_], writes=[Mt])
                            for r in range(R):
                                h = g * R + r
                                P.op("pe", lambda e, Mt=Mt, r=r, h=h: e.matmul(ydo[:, r * 64:(r + 1) * 64], lhsT=Mt[:, r, :], rhs=Xm[:, h * 64:(h + 1) * 64], start=True, stop=True), reads=[Mt, Xm], writes=[ydo])
                            P.op("pe", lambda e, CT=CT, g=g: e.matmul(ydo[:, 256:512], lhsT=CT, rhs=hT[g][:], start=True, stop=True), reads=[xc, hT[g]], writes=[ydo])
                            P.op("dve", lambda e, tb=tb, g=g: e.tensor_tensor(out=tb[:].rearrange("p (h d) -> p h d", d=PD), in0=ydo[:, 256:512].rearrange("p (h d) -> p h d", d=PD), in1=esc[:, g * R:(g + 1) * R].unsqueeze(2).to_broadcast([128, R, PD]), op=ALU.mult), reads=[ydo, esc], writes=[tb])
                            P.op("dve", lambda e, tb=tb, g=g: e.tensor_tensor(out=yacc[g][:], in0=ydo[:, 0:256], in1=tb[:], op=ALU.add), reads=[ydo, tb], writes=[yacc[g]])
                            P.op("pe", lambda e, g=g: e.matmul(stp[:], lhsT=B_tm[:, g * 128:(g + 1) * 128], rhs=Xd[:, g * 256:(g + 1) * 256], start=True, stop=True), reads=[B_tm, Xd], writes=[stp])
                            P.op("pool", lambda e, g=g: e.tensor_tensor(out=hT[g][:].rearrange("p (h d) -> p h d", d=PD), in0=hT[g][:].rearrange("p (h d) -> p h d", d=PD), in1=esc[:, 64 + g * R:64 + (g + 1) * R].unsqueeze(2).to_broadcast([128, R, PD]), op=ALU.mult), reads=[hT[g], esc], writes=[hT[g]])
                            P.op("dve", lambda e, g=g: e.tensor_tensor(out=hT[g][:], in0=hT[g][:], in1=stp[:], op=ALU.add), reads=[hT[g], stp], writes=[hT[g]])
                        if dname == "f":
                            P.dma("sp", lambda e, t0=t0: e.dma_start(out=yf[t0:t0 + 128, :], in_=yacc_t[:]), reads=yacc, writes=[yf], sembuf=yacc[0])
                        else:
                            ya = yacc_t
                            P.op("dve", lambda e, yfc=yfc: e.tensor_tensor(out=ya[:], in0=ya[:], in1=yfc[:], op=ALU.add), reads=yacc + [yfc], writes=yacc)
                            P.op("pool", lambda e, xs3=xs3: e.tensor_tensor(out=ysq[:].rearrange("p (h d) -> p h d", d=PD), in0=xs3, in1=dsk[:].unsqueeze(2).to_broadcast([128, H, PD]), op=ALU.mult), reads=[xs_tm, dsk], writes=[ysq])
                            P.op("dve", lambda e: e.tensor_tensor(out=ya[:], in0=ya[:], in1=ysq[:], op=ALU.add), reads=yacc + [ysq], writes=yacc)
                            P.op("dve", lambda e, szc=szc: e.tensor_tensor(out=ya[:], in0=ya[:], in1=szc[:], op=ALU.mult), reads=yacc + [szc], writes=yacc)
                            P.op("pool", lambda e: e.tensor_tensor(out=ysq[:], in0=ya[:], in1=ya[:], op=ALU.mult), reads=yacc, writes=[ysq])
                            P.op("dve", lambda e: e.tensor_reduce(out=gss[:], in_=ysq[:].rearrange("p (g d) -> p g d", d=256), axis=AX.X, op=ALU.add), reads=[ysq], writes=[gss])
                            P.op("act", lambda e: e.activation(out=gss[:], in_=gss[:], func=AF.Sqrt, bias=epsb[:, 0:1], scale=1.0 / 256), reads=[gss, epsb], writes=[gss])
                            P.op("dve", lambda e: e.reciprocal(out=gss[:], in_=gss[:]), reads=[gss], writes=[gss])
                            P.op("dve", lambda e: e.tensor_tensor(out=ya[:].rearrange("p (g d) -> p g d", d=256), in0=ya[:].rearrange("p (g d) -> p g d", d=256), in1=gss[:].unsqueeze(2).to_broadcast([128, G, 256]), op=ALU.mult), reads=yacc + [gss], writes=yacc)
                            P.op("pool", lambda e: e.tensor_tensor(out=ya[:], in0=ya[:], in1=snw[:], op=ALU.mult), reads=yacc + [snw], writes=yacc)
                            for q in range(4):
                                tp = tpp.next()
                                for j in range(4):
                                    kc = q * 4 + j
                                    P.op("pe", lambda e, tp=tp, j=j, kc=kc: e.transpose(out=tp[:, j * 128:(j + 1) * 128], in_=ya[:, kc * 128:(kc + 1) * 128], identity=identf[:]), reads=yacc + [identf], writes=[tp])
                                P.op("act", lambda e, tp=tp, q=q: e.copy(out=ynb[:, q * 4:(q + 1) * 4, :], in_=tp[:].rearrange("p (a b) -> p a b", b=128)), reads=[tp], writes=[ynb])
                            P.dma("sp", lambda e, t0=t0: e.dma_start(out=ynT[:, t0:t0 + 128].rearrange("(kc p) t -> p kc t", p=128), in_=ynb[:]), reads=[ynb], writes=[ynT])
                P.end_phase()


        if 4 in phases:
            with ExitStack() as ph:
                P.stack = ph
                TA = 512
                NTA = S // TA
                wpg = P.sb("wpg", [128, 8, 256], BF16)
                psc = P.sb("psc", [128, 8], F32)
                ped = P.sb("ped", [128, 4, 16], F32)
                P.dma("sp", lambda e: e.dma_start(out=wpg[:], in_=wpg_d[:]), reads=[wpg_d], writes=[wpg])
                P.dma("sp", lambda e: e.dma_start(out=psc[:], in_=pool_scale[0, :].rearrange("(cc p) -> p cc", p=128), allow_slow_non_contiguous=True), reads=[pool_scale], writes=[psc])
                P.dma("sp", lambda e: e.dma_start(out=ped[:].rearrange("p a b -> p (a b)"), in_=pedge[0:1, :].to_broadcast([128, 64])), reads=[pedge], writes=[ped])
                ynt_r = Ring([P.sb("ynt%d" % i, [128, 16, TA], BF16) for i in range(2)])
                xpt_r = Ring([P.sb("xpt%d" % i, [128, 8, TA + 16], BF16) for i in range(2)])
                lv = [P.sb("lv%d" % i, [128, 2, TA + 16], F32) for i in range(2)]
                pooledT = P.sb("pooledT", [128, 8, TA], BF16)
                p2T = P.sb("p2T", [128, 8, TA], BF16)
                gtr = Ring([P.sb("gt%d" % i, [128, 8, TA], BF16) for i in range(2)])
                merged = P.sb("merged", [128, 16, TA], BF16)
                t1r = Ring([P.sb("t1_%d" % i, [128, TA], F32) for i in range(2)])
                t2r = Ring([P.sb("t2_%d" % i, [128, TA], F32) for i in range(2)])
                wbk = Ring([P.sb("wb3_%d" % i, [128, 16, 512], BF16) for i in range(3)])
                wpbk = Ring([P.sb("wpb%d" % i, [128, 8, 512], BF16) for i in range(2)])
                xsr = Ring([P.sb("xs3_%d" % i, [128, D], F32) for i in range(2)])
                acc = Ring([P.ps("acc3_%d" % i, [128, 512], F32) for i in range(6)])
                WINS = (2, 4, 8, 16)
                def mk_tload(ti):
                    def f():
                        t0 = ti * TA
                        ynt = ynt_r.next(); xpt = xpt_r.next()
                        P.dma("sp", lambda e: e.dma_start(out=ynt[:], in_=ynT[:, t0:t0 + TA].rearrange("(kc p) t -> p kc t", p=128)), reads=[ynT], writes=[ynt])
                        P.dma("sp", lambda e: e.dma_start(out=xpt[:], in_=xp_raw[:, t0:t0 + TA + 16].rearrange("(cc p) t -> p cc t", p=128)), reads=[xp_raw], writes=[xpt])
                        return ynt, xpt
                    return f

                def mk_mload(ti, b):
                    def f():
                        t0 = ti * TA
                        wb = wbk.next(); wp = wpbk.next(); gt = gtr.next()
                        P.dma("sp", lambda e: e.dma_start(out=wb[:], in_=wssd_d[b][:]), reads=[wssd_d[b]], writes=[wb])
                        P.dma("sp", lambda e: e.dma_start(out=wp[:], in_=wpb_d[b][:]), reads=[wpb_d[b]], writes=[wp])
                        P.dma("sp", lambda e: e.dma_start(out=gt[:, 0:4, :], in_=gT[b * 512:(b + 1) * 512, t0:t0 + TA].rearrange("(cc p) t -> p cc t", p=128)), reads=[gT], writes=[gt])
                        P.dma("sp", lambda e: e.dma_start(out=gt[:, 4:8, :], in_=gT[D + b * 512:D + (b + 1) * 512, t0:t0 + TA].rearrange("(cc p) t -> p cc t", p=128)), reads=[gT], writes=[gt])
                        return wb, wp, gt
                    return f

                def mk_oload(b):
                    def f():
                        wb = wbk.next()
                        P.dma("sp", lambda e: e.dma_start(out=wb[:], in_=wout_d[b][:]), reads=[wout_d[b]], writes=[wb])
                        return wb
                    return f
                tpf = Prefetch([mk_tload(ti) for ti in range(NTA)], 1)
                wl = []
                for ti in range(NTA):
                    wl += [mk_mload(ti, b) for b in range(4)]
                    for pair in range(TA // 256):
                        wl += [mk_oload(b) for b in range(4)]
                wpf3 = Prefetch(wl, 1)
                wk3 = 0
                for ti in range(NTA):
                    t0 = ti * TA
                    ynt, xpt = tpf.get(ti)
                    for gi, w in enumerate(WINS):
                        src = xpt[:, 2 * gi:2 * gi + 2, :]
                        xpt_ = xpt
                        L = TA + 16
                        step = 1
                        cur = None
                        li = 0
                        while step < w:
                            dst = lv[li % 2]
                            a_in = src if cur is None else cur[:, :, :]
                            rd = [xpt] if cur is None else [cur]
                            P.op("pool", lambda e, dst=dst, a_in=a_in, L=L, step=step: e.tensor_tensor(out=dst[:, :, 0:L - step], in0=a_in[:, :, 0:L - step], in1=a_in[:, :, step:L], op=ALU.add), reads=rd, writes=[dst])
                            cur = dst; L -= step; step *= 2; li += 1
                        off = 8 - w // 2
                        if ti == 0:
                            P.op("pool", lambda e, cur=cur, off=off, gi=gi: e.tensor_tensor(out=cur[:, :, off:off + 8], in0=cur[:, :, off:off + 8], in1=ped[:, gi, 0:8].unsqueeze(1).to_broadcast([128, 2, 8]), op=ALU.mult), reads=[cur, ped], writes=[cur])
                        if ti == NTA - 1:
                            P.op("pool", lambda e, cur=cur, off=off, gi=gi: e.tensor_tensor(out=cur[:, :, off + TA - 8:off + TA], in0=cur[:, :, off + TA - 8:off + TA], in1=ped[:, gi, 8:16].unsqueeze(1).to_broadcast([128, 2, 8]), op=ALU.mult), reads=[cur, ped], writes=[cur])
                        P.op("dve", lambda e, cur=cur, off=off, gi=gi, w=w, xpt_=xpt_: e.scalar_tensor_tensor(out=pooledT[:, 2 * gi:2 * gi + 2, :], in0=cur[:, :, off:off + TA], scalar=1.0 / w, in1=xpt_[:, 2 * gi:2 * gi + 2, 8:8 + TA], op0=ALU.mult, op1=ALU.subtract), reads=[cur, xpt_], writes=[pooledT])
                    for gi in range(4):
                        for dc in range(2):
                            a = acc.next()
                            for kc in range(2):
                                P.op("pe", lambda e, a=a, gi=gi, dc=dc, kc=kc: e.matmul(a[:, 0:TA], lhsT=wpg[:, gi * 2 + kc, dc * 128:(dc + 1) * 128], rhs=pooledT[:, gi * 2 + kc, :], start=(kc == 0), stop=(kc == 1)), reads=[wpg, pooledT], writes=[a])
                            P.op("act", lambda e, a=a, gi=gi, dc=dc: e.activation(out=p2T[:, gi * 2 + dc, :], in_=a[:, 0:TA], func=AF.Copy, scale=psc[:, gi * 2 + dc:gi * 2 + dc + 1]), reads=[a, psc], writes=[p2T])
                    for b in range(4):
                        wb, wp, gt = wpf3.get(wk3); wk3 += 1
                        for cc in range(4):
                            dch = b * 4 + cc
                            a1 = acc.next(); a2 = acc.next()
                            for kc in range(16):
                                P.op("pe", lambda e, a1=a1, wb=wb, kc=kc, cc=cc, ynt=ynt: e.matmul(a1[:, 0:TA], lhsT=wb[:, kc, cc * 128:(cc + 1) * 128], rhs=ynt[:, kc, :], start=(kc == 0), stop=(kc == 15)), reads=[wb, ynt], writes=[a1])
                            for kc in range(8):
                                P.op("pe", lambda e, a2=a2, wp=wp, kc=kc, cc=cc: e.matmul(a2[:, 0:TA], lhsT=wp[:, kc, cc * 128:(cc + 1) * 128], rhs=p2T[:, kc, :], start=(kc == 0), stop=(kc == 7)), reads=[wp, p2T], writes=[a2])
                            t1 = t1r.next(); t2 = t2r.next()
                            P.op("dve", lambda e, a1=a1, t1=t1, gt=gt, cc=cc: e.tensor_tensor(out=t1[:], in0=a1[:, 0:TA], in1=gt[:, cc, :], op=ALU.mult), reads=[a1, gt], writes=[t1])
                            P.op("dve", lambda e, a2=a2, t2=t2, gt=gt, cc=cc: e.tensor_tensor(out=t2[:], in0=a2[:, 0:TA], in1=gt[:, 4 + cc, :], op=ALU.mult), reads=[a2, gt], writes=[t2])
                            P.op("pool", lambda e, t1=t1, t2=t2, dch=dch: e.tensor_tensor(out=merged[:, dch, :], in0=t1[:], in1=t2[:], op=ALU.add), reads=[t1, t2], writes=[merged])
                    for pair in range(TA // 256):
                        subs = (2 * pair, 2 * pair + 1)
                        xt = {}
                        for sub in subs:
                            xt[sub] = xsr.next()
                            P.dma("sp", lambda e, xs_=xt[sub], sub=sub, t0=t0: e.dma_start(out=xs_[:], in_=x[t0 + sub * 128:t0 + (sub + 1) * 128, :]), reads=[x], writes=[xt[sub]])
                        for b in range(4):
                            wb = wpf3.get(wk3); wk3 += 1
                            for sub in subs:
                                a = acc.next()
                                for kc in range(16):
                                    P.op("pe", lambda e, a=a, wb=wb, kc=kc, sub=sub: e.matmul(a[:], lhsT=merged[:, kc, sub * 128:(sub + 1) * 128], rhs=wb[:, kc, :], start=(kc == 0), stop=(kc == 15)), reads=[wb, merged], writes=[a])
                                P.op("dve", lambda e, a=a, xs_=xt[sub], b=b: e.tensor_tensor(out=xs_[:, b * 512:(b + 1) * 512], in0=a[:], in1=xs_[:, b * 512:(b + 1) * 512], op=ALU.add), reads=[a, xt[sub]], writes=[xt[sub]])
                        for sub in subs:
                            P.dma("sp", lambda e, xs_=xt[sub], sub=sub, t0=t0: e.dma_start(out=hsc[t0 + sub * 128:t0 + (sub + 1) * 128, :], in_=xs_[:]), reads=[xt[sub]], writes=[hsc])
                P.end_phase()


        if 5 in phases:
            with ExitStack() as ph:
                P.stack = ph
                wq = P.sb("wq", [128, 4, 16, 512], BF16)
                qtm = P.sb("qtm", [128, D], F32)
                keysN = Buf("keysN", qtm.t[:].rearrange("p (h d) -> p h d", d=128), "sb")
                keysT = P.sb("keysT", [128, 16, 128], F32)
                fw = P.sb("fw", [128, D], F32)
                epsb = P.sb("epsb", [128, 1], F32)
                identf = P.sb("identf", [128, 128], F32)
                identb = P.sb("identb", [128, 128], BF16)
                iot_i = P.sb("iot_i", [128, 16], I32)
                iot = P.sb("iot", [128, 16], F32)
                for b in range(4):
                    P.dma("sp", lambda e, b=b: e.dma_start(out=wq[:, b, :, :], in_=wq_d[b][:]), reads=[wq_d[b]], writes=[wq])
                P.dma("sp", lambda e: e.dma_start(out=keysN[:], in_=sub_keys[:].rearrange("h n d -> n h d")), reads=[sub_keys], writes=[qtm])
                P.dma("sp", lambda e: e.dma_start(out=fw[:], in_=ffn_norm_w[0:1, :].to_broadcast([128, D])), reads=[ffn_norm_w], writes=[fw])
                P.op("pool", lambda e: e.memset(epsb[:], EPS), writes=[epsb])
                P.op("pool", lambda e: e.memset(identf[:], 0.0), writes=[identf])
                P.op("pool", lambda e: e.affine_select(out=identf[:], in_=identf[:], pattern=[[-1, 128]], compare_op=ALU.not_equal, fill=1.0, base=0, channel_multiplier=1), reads=[identf], writes=[identf])
                P.op("dve", lambda e: e.tensor_copy(out=identb[:], in_=identf[:]), reads=[identf], writes=[identb])
                P.op("pool", lambda e: e.iota(iot_i[:], pattern=[[1, 16]], base=0, channel_multiplier=0), writes=[iot_i])
                P.op("dve", lambda e: e.tensor_copy(out=iot[:], in_=iot_i[:]), reads=[iot_i], writes=[iot])
                tpq = Ring([P.ps("tpq%d" % i, [128, 4, 128], F32) for i in range(2)])
                tpb = Ring([P.ps("tpb%d" % i, [128, 8, 128], BF16) for i in range(2)])
                accq = Ring([P.ps("accq%d" % i, [128, 512], F32) for i in range(2)])
                for q4 in range(4):
                    tp = tpq.next()
                    for j in range(4):
                        hh = q4 * 4 + j
                        P.op("pe", lambda e, tp=tp, j=j, hh=hh: e.transpose(out=tp[:, j, :], in_=keysN[:, hh, :], identity=identf[:]), reads=[qtm, identf], writes=[tp])
                    P.op("act", lambda e, tp=tp, q4=q4: e.copy(out=keysT[:, q4 * 4:(q4 + 1) * 4, :], in_=tp[:]), reads=[tp], writes=[keysT])

                hr = Ring([P.sb("h4_%d" % i, [128, D], F32) for i in range(2)])
                hn = P.sb("hn", [128, D], F32)
                hnb = P.sb("hnb", [128, D], BF16)
                junkf = P.sb("junkf", [128, D], F32)
                ssq = P.sb("ssq4", [128, 1], F32)
                hnT = P.sb("hnT", [128, 16, 128], BF16)
                qT = P.sb("qT", [128, 16, 128], F32)
                scr = [P.sb("sc%d" % i, [128, 16, 128], F32) for i in range(2)]
                wk = P.sb("wk", [128, 128], F32)
                v1 = P.sb("v1", [128, 16, 16], F32)
                i1 = P.sb("i1", [128, 16, 16], U32)
                i1f = P.sb("i1f", [128, 16, 16], F32)
                cand = P.sb("cand", [128, 8, 256], F32)
                cwk = P.sb("cwk", [128, 256], F32)
                tv = P.sb("tv", [128, 8, 16], F32)
                pos = P.sb("pos", [128, 8, 16], U32)
                posf = P.sb("posf", [128, 8, 16], F32)
                r1f = P.sb("r1f", [128, 8, 16], F32)
                r1u = P.sb("r1u", [128, 8, 16], U32)
                r2u = P.sb("r2u", [128, 8, 16], U32)
                r2f = P.sb("r2f", [128, 8, 16], F32)
                eqb = Buf("eqb", cand.t[:].rearrange("p h (a b) -> p h a b", b=16), "sb")
                sel1 = P.sb("sel1", [128, 8, 16], F32)
                sel2 = P.sb("sel2", [128, 8, 16], F32)
                sel1r = P.sb("sel1r", [128, 8, 16], F32)
                rtr = Ring([P.sb("rt%d" % i, [128, 3, 128], F32) for i in range(2)])
                idx = P.sb("idx", [128, 128], I32)
                gexp = P.sb("gexp", [128, 8, 16], F32)
                gsum = P.sb("gsum", [128, 8], F32)
                gate = P.sb("gate", [128, 8, 16], F32)

                def front(c):
                    t0 = c * 128
                    ht = hr.next()
                    P.dma("sp", lambda e, ht=ht, t0=t0: e.dma_start(out=ht[:], in_=hsc[t0:t0 + 128, :]), reads=[hsc], writes=[ht])
                    P.op("act", lambda e, ht=ht: e.activation(out=junkf[:], in_=ht[:], func=AF.Square, accum_out=ssq[:]), reads=[ht], writes=[junkf, ssq])
                    P.op("act", lambda e: e.activation(out=ssq[:], in_=ssq[:], func=AF.Sqrt, bias=epsb[:, 0:1], scale=1.0 / D), reads=[ssq, epsb], writes=[ssq])
                    P.op("dve", lambda e: e.reciprocal(out=ssq[:], in_=ssq[:]), reads=[ssq], writes=[ssq])
                    P.op("dve", lambda e, ht=ht: e.scalar_tensor_tensor(out=hn[:], in0=ht[:], scalar=ssq[:, 0:1], in1=fw[:], op0=ALU.mult, op1=ALU.mult), reads=[ht, ssq, fw], writes=[hn])
                    P.op("act", lambda e: e.copy(out=hnb[:], in_=hn[:]), reads=[hn], writes=[hnb])
                    for half in range(2):
                        tp = tpb.next()
                        for j in range(8):
                            kc = half * 8 + j
                            P.op("pe", lambda e, tp=tp, j=j, kc=kc: e.transpose(out=tp[:, j, :], in_=hnb[:, kc * 128:(kc + 1) * 128], identity=identb[:]), reads=[hnb, identb], writes=[tp])
                        P.op("act", lambda e, tp=tp, half=half: e.copy(out=hnT[:, half * 8:(half + 1) * 8, :], in_=tp[:]), reads=[tp], writes=[hnT])
                    for b in range(4):
                        a = accq.next()
                        for kc in range(16):
                            P.op("pe", lambda e, a=a, b=b, kc=kc: e.matmul(a[:], lhsT=hnT[:, kc, :], rhs=wq[:, b, kc, :], start=(kc == 0), stop=(kc == 15)), reads=[hnT, wq], writes=[a])
                        P.op("act", lambda e, a=a, b=b: e.copy(out=qtm[:, b * 512:(b + 1) * 512], in_=a[:]), reads=[a], writes=[qtm])
                    for q4 in range(4):
                        tp = tpq.next()
                        for j in range(4):
                            hh = q4 * 4 + j
                            P.op("pe", lambda e, tp=tp, j=j, hh=hh: e.transpose(out=tp[:, j, :], in_=qtm[:, hh * 128:(hh + 1) * 128], identity=identf[:]), reads=[qtm, identf], writes=[tp])
                        P.op("act", lambda e, tp=tp, q4=q4: e.copy(out=qT[:, q4 * 4:(q4 + 1) * 4, :], in_=tp[:]), reads=[tp], writes=[qT])
                    for q4 in range(4):
                        tp = tpq.next()
                        for j in range(4):
                            hh = q4 * 4 + j
                            P.op("pe", lambda e, tp=tp, j=j, hh=hh: e.matmul(tp[:, j, :], lhsT=qT[:, hh, :], rhs=keysT[:, hh, :], start=True, stop=True), reads=[qT, keysT], writes=[tp])
                        P.op("act", lambda e, tp=tp, q4=q4: e.copy(out=scr[c % 2][:, q4 * 4:(q4 + 1) * 4, :], in_=tp[:]), reads=[tp], writes=[scr[c % 2]])
                    P.dma("sp", lambda e, c=c: e.dma_start(out=hnT_d[c], in_=hnT[:]), reads=[hnT], writes=[hnT_d])

                def topk(c):
                    t0 = c * 128
                    for hh in range(16):
                        P.op("dve", lambda e, hh=hh: e.max(out=v1[:, hh, 0:8], in_=scr[c % 2][:, hh, :]), reads=[scr[c % 2]], writes=[v1])
                        P.op("dve", lambda e, hh=hh: e.max_index(out=i1[:, hh, 0:8], in_max=v1[:, hh, 0:8], in_values=scr[c % 2][:, hh, :]), reads=[scr[c % 2], v1], writes=[i1])
                        P.op("dve", lambda e, hh=hh: e.match_replace(out=wk[:], in_to_replace=v1[:, hh, 0:8], in_values=scr[c % 2][:, hh, :], imm_value=-1e30), reads=[scr[c % 2], v1], writes=[wk])
                        P.op("dve", lambda e, hh=hh: e.max(out=v1[:, hh, 8:16], in_=wk[:]), reads=[wk], writes=[v1])
                        P.op("dve", lambda e, hh=hh: e.max_index(out=i1[:, hh, 8:16], in_max=v1[:, hh, 8:16], in_values=wk[:]), reads=[wk, v1], writes=[i1])
                    v4 = v1[:].rearrange("p (h two) r -> p h two r", two=2)
                    P.op("dve", lambda e, v4=v4: e.tensor_tensor(out=cand[:].rearrange("p h (a b) -> p h a b", b=16), in0=v4[:, :, 0, :].unsqueeze(3).to_broadcast([128, 8, 16, 16]), in1=v4[:, :, 1, :].unsqueeze(2).to_broadcast([128, 8, 16, 16]), op=ALU.add), reads=[v1], writes=[cand])
                    for h in range(8):
                        P.op("dve", lambda e, h=h: e.max(out=tv[:, h, 0:8], in_=cand[:, h, :]), reads=[cand], writes=[tv])
                        P.op("dve", lambda e, h=h: e.max_index(out=pos[:, h, 0:8], in_max=tv[:, h, 0:8], in_values=cand[:, h, :]), reads=[cand, tv], writes=[pos])
                        P.op("dve", lambda e, h=h: e.match_replace(out=cwk[:], in_to_replace=tv[:, h, 0:8], in_values=cand[:, h, :], imm_value=-1e30), reads=[cand, tv], writes=[cwk])
                        P.op("dve", lambda e, h=h: e.max(out=tv[:, h, 8:16], in_=cwk[:]), reads=[cwk], writes=[tv])
                        P.op("dve", lambda e, h=h: e.max_index(out=pos[:, h, 8:16], in_max=tv[:, h, 8:16], in_values=cwk[:]), reads=[cwk, tv], writes=[pos])
                    P.op("dve", lambda e: e.tensor_tensor(out=gexp[:], in0=tv[:], in1=tv[:, :, 0:1].to_broadcast([128, 8, 16]), op=ALU.subtract), reads=[tv], writes=[gexp])
                    P.op("act", lambda e: e.activation(out=gexp[:], in_=gexp[:], func=AF.Exp), reads=[gexp], writes=[gexp])
                    P.op("dve", lambda e: e.tensor_reduce(out=gsum[:], in_=gexp[:], axis=AX.X, op=ALU.add), reads=[gexp], writes=[gsum])
                    P.op("dve", lambda e: e.reciprocal(out=gsum[:], in_=gsum[:]), reads=[gsum], writes=[gsum])
                    P.op("dve", lambda e: e.tensor_tensor(out=gate[:], in0=gexp[:], in1=gsum[:].unsqueeze(2).to_broadcast([128, 8, 16]), op=ALU.mult), reads=[gexp, gsum], writes=[gate])
                    P.op("dve", lambda e: e.tensor_copy(out=posf[:], in_=pos[:]), reads=[pos], writes=[posf])
                    P.op("dve", lambda e: e.tensor_copy(out=i1f[:], in_=i1[:]), reads=[i1], writes=[i1f])
                    P.op("dve", lambda e: e.tensor_single_scalar(out=r1u[:], in_=pos[:], scalar=4, op=ALU.logical_shift_right), reads=[pos], writes=[r1u])
                    P.op("dve", lambda e: e.tensor_single_scalar(out=r2u[:], in_=pos[:], scalar=15, op=ALU.bitwise_and), reads=[pos], writes=[r2u])
                    P.op("dve", lambda e: e.tensor_copy(out=r1f[:], in_=r1u[:]), reads=[r1u], writes=[r1f])
                    P.op("dve", lambda e: e.tensor_copy(out=r2f[:], in_=r2u[:]), reads=[r2u], writes=[r2f])
                    i4 = i1f[:].rearrange("p (h two) r -> p h two r", two=2)
                    iot4 = iot[:].unsqueeze(1).unsqueeze(1).to_broadcast([128, 8, 16, 16])
                    for rf, two, sel in ((r1f, 0, sel1), (r2f, 1, sel2)):
                        P.op("dve", lambda e, rf=rf: e.tensor_tensor(out=eqb[:], in0=rf[:].unsqueeze(3).to_broadcast([128, 8, 16, 16]), in1=iot4, op=ALU.is_equal), reads=[rf, iot], writes=[cand])
                        P.op("dve", lambda e, two=two, i4=i4: e.tensor_tensor(out=eqb[:], in0=eqb[:], in1=i4[:, :, two, :].unsqueeze(2).to_broadcast([128, 8, 16, 16]), op=ALU.mult), reads=[cand, i1f], writes=[cand])
                        P.op("dve", lambda e, sel=sel: e.tensor_reduce(out=sel[:], in_=eqb[:], axis=AX.X, op=ALU.add), reads=[cand], writes=[sel])
                    P.op("dve", lambda e: e.tensor_copy(out=sel1r[:], in_=sel1[:]), reads=[sel1], writes=[sel1r])
                    if debug:
                        P.op("dve", lambda e: e.scalar_tensor_tensor(out=sel1[:], in0=sel1[:], scalar=128.0, in1=sel2[:], op0=ALU.mult, op1=ALU.add), reads=[sel1, sel2], writes=[sel1])
                        P.op("dve", lambda e: e.tensor_copy(out=idx[:], in_=sel1[:].rearrange("p h k -> p (h k)")), reads=[sel1], writes=[idx])
                    tp = tpq.next()
                    P.op("pe", lambda e, tp=tp: e.transpose(out=tp[:, 0, :], in_=sel1r[:].rearrange("p h k -> p (h k)"), identity=identf[:]), reads=[sel1r, identf], writes=[tp])
                    P.op("pe", lambda e, tp=tp: e.transpose(out=tp[:, 1, :], in_=sel2[:].rearrange("p h k -> p (h k)"), identity=identf[:]), reads=[sel2, identf], writes=[tp])
                    P.op("pe", lambda e, tp=tp: e.transpose(out=tp[:, 2, :], in_=gate[:].rearrange("p h k -> p (h k)"), identity=identf[:]), reads=[gate, identf], writes=[tp])
                    rt = rtr.next()
                    P.op("act", lambda e, tp=tp, rt=rt: e.copy(out=rt[:], in_=tp[:, 0:3, :]), reads=[tp], writes=[rt])
                    P.dma("sp", lambda e, rt=rt, c=c: e.dma_start(out=rt_d[c], in_=rt[:]), reads=[rt], writes=[rt_d])
                    if debug:
                        P.dma("sp", lambda e, t0=t0: e.dma_start(out=idx_dbg[t0:t0 + 128, :], in_=idx[:]), reads=[idx], writes=[idx_dbg])
                        P.dma("sp", lambda e, t0=t0: e.dma_start(out=gate_dbg[t0:t0 + 128, :], in_=gate[:].rearrange("p h k -> p (h k)")), reads=[gate], writes=[gate_dbg])

                front(0)
                for c in range(NCH):
                    if c + 1 < NCH:
                        front(c + 1)
                    topk(c)
                P.end_phase()


        if 6 in phases:
            with ExitStack() as ph:
                P.stack = ph
                TP = 256
                NTP = S // TP
                NSUB = TP // 128
                GC = 16
                NG = 128 // GC
                fnw = P.sb("fnw", [128, D], F32)
                epsb = P.sb("epsb", [128, 1], F32)
                iot_i = P.sb("iotr_i", [128, 128], I32)
                iotr = P.sb("iotr", [128, 128], F32)
                P.dma("sp", lambda e: e.dma_start(out=fnw[:], in_=final_norm_w[0:1, :].to_broadcast([128, D])), reads=[final_norm_w], writes=[fnw])
                P.op("pool", lambda e: e.memset(epsb[:], EPS), writes=[epsb])
                P.op("pool", lambda e: e.iota(iot_i[:], pattern=[[1, 128]], base=0, channel_multiplier=0), writes=[iot_i])
                P.op("dve", lambda e: e.tensor_copy(out=iotr[:], in_=iot_i[:]), reads=[iot_i], writes=[iotr])
                GT = P.sb("GT", [128, 128, TP], BF16)
                hnr = Ring([P.sb("hnT4_%d" % i, [128, 16, TP], BF16) for i in range(2)])
                rtr4 = Ring([P.sb("rt4_%d" % i, [128, NSUB, 3, 128], F32) for i in range(2)])
                oh1r = Ring([P.sb("oh1_%d" % i, [128, 8, 128], BF16) for i in range(2)])
                oh2r = Ring([P.sb("oh2_%d" % i, [128, 8, 128], BF16) for i in range(2)])
                utr = Ring([P.sb("ut%d" % i, [128, 16, 128], BF16) for i in range(4)])
                actr = Ring([P.sb("gel%d" % i, [128, TP], F32) for i in range(2)])
                wtr = Ring([P.sb("wt%d" % i, [128, GC, TP], BF16) for i in range(2)])
                vtr = Ring([P.sb("vt%d" % i, [128, GC, 512], BF16) for i in range(2)])
                pacc = [P.sb("pacc%d" % i, [128, D], F32) for i in range(NSUB)]
                h4r = Ring([P.sb("h4f%d" % i, [128, D], F32) for i in range(2)])
                junkb = P.sb("junkb", [128, D], BF16)
                ssq = P.sb("ssq6", [128, 1], F32)
                gps = Ring([P.ps("gps%d" % i, [128, 4, 128], F32) for i in range(2)])
                aps = Ring([P.ps("aps%d" % i, [128, TP], F32) for i in range(2)])
                ops = Ring([P.ps("ops%d" % i, [128, 512], F32) for i in range(3)])

                def mk_tl(ti):
                    def f():
                        hn_ = hnr.next(); rt_ = rtr4.next()
                        for sub in range(NSUB):
                            c = ti * NSUB + sub
                            P.dma("sp", lambda e, sub=sub, c=c: e.dma_start(out=hn_[:, :, sub * 128:(sub + 1) * 128], in_=hnT_d[c]), reads=[hnT_d], writes=[hn_])
                            P.dma("sp", lambda e, sub=sub, c=c: e.dma_start(out=rt_[:, sub, :, :], in_=rt_d[c]), reads=[rt_d], writes=[rt_])
                        return hn_, rt_
                    return f

                def mk_ul(c):
                    def f():
                        ut = utr.next()
                        P.dma("sp", lambda e: e.dma_start(out=ut[:], in_=UT_d[c]), reads=[UT_d], writes=[ut])
                        return ut
                    return f

                def mk_vl(g, blk):
                    def f():
                        vt = vtr.next()
                        P.dma("act", lambda e: e.dma_start(out=vt[:], in_=Vbf_d[g * GC * 128:(g + 1) * GC * 128, blk * 512:(blk + 1) * 512].rearrange("(ci p) d -> p ci d", p=128)), reads=[Vbf_d], writes=[vt])
                        return vt
                    return f
                tpf4 = Prefetch([mk_tl(ti) for ti in range(NTP)], 1)
                upf = Prefetch([mk_ul(c) for ti in range(NTP) for c in range(128)], 3)
                vpf = Prefetch([mk_vl(g, blk) for ti in range(NTP) for g in range(NG) for blk in range(4)], 1)
                cnt = {"u": 0, "v": 0}

                def GT_units(rt_):
                    units = []
                    for tb in range(TP // 8):
                        def f(tb=tb):
                            sub, tt = (tb * 8) // 128, (tb * 8) % 128
                            o1 = oh1r.next(); o2 = oh2r.next()
                            iob = iotr[:].unsqueeze(1).to_broadcast([128, 8, 128])
                            P.op("dve", lambda e: e.tensor_tensor(out=o1[:], in0=iob, in1=rt_[:, sub, 0, tt:tt + 8].unsqueeze(2).to_broadcast([128, 8, 128]), op=ALU.is_equal), reads=[iotr, rt_], writes=[o1])
                            P.op("dve", lambda e: e.tensor_tensor(out=o2[:], in0=iob, in1=rt_[:, sub, 1, tt:tt + 8].unsqueeze(2).to_broadcast([128, 8, 128]), op=ALU.is_equal), reads=[iotr, rt_], writes=[o2])
                            P.op("pool", lambda e: e.tensor_tensor(out=o2[:], in0=o2[:], in1=rt_[:, sub, 2, tt:tt + 8].unsqueeze(2).to_broadcast([128, 8, 128]), op=ALU.mult), reads=[o2, rt_], writes=[o2])
                            for q in range(2):
                                gp = gps.next()
                                for u in range(4):
                                    P.op("pe", lambda e, gp=gp, u=u, q=q: e.matmul(gp[:, u, :], lhsT=o2[:, q * 4 + u, :], rhs=o1[:, q * 4 + u, :], start=True, stop=True), reads=[o1, o2], writes=[gp])
                                tq = tb * 2 + q
                                P.op("act", lambda e, gp=gp, tq=tq: e.copy(out=GT[:, :, tq * 4:(tq + 1) * 4], in_=gp[:].rearrange("p t c -> p c t")), reads=[gp], writes=[GT])
                        units.append(f)
                    return units

                def A_units(hn_, g, wt):
                    units = []
                    for ci in range(GC):
                        def f(ci=ci):
                            c = g * GC + ci
                            ut = upf.get(cnt["u"]); cnt["u"] += 1
                            ap_ = aps.next(); ab = actr.next()
                            for kc in range(16):
                                P.op("pe", lambda e, kc=kc: e.matmul(ap_[:], lhsT=ut[:, kc, :], rhs=hn_[:, kc, :], start=(kc == 0), stop=(kc == 15)), reads=[ut, hn_], writes=[ap_])
                            P.op("act", lambda e: e.activation(out=ab[:], in_=ap_[:], func=AF.Gelu), reads=[ap_], writes=[ab])
                            P.op("dve", lambda e: e.tensor_tensor(out=wt[:, ci, :], in0=ab[:], in1=GT[:, c, :], op=ALU.mult), reads=[ab, GT], writes=[wt])
                        units.append(f)
                    return units

                def WV_units(g, wt):
                    units = []
                    for blk in range(4):
                        for sub in range(NSUB):
                            def f(blk=blk, sub=sub):
                                if sub == 0:
                                    WV_units.vt = vpf.get(cnt["v"]); cnt["v"] += 1
                                vt = WV_units.vt
                                op_ = ops.next()
                                for ci in range(GC):
                                    P.op("pe", lambda e, ci=ci: e.matmul(op_[:], lhsT=wt[:, ci, sub * 128:(sub + 1) * 128], rhs=vt[:, ci, :], start=(ci == 0), stop=(ci == GC - 1)), reads=[wt, vt], writes=[op_])
                                if g == 0:
                                    P.op("dve", lambda e: e.tensor_copy(out=pacc[sub][:, blk * 512:(blk + 1) * 512], in_=op_[:]), reads=[op_], writes=[pacc[sub]])
                                else:
                                    P.op("dve", lambda e: e.tensor_tensor(out=pacc[sub][:, blk * 512:(blk + 1) * 512], in0=pacc[sub][:, blk * 512:(blk + 1) * 512], in1=op_[:], op=ALU.add), reads=[op_, pacc[sub]], writes=[pacc[sub]])
                            units.append(f)
                    return units

                def interleave(xs_, ys_):
                    nx, ny = len(xs_), len(ys_)
                    ix = 0
                    for iy in range(ny):
                        tgt = (iy + 1) * nx // ny
                        while ix < tgt:
                            xs_[ix](); ix += 1
                        ys_[iy]()
                    while ix < nx:
                        xs_[ix](); ix += 1

                def final(ti):
                    for sub in range(NSUB):
                        t0 = ti * TP + sub * 128
                        ht = h4r.next()
                        P.dma("sp", lambda e, ht=ht, t0=t0: e.dma_start(out=ht[:], in_=hsc[t0:t0 + 128, :]), reads=[hsc], writes=[ht])
                        P.op("pool", lambda e, ht=ht, sub=sub: e.tensor_tensor(out=ht[:], in0=ht[:], in1=pacc[sub][:], op=ALU.add), reads=[ht, pacc[sub]], writes=[ht])
                        P.op("act", lambda e, ht=ht: e.activation(out=junkb[:], in_=ht[:], func=AF.Square, accum_out=ssq[:]), reads=[ht], writes=[junkb, ssq])
                        P.op("act", lambda e: e.activation(out=ssq[:], in_=ssq[:], func=AF.Sqrt, bias=epsb[:, 0:1], scale=1.0 / D), reads=[ssq, epsb], writes=[ssq])
                        P.op("dve", lambda e: e.reciprocal(out=ssq[:], in_=ssq[:]), reads=[ssq], writes=[ssq])
                        P.op("dve", lambda e, ht=ht: e.scalar_tensor_tensor(out=ht[:], in0=ht[:], scalar=ssq[:, 0:1], in1=fnw[:], op0=ALU.mult, op1=ALU.mult), reads=[ht, ssq, fnw], writes=[ht])
                        P.dma("sp", lambda e, ht=ht, t0=t0: e.dma_start(out=out[t0:t0 + 128, :], in_=ht[:]), reads=[ht], writes=[out])

                hn_, rt_ = tpf4.get(0)
                for f in GT_units(rt_):
                    f()
                wt_cur = wtr.next()
                for f in A_units(hn_, 0, wt_cur):
                    f()
                for ti in range(NTP):
                    for g in range(NG):
                        if g < NG - 1:
                            wt_nxt = wtr.next()
                            interleave(A_units(hn_, g + 1, wt_nxt), WV_units(g, wt_cur))
                            wt_cur = wt_nxt
                        else:
                            if ti + 1 < NTP:
                                hn_n, rt_n = tpf4.get(ti + 1)
                                interleave(GT_units(rt_n), WV_units(g, wt_cur))
                                final(ti)
                                wt_cur = wtr.next()
                                for f in A_units(hn_n, 0, wt_cur):
                                    f()
                                hn_, rt_ = hn_n, rt_n
                            else:
                                for f in WV_units(g, wt_cur):
                                    f()
                                final(ti)
                P.end_phase()

        P.wait_all("sp", outs)
        P.emit()
    return nc


def _pedge_const(S):
    pe = np.ones((4, 16), np.float32)
    for gi, w in enumerate((2, 4, 8, 16)):
        for j in range(8):
            for t, col in ((j, j), (S - 8 + j, 8 + j)):
                lo = max(t - w // 2, 0)
                hi = min(t + w // 2, S)
                pe[gi, col] = w / float(hi - lo)
    return pe.reshape(1, 64)


_NC_CACHE = {}


def kernel(**inputs):
    x = np.ascontiguousarray(np.asarray(inputs["x"], dtype=np.float32))
    B, S, _ = x.shape
    f = lambda k: np.ascontiguousarray(np.asarray(inputs[k], dtype=np.float32))
    shared = dict(
        mixer_norm_w=f("mixer_norm_w").reshape(1, D),
        w_in=f("w_in").reshape(D, IPW),
        conv_w=f("conv_w").reshape(5, XBC),
        conv_b=f("conv_b").reshape(1, XBC),
        dt_bias=f("dt_bias").reshape(1, 64),
        a_log=f("a_log").reshape(1, 64),
        d_skip=f("d_skip").reshape(1, H),
        ssd_norm_w=f("ssd_norm_w").reshape(1, D),
        w_ssd_branch=f("w_ssd_branch").reshape(D, D),
        w_pool_group=f("w_pool_group").reshape(4, 256, 256),
        pool_scale=f("pool_scale").reshape(1, PW),
        w_pool_branch=f("w_pool_branch").reshape(PW, D),
        w_out=f("w_out").reshape(D, D),
        ffn_norm_w=f("ffn_norm_w").reshape(1, D),
        w_query=f("w_query").reshape(D, D),
        sub_keys=f("sub_keys").reshape(16, 128, 128),
        expert_u=f("expert_u").reshape(NE, D),
        expert_v=f("expert_v").reshape(NE, D),
        final_norm_w=f("final_norm_w").reshape(1, D),
        pedge=_pedge_const(S),
    )
    if S not in _NC_CACHE:
        _NC_CACHE[S] = build(S)
    nc = _NC_CACHE[S]
    in_maps = [dict(shared, x=x[b]) for b in range(B)]
    res = run_bass_kernel_spmd(nc, in_maps, core_ids=list(range(B)))
    return np.stack([np.asarray(r["out"], dtype=np.float32) for r in res.results], axis=0)
```
